# Optimizing a Trainium2 kernel written in Bass

```python
import functools
import numpy as np
import jax
import jax.numpy as jnp
from jax import lax

D_MODEL = 2048
BATCH = 4
SEQ = 4096
DEPTH = 2

GRID_W = 64
CTX_LEN = 256
D_MIX = D_MODEL
N_MIXERS = 4
GROUP_W = D_MIX // N_MIXERS
CONV_W = 5
EPS = 1e-6
N_MOD = 6
D_FF = -(-8 * D_MODEL // (3 * 256)) * 256

SSD_HEAD_DIM = 64
SSD_HEADS = GROUP_W // SSD_HEAD_DIM
SSD_GROUPS = 2
SSD_HPG = SSD_HEADS // SSD_GROUPS
SSD_STATE = 128
SSD_CHUNK = 128
SSD_SPLIT = (GROUP_W, GROUP_W, SSD_GROUPS * SSD_STATE, SSD_GROUPS * SSD_STATE, 2 * SSD_HEADS)

ML_HEADS = 4
ML_HEAD_DIM = GROUP_W // ML_HEADS
ML_CHUNK = 64
ML_SPLIT = (GROUP_W, GROUP_W, GROUP_W, GROUP_W, 2 * ML_HEADS, 2 * ML_HEADS)

LRU_BLOCKS = 8
LRU_BLOCK_W = GROUP_W // LRU_BLOCKS
LRU_C = 8.0
LRU_SPLIT = (GROUP_W, GROUP_W)

GLA_HEADS = 4
GLA_DK = GROUP_W // (2 * GLA_HEADS)
GLA_DV = GROUP_W // GLA_HEADS
GLA_RANK = 16
GLA_TAU = 16.0
GLA_CHUNK = 64
GLA_SPLIT = (GLA_HEADS * GLA_DK, GLA_HEADS * GLA_DK, GROUP_W, GROUP_W, 2 * GLA_RANK)

MIXER_COLS = (sum(SSD_SPLIT), sum(ML_SPLIT), sum(LRU_SPLIT), sum(GLA_SPLIT))
P_IN = sum(MIXER_COLS)

kernel_name = 'hybrid_ssd_mlstm_rglru_gla_prefix_dit'


def split(a, sizes):
    return jnp.split(a, np.cumsum(sizes)[:-1].tolist(), axis=-1)


def rmsnorm(x, g):
    xf = x.astype(jnp.float32)
    y = xf * lax.rsqrt(jnp.mean(xf * xf, axis=-1, keepdims=True) + EPS)
    return (y * g.astype(jnp.float32)).astype(x.dtype)


def head_rmsnorm(y, g, n_heads):
    bsz, length, width = y.shape
    yh = y.astype(jnp.float32).reshape(bsz, length, n_heads, width // n_heads)
    yh = yh * lax.rsqrt(jnp.mean(yh * yh, axis=-1, keepdims=True) + EPS)
    return yh.reshape(bsz, length, width) * g.astype(jnp.float32)


def modulate(h, shift, scale):
    return h * (1.0 + scale) + shift


def swiglu(h, w_gate, w_up, w_down):
    return (jax.nn.silu(h @ w_gate) * (h @ w_up)) @ w_down


def dwconv(x, w, b):
    ch = x.shape[-1]
    out = lax.conv_general_dilated(x, w.astype(x.dtype)[:, None, :], window_strides=(1,),
                                   padding=[(CONV_W // 2, CONV_W // 2)],
                                   dimension_numbers=('NWC', 'WIO', 'NWC'), feature_group_count=ch)
    return out + b


def to_colmajor(u, rows):
    bsz, length, ch = u.shape
    return u.reshape(bsz, rows, GRID_W, ch).transpose(0, 2, 1, 3).reshape(bsz, length, ch)


def from_colmajor(y, rows):
    bsz, length, ch = y.shape
    return y.reshape(bsz, GRID_W, rows, ch).transpose(0, 2, 1, 3).reshape(bsz, length, ch)


def to_chunks(a, chunk):
    bsz, length = a.shape[:2]
    return jnp.moveaxis(a.reshape(bsz, length // chunk, chunk, *a.shape[2:]), 1, 0)


def from_chunks(a):
    n, bsz, chunk = a.shape[:3]
    return jnp.moveaxis(a, 0, 1).reshape(bsz, n * chunk, *a.shape[3:])


def chunked_scan(step, inputs, state, chunk):
    state, ys = lax.scan(step, state, tuple(to_chunks(a, chunk) for a in inputs))
    return from_chunks(ys), state


def bidir_scan(run, ctx_dirs, lat_dirs, init):
    flip = lambda t: tuple(jnp.flip(a, 1) for a in t)
    yc_f, s_f = run(ctx_dirs[0], init)
    yc_b, s_b = run(flip(ctx_dirs[1]), init)
    yl_f, _ = run(lat_dirs[0], s_f)
    yl_b, _ = run(flip(lat_dirs[1]), s_b)
    return yc_f + jnp.flip(yc_b, 1), yl_f + jnp.flip(yl_b, 1)


def tril_mask(t):
    return jnp.tril(jnp.ones((t, t), dtype=bool))


def ssd_step(s, inp):
    x, bm, cm, dt, loga = inp
    b = jnp.cumsum(loga, axis=1)
    mask = tril_mask(x.shape[1])[None, :, :, None, None]
    seg = jnp.exp(jnp.where(mask, b[:, :, None] - b[:, None], -jnp.inf))
    cb = jnp.einsum('btgn,bsgn->btsg', cm, bm)
    w = cb[..., None] * seg * dt[:, None]
    y = jnp.einsum('btsgh,bsghp->btghp', w, x)
    y = y + jnp.einsum('btgn,bghpn->btghp', cm, s) * jnp.exp(b)[..., None]
    xw = x * (jnp.exp(b[:, -1:] - b) * dt)[..., None]
    s = jnp.exp(b[:, -1])[..., None, None] * s + jnp.einsum('bsghp,bsgn->bghpn', xw, bm)
    return s, y


def ssd_mixer(u_ctx, u_lat, conv_w, conv_b, dt_bias, a_log, d_skip, norm_g, need_ctx):
    neg_a = (-jnp.exp(a_log.astype(jnp.float32))).reshape(2, SSD_GROUPS, SSD_HPG)

    def prep(u):
        bsz, length = u.shape[:2]
        z, xs, bm, cm, dt_raw = split(u, SSD_SPLIT)
        xbc = jax.nn.silu(dwconv(jnp.concatenate([xs, bm, cm], axis=-1), conv_w, conv_b))
        xs, bm, cm = split(xbc, (GROUP_W, SSD_GROUPS * SSD_STATE, SSD_GROUPS * SSD_STATE))
        xs = xs.reshape(bsz, length, SSD_GROUPS, SSD_HPG, SSD_HEAD_DIM)
        bm = bm.reshape(bsz, length, SSD_GROUPS, SSD_STATE)
        cm = cm.reshape(bsz, length, SSD_GROUPS, SSD_STATE)
        dt = jax.nn.softplus(dt_raw.astype(jnp.float32).reshape(bsz, length, 2, SSD_GROUPS, SSD_HPG)
                             + dt_bias.astype(jnp.float32).reshape(2, SSD_GROUPS, SSD_HPG))
        loga = dt * neg_a
        dirs = tuple((xs, bm, cm, dt[:, :, d], loga[:, :, d]) for d in range(2))
        return z, xs, dirs

    z_c, xs_c, dirs_c = prep(u_ctx)
    z_l, xs_l, dirs_l = prep(u_lat)
    init = jnp.zeros((u_lat.shape[0], SSD_GROUPS, SSD_HPG, SSD_HEAD_DIM, SSD_STATE), jnp.float32)
    y_c, y_l = bidir_scan(functools.partial(chunked_scan, ssd_step, chunk=SSD_CHUNK), dirs_c, dirs_l, init)

    def finish(y, xs, z):
        bsz, length = z.shape[:2]
        y = (y + d_skip.reshape(SSD_GROUPS, SSD_HPG, 1) * xs).reshape(bsz, length, GROUP_W)
        return rmsnorm(y * jax.nn.silu(z.astype(jnp.float32)), norm_g).astype(z.dtype)

    return (finish(y_c, xs_c, z_c) if need_ctx else None), finish(y_l, xs_l, z_l)


def mlstm_step(state, inp):
    cs, ns, m = state
    q, k, v, ig, logf = inp
    b = jnp.cumsum(logf, axis=1)
    mask = tril_mask(q.shape[1])[None, :, :, None]
    dmat = jnp.where(mask, b[:, :, None] - b[:, None] + ig[:, None], -jnp.inf)
    inter = b + m[:, None]
    mt = jnp.maximum(inter, jnp.max(dmat, axis=2))
    w = jnp.exp(dmat - mt[:, :, None])
    sc = jnp.exp(inter - mt)
    sw = jnp.einsum('bthd,bshd->btsh', q, k) * w
    num = jnp.einsum('btsh,bshv->bthv', sw, v) + sc[..., None] * jnp.einsum('bhvd,bthd->bthv', cs, q)
    den = jnp.sum(sw, axis=2) + sc * jnp.einsum('bhd,bthd->bth', ns, q)
    y = num / jnp.maximum(jnp.abs(den), jnp.exp(-mt))[..., None]
    bl = b[:, -1]
    tail = bl[:, None] - b + ig
    m_new = jnp.maximum(bl + m, jnp.max(tail, axis=1))
    ws = jnp.exp(tail - m_new[:, None])
    sc_end = jnp.exp(bl + m - m_new)
    cs = sc_end[..., None, None] * cs + jnp.einsum('bshv,bshd->bhvd', v * ws[..., None], k)
    ns = sc_end[..., None] * ns + jnp.einsum('bsh,bshd->bhd', ws, k)
    return (cs, ns, m_new), y


def mlstm_mixer(u_ctx, u_lat, conv_w, conv_b, igate_b, fgate_b, norm_g, need_ctx):
    def prep(u):
        bsz, length = u.shape[:2]
        q, k, v, o, ig, fg = split(u, ML_SPLIT)
        q, k = split(jax.nn.silu(dwconv(jnp.concatenate([q, k], axis=-1), conv_w, conv_b)), (GROUP_W, GROUP_W))
        hs = lambda a: a.reshape(bsz, length, ML_HEADS, ML_HEAD_DIM)
        q, k, v = hs(q), hs(k) * ML_HEAD_DIM ** -0.5, hs(v)
        ig = ig.astype(jnp.float32).reshape(bsz, length, 2, ML_HEADS) + igate_b.astype(jnp.float32)
        logf = jax.nn.log_sigmoid(fg.astype(jnp.float32).reshape(bsz, length, 2, ML_HEADS) + fgate_b.astype(jnp.float32))
        dirs = tuple((q, k, v, ig[:, :, d], logf[:, :, d]) for d in range(2))
        return o, dirs

    o_c, dirs_c = prep(u_ctx)
    o_l, dirs_l = prep(u_lat)
    bsz = u_lat.shape[0]
    init = (jnp.zeros((bsz, ML_HEADS, ML_HEAD_DIM, ML_HEAD_DIM), jnp.float32),
            jnp.zeros((bsz, ML_HEADS, ML_HEAD_DIM), jnp.float32),
            jnp.zeros((bsz, ML_HEADS), jnp.float32))
    y_c, y_l = bidir_scan(functools.partial(chunked_scan, mlstm_step, chunk=ML_CHUNK), dirs_c, dirs_l, init)

    def finish(y, o):
        bsz, length = o.shape[:2]
        y = head_rmsnorm(y.reshape(bsz, length, GROUP_W), norm_g, ML_HEADS)
        return (jax.nn.sigmoid(o.astype(jnp.float32)) * y).astype(o.dtype)

    return (finish(y_c, o_c) if need_ctx else None), finish(y_l, o_l)


def lru_run(inputs, h0):
    a, bx = inputs
    bx = bx.at[:, 0].add(a[:, 0] * h0)
    combine = lambda l, r: (l[0] * r[0], r[0] * l[1] + r[1])
    _, h = lax.associative_scan(combine, (a, bx), axis=1)
    return h, h[:, -1]


def lru_mixer(u_ctx, u_lat, conv_w, conv_b, wa, ba, wx, bx_b, lam, need_ctx):
    log_sig_lam = jax.nn.log_sigmoid(lam.astype(jnp.float32))

    def prep(u):
        bsz, length = u.shape[:2]
        gate, xb = split(u, LRU_SPLIT)
        xf = dwconv(xb, conv_w, conv_b).astype(jnp.float32)
        xblk = xf.reshape(bsz, length, LRU_BLOCKS, LRU_BLOCK_W)
        blockdiag = lambda w, bias: jnp.einsum('blni,dnij->bldnj', xblk, w).reshape(bsz, length, 2, GROUP_W) + bias
        r = jax.nn.sigmoid(blockdiag(wa, ba))
        i = jax.nn.sigmoid(blockdiag(wx, bx_b))
        loga = LRU_C * r * log_sig_lam
        a = jnp.exp(loga)
        bx = jnp.sqrt(-jnp.expm1(2.0 * loga)) * i * xf[:, :, None]
        dirs = tuple((a[:, :, d], bx[:, :, d]) for d in range(2))
        return gate, dirs

    g_c, dirs_c = prep(u_ctx)
    g_l, dirs_l = prep(u_lat)
    init = jnp.zeros((u_lat.shape[0], GROUP_W), jnp.float32)
    h_c, h_l = bidir_scan(lru_run, dirs_c, dirs_l, init)

    def finish(h, gate):
        return (h * jax.nn.gelu(gate.astype(jnp.float32))).astype(gate.dtype)

    return (finish(h_c, g_c) if need_ctx else None), finish(h_l, g_l)


def gla_step(s, inp):
    q, k, v, la = inp
    b = jnp.cumsum(la, axis=1)
    mask = tril_mask(q.shape[1])[None, :, :, None, None]
    dec = jnp.exp(jnp.where(mask, b[:, :, None] - b[:, None], -jnp.inf))
    att = jnp.einsum('btshd,bshd->btsh', dec * q[:, :, None], k)
    o = jnp.einsum('btsh,bshv->bthv', att, v) + jnp.einsum('bthd,bhdv->bthv', q * jnp.exp(b), s)
    bl = b[:, -1]
    s = jnp.exp(bl)[..., None] * s + jnp.einsum('bshd,bshv->bhdv', k * jnp.exp(bl[:, None] - b), v)
    return s, o


def gla_mixer(u_ctx, u_lat, wg2, bg, norm_g, need_ctx):
    def prep(u):
        bsz, length = u.shape[:2]
        q, k, v, r, g1 = split(u, GLA_SPLIT)
        q = q.reshape(bsz, length, GLA_HEADS, GLA_DK) * GLA_DK ** -0.5
        k = k.reshape(bsz, length, GLA_HEADS, GLA_DK)
        v = v.reshape(bsz, length, GLA_HEADS, GLA_DV)
        glog = jnp.einsum('bldr,drk->bldk', g1.reshape(bsz, length, 2, GLA_RANK), wg2) + bg
        la = (jax.nn.log_sigmoid(glog.astype(jnp.float32)) / GLA_TAU).reshape(bsz, length, 2, GLA_HEADS, GLA_DK)
        dirs = tuple((q, k, v, la[:, :, d]) for d in range(2))
        return r, dirs

    r_c, dirs_c = prep(u_ctx)
    r_l, dirs_l = prep(u_lat)
    init = jnp.zeros((u_lat.shape[0], GLA_HEADS, GLA_DK, GLA_DV), jnp.float32)
    o_c, o_l = bidir_scan(functools.partial(chunked_scan, gla_step, chunk=GLA_CHUNK), dirs_c, dirs_l, init)

    def finish(o, r):
        bsz, length = r.shape[:2]
        o = head_rmsnorm(o.reshape(bsz, length, GROUP_W), norm_g, GLA_HEADS)
        return (o * jax.nn.silu(r.astype(jnp.float32))).astype(r.dtype)

    return (finish(o_c, r_c) if need_ctx else None), finish(o_l, r_l)


def setup_inputs(seed: int = 0) -> dict:
    key = jax.random.key(seed)
    keys = iter(jax.random.split(key, 48))
    f32 = jnp.float32
    D = D_MODEL

    def normal(shape, scale):
        return jax.random.normal(next(keys), shape, f32) * scale

    def uniform(shape, lo, hi):
        return jax.random.uniform(next(keys), shape, f32, lo, hi)

    def gain(shape):
        return 1.0 + normal(shape, 0.02)

    x = normal((BATCH, SEQ, D), 1.0)
    c = normal((BATCH, D), 1.0)
    ctx = normal((BATCH, CTX_LEN, D), 1.0)
    c_ctx = normal((D,), 1.0)
    norm1_g = gain((DEPTH, D))
    norm2_g = gain((DEPTH, D))
    w_mod = normal((DEPTH, D, N_MOD * D), 0.5 * D ** -0.5)
    b_mod = normal((DEPTH, N_MOD * D), 0.02)
    w_in = normal((DEPTH, D, P_IN), D ** -0.5)
    w_out = normal((DEPTH, D_MIX, D), D_MIX ** -0.5)
    ssd_ch = GROUP_W + 2 * SSD_GROUPS * SSD_STATE
    ssd_conv_w = normal((DEPTH, CONV_W, ssd_ch), CONV_W ** -0.5)
    ssd_conv_b = normal((DEPTH, ssd_ch), 0.02)
    dt0 = jnp.exp(uniform((DEPTH, 2, SSD_HEADS), float(np.log(1e-3)), float(np.log(1e-1))))
    ssd_dt_bias = dt0 + jnp.log(-jnp.expm1(-dt0))
    ssd_a_log = jnp.log(uniform((DEPTH, 2, SSD_HEADS), 1.0, 16.0))
    ssd_d = 1.0 + normal((DEPTH, SSD_HEADS), 0.1)
    ssd_norm_g = gain((DEPTH, GROUP_W))
    ml_conv_w = normal((DEPTH, CONV_W, 2 * GROUP_W), CONV_W ** -0.5)
    ml_conv_b = normal((DEPTH, 2 * GROUP_W), 0.02)
    ml_igate_b = normal((DEPTH, 2, ML_HEADS), 0.1)
    ml_fgate_b = jnp.linspace(3.0, 6.0, ML_HEADS, dtype=f32) + normal((DEPTH, 2, ML_HEADS), 0.1)
    ml_norm_g = gain((DEPTH, GROUP_W))
    lru_conv_w = normal((DEPTH, CONV_W, GROUP_W), CONV_W ** -0.5)
    lru_conv_b = normal((DEPTH, GROUP_W), 0.02)
    lru_wa = normal((DEPTH, 2, LRU_BLOCKS, LRU_BLOCK_W, LRU_BLOCK_W), LRU_BLOCK_W ** -0.5)
    lru_ba = normal((DEPTH, 2, GROUP_W), 0.02)
    lru_wx = normal((DEPTH, 2, LRU_BLOCKS, LRU_BLOCK_W, LRU_BLOCK_W), LRU_BLOCK_W ** -0.5)
    lru_bx = normal((DEPTH, 2, GROUP_W), 0.02)
    s = uniform((DEPTH, 2, GROUP_W), 0.9, 0.999) ** (1.0 / LRU_C)
    lru_lambda = jnp.log(s) - jnp.log1p(-s)
    gla_wg2 = normal((DEPTH, 2, GLA_RANK, GLA_HEADS * GLA_DK), GLA_RANK ** -0.5)
    gla_bg = normal((DEPTH, 2, GLA_HEADS * GLA_DK), 0.1)
    gla_norm_g = gain((DEPTH, GROUP_W))
    w_gate = normal((DEPTH, D, D_FF), D ** -0.5)
    w_up = normal((DEPTH, D, D_FF), D ** -0.5)
    w_down = normal((DEPTH, D_FF, D), D_FF ** -0.5)
    final_g = gain((D,))
    return {'x': x, 'c': c, 'ctx': ctx, 'c_ctx': c_ctx, 'norm1_g': norm1_g, 'norm2_g': norm2_g,
            'w_mod': w_mod, 'b_mod': b_mod, 'w_in': w_in, 'w_out': w_out,
            'ssd_conv_w': ssd_conv_w, 'ssd_conv_b': ssd_conv_b, 'ssd_dt_bias': ssd_dt_bias,
            'ssd_a_log': ssd_a_log, 'ssd_d': ssd_d, 'ssd_norm_g': ssd_norm_g,
            'ml_conv_w': ml_conv_w, 'ml_conv_b': ml_conv_b, 'ml_igate_b': ml_igate_b,
            'ml_fgate_b': ml_fgate_b, 'ml_norm_g': ml_norm_g,
            'lru_conv_w': lru_conv_w, 'lru_conv_b': lru_conv_b, 'lru_wa': lru_wa, 'lru_ba': lru_ba,
            'lru_wx': lru_wx, 'lru_bx': lru_bx, 'lru_lambda': lru_lambda,
            'gla_wg2': gla_wg2, 'gla_bg': gla_bg, 'gla_norm_g': gla_norm_g,
            'w_gate': w_gate, 'w_up': w_up, 'w_down': w_down, 'final_g': final_g}


def reference(x, c, ctx, c_ctx, norm1_g, norm2_g, w_mod, b_mod, w_in, w_out,
              ssd_conv_w, ssd_conv_b, ssd_dt_bias, ssd_a_log, ssd_d, ssd_norm_g,
              ml_conv_w, ml_conv_b, ml_igate_b, ml_fgate_b, ml_norm_g,
              lru_conv_w, lru_conv_b, lru_wa, lru_ba, lru_wx, lru_bx, lru_lambda,
              gla_wg2, gla_bg, gla_norm_g, w_gate, w_up, w_down, final_g):
    length = x.shape[1]
    rows = length // GRID_W
    silu_c = jax.nn.silu(c)
    silu_cc = jax.nn.silu(c_ctx)
    xl, xt = x, ctx
    for i in range(DEPTH):
        need_ctx = i < DEPTH - 1
        m_l = jnp.split((silu_c @ w_mod[i] + b_mod[i])[:, None, :], N_MOD, axis=-1)
        m_t = jnp.split(silu_cc @ w_mod[i] + b_mod[i], N_MOD, axis=-1)
        hl = modulate(rmsnorm(xl, norm1_g[i]), m_l[0], m_l[1])
        ht = modulate(rmsnorm(xt, norm1_g[i]), m_t[0], m_t[1])
        ul = split(hl @ w_in[i], MIXER_COLS)
        ut = split(ht @ w_in[i], MIXER_COLS)
        ya_t, ya_l = ssd_mixer(ut[0], ul[0], ssd_conv_w[i], ssd_conv_b[i], ssd_dt_bias[i], ssd_a_log[i],
                               ssd_d[i], ssd_norm_g[i], need_ctx)
        yb_t, yb_l = mlstm_mixer(ut[1], to_colmajor(ul[1], rows), ml_conv_w[i], ml_conv_b[i],
                                 ml_igate_b[i], ml_fgate_b[i], ml_norm_g[i], need_ctx)
        yb_l = from_colmajor(yb_l, rows)
        yc_t, yc_l = lru_mixer(ut[2], ul[2], lru_conv_w[i], lru_conv_b[i], lru_wa[i], lru_ba[i],
                               lru_wx[i], lru_bx[i], lru_lambda[i], need_ctx)
        yd_t, yd_l = gla_mixer(ut[3], to_colmajor(ul[3], rows), gla_wg2[i], gla_bg[i], gla_norm_g[i], need_ctx)
        yd_l = from_colmajor(yd_l, rows)
        xl = xl + m_l[2] * (jnp.concatenate([ya_l, yb_l, yc_l, yd_l], axis=-1) @ w_out[i])
        xl = xl + m_l[5] * swiglu(modulate(rmsnorm(xl, norm2_g[i]), m_l[3], m_l[4]), w_gate[i], w_up[i], w_down[i])
        if need_ctx:
            xt = xt + m_t[2] * (jnp.concatenate([ya_t, yb_t, yc_t, yd_t], axis=-1) @ w_out[i])
            xt = xt + m_t[5] * swiglu(modulate(rmsnorm(xt, norm2_g[i]), m_t[3], m_t[4]), w_gate[i], w_up[i], w_down[i])
    return rmsnorm(xl, final_g)
```

```python
import numpy as np
import math
from contextlib import ExitStack
import concourse.bass as bass
import concourse.mybir as mybir
from concourse.bass_utils import run_bass_kernel_spmd

F32, BF16 = mybir.dt.float32, mybir.dt.bfloat16
AF = mybir.ActivationFunctionType
ALU = mybir.AluOpType
AX = mybir.AxisListType

D = 2048
L = 4096
CTX = 256
T = L + CTX
NCH = T // 128
DFF = 5632
KC = D // 128
KF = DFF // 128
PIN = 6208
EPS = 1e-6
DEPTH = 2
CA, CB, CC, CD = 0, 1552, 3616, 4640
NEG = -30000.0
N_ACTIVE = 4

DEBUG = {}


class Buf:
    __slots__ = ("name", "w", "r", "dsem", "dcnt", "dlast", "ep")

    def __init__(self, name):
        self.name = name
        self.w = []
        self.r = {}
        self.dsem = None
        self.dcnt = 0
        self.dlast = None
        self.ep = 0


class Trk:
    SEM_CAP = 30000

    def __init__(self, nc):
        self.nc = nc
        self.eng = {"pe": nc.tensor, "act": nc.scalar, "dve": nc.vector, "pool": nc.gpsimd, "sp": nc.sync}
        self.esem = {}
        self.ecnt = {}
        self.seen = {e: {} for e in self.eng}
        self.nsem = 0
        self.ep = 0
        self.dma_slots = {}
        self.free_dsems = []
        self.n_ins = 0
        for e in self.eng:
            self.esem[e] = self._newsem("e_" + e)
            self.ecnt[e] = 0

    def _newsem(self, name):
        self.nsem += 1
        return self.nc.alloc_semaphore("%s_%d" % (name, self.nsem))

    def _fresh(self, b):
        if b.ep != self.ep:
            b.w = []
            b.r = {}
            b.dlast = None
            b.ep = self.ep

    def _wait(self, e, deps):
        seen = self.seen[e]
        own = self.esem[e]
        for (sem, val) in deps:
            k = id(sem)
            if seen.get(k, 0) >= val:
                continue
            if e == "pe" and sem is own:
                continue
            self.eng[e].wait_ge(sem, val)
            seen[k] = val

    def _deps(self, reads, writes):
        deps = []
        for b in reads:
            self._fresh(b)
            deps += b.w
        for b in writes:
            self._fresh(b)
            deps += b.w
            deps += list(b.r.values())
        return deps

    def _record(self, ev, reads, writes):
        k = id(ev[0])
        for b in reads:
            b.r[k] = ev
        for b in writes:
            b.w = [ev]
            b.r = {}

    def op(self, e, fn, reads=(), writes=()):
        self._wait(e, self._deps(reads, writes))
        ins = fn(self.eng[e])
        if self.ecnt[e] >= self.SEM_CAP:
            self.esem[e] = self._newsem("e_" + e)
            self.ecnt[e] = 0
        self.ecnt[e] += 1
        sem = self.esem[e]
        ins.then_inc(sem, 1)
        self._record((sem, self.ecnt[e]), reads, writes)
        self.n_ins += 1

    def dma(self, q, out, in_, slot, reads=(), writes=()):
        deps = self._deps(reads, writes)
        self._fresh(slot)
        if slot.dlast is not None:
            deps.append(slot.dlast)
        self._wait(q, deps)
        if slot.dsem is not None and slot.dcnt + 16 > self.SEM_CAP:
            slot.dsem = None
        if slot.dsem is None:
            while self.free_dsems:
                sem, cnt = self.free_dsems.pop()
                if cnt + 16 <= self.SEM_CAP:
                    slot.dsem, slot.dcnt = sem, cnt
                    break
            else:
                slot.dsem = self._newsem("d")
                slot.dcnt = 0
        slot.dcnt += 16
        self.eng[q].dma_start(out=out, in_=in_).then_inc(slot.dsem, 16)
        ev = (slot.dsem, slot.dcnt)
        slot.dlast = ev
        self.dma_slots[id(slot)] = slot
        self._record(ev, reads, writes)
        self.n_ins += 1

    def barrier(self):
        evs = [(self.esem[e], self.ecnt[e]) for e in self.eng if self.ecnt[e] > 0]
        for s in self.dma_slots.values():
            if s.ep == self.ep and s.dlast is not None:
                evs.append(s.dlast)
        for e in self.eng:
            seen = self.seen[e]
            for (sem, val) in evs:
                if sem is self.esem[e]:
                    continue
                if seen.get(id(sem), 0) >= val:
                    continue
                self.eng[e].wait_ge(sem, val)
                seen[id(sem)] = val
        for s in self.dma_slots.values():
            if s.dsem is not None:
                self.free_dsems.append((s.dsem, s.dcnt))
                s.dsem = None
        self.ep += 1
        self.dma_slots = {}


class Tile:
    def __init__(self, t, name):
        self.t = t
        self.b = Buf(name)

    def __getitem__(self, k):
        return self.t[k]


class Sub:
    def __init__(self, ap, b):
        self.t = ap
        self.b = b

    def __getitem__(self, k):
        return self.t[k]


class Ctx:
    pass


_UNIQ = [0]


def uniq(name):
    _UNIQ[0] += 1
    return "%s_u%d" % (name, _UNIQ[0])


def dram_ap(t, offset, pattern):
    return bass.AP(t.tensor, offset, pattern)


def build_program(debug_outs=()):
    nc = bass.Bass("TRN2", target_bir_lowering=False)
    K = Trk(nc)
    g = Ctx()
    g.nc, g.K = nc, K

    def din(name, shape, dt=F32):
        return nc.dram_tensor(name, list(shape), dt, kind="ExternalInput").ap()

    def dscr(name, shape, dt):
        kind = "ExternalOutput" if name in debug_outs else "Internal"
        return nc.dram_tensor(name, list(shape), dt, kind=kind).ap()

    I = {}
    I["x"] = din("x", [L, D])
    I["ctx"] = din("ctx", [CTX, D])
    I["ccol"] = din("ccol", [128, 2, KC])
    I["gcol"] = din("gcol", [128, 5, KC])
    I["bmodcol"] = din("bmodcol", [128, DEPTH, 96])
    I["w_mod"] = din("w_mod", [DEPTH, D, 6 * D])
    I["w_in"] = din("w_in", [DEPTH, D, PIN])
    I["w_out"] = din("w_out", [DEPTH, D, D])
    I["w_gate"] = din("w_gate", [DEPTH, D, DFF])
    I["w_up"] = din("w_up", [DEPTH, D, DFF])
    I["w_down"] = din("w_down", [DEPTH, DFF, D])
    for nm, shp in MIX_PARAM_SHAPES.items():
        I[nm] = din(nm, shp)
    out = nc.dram_tensor("out", [L, D], F32, kind="ExternalOutput").ap()

    S = {}
    S["xT"] = dscr("s_xT", [D, T], F32)
    S["UA"] = dscr("s_UA", [T, 1536], BF16)
    S["GA"] = dscr("s_GA", [T, 16], F32)
    S["UB"] = dscr("s_UB", [T, 2048], BF16)
    S["GB"] = dscr("s_GB", [T, 16], F32)
    S["UC"] = dscr("s_UC", [T, 1024], BF16)
    S["UD"] = dscr("s_UD", [T, 1536], BF16)
    S["GD"] = dscr("s_GD", [T, 32], F32)
    S["UAf"] = dscr("s_UAf", [1024, T], BF16)
    S["UCf"] = dscr("s_UCf", [1024, T], BF16)
    S["Y"] = dscr("s_Y", [T, D], BF16)
    S["HID"] = dscr("s_HID", [DFF, T], BF16)
    S["MOD"] = dscr("s_MOD", [DEPTH * 2, 128, 96], F32)
    g.I, g.S, g.out = I, S, out
    g.dbufs = {}

    def dbuf(name, blk=0):
        k = (name, blk)
        if k not in g.dbufs:
            g.dbufs[k] = Buf("%s_%s" % (name, blk))
        return g.dbufs[k]
    g.dbuf = dbuf

    with ExitStack() as top:
        def sb(name, shape, dt):
            return Tile(top.enter_context(nc.sbuf_tensor(name, list(shape), dt)), name)

        g.ident_f = sb("ident_f", [128, 128], F32)
        g.ident_b = sb("ident_b", [128, 128], BF16)
        g.ones_b = sb("ones_b", [128, 128], BF16)
        g.ones_f = sb("ones_f", [128, 128], F32)
        g.triU_f = sb("triU_f", [128, 128], F32)
        g.triL_f = sb("triL_f", [128, 128], F32)
        g.negF_f = sb("negF_f", [128, 128], F32)
        g.negB_f = sb("negB_f", [128, 128], F32)
        g.mskF_b = sb("mskF_b", [128, 128], BF16)
        g.mskB_b = sb("mskB_b", [128, 128], BF16)
        g.mid_f = [sb("midF_f", [128, 128], F32), sb("midB_f", [128, 128], F32)]
        g.mod = sb("modcols", [128, DEPTH * 2, 96], F32)
        g.gcol = sb("gcols", [128, 5, KC], F32)
        g.epsc = sb("epsc", [128, 1], F32)
        g.psum = []
        for i in range(6):
            g.psum.append(Tile(top.enter_context(nc.psum_tensor("ps%d" % i, [128, 512], F32)), "ps%d" % i))
        g.psb = []
        for i in range(2):
            g.psb.append(Tile(top.enter_context(nc.psum_tensor("psb%d" % i, [128, 1024], BF16)), "psb%d" % i))
        g.ps_i = 0
        g.psb_i = 0

        build_consts(g)
        phase_mod(g)
        phase_x0(g)
        for li in range(DEPTH):
            last = li == DEPTH - 1
            phase_norm_proj(g, li)
            if "stop_after_proj" in DEBUG and DEBUG["stop_after_proj"] == li:
                break
            phase_ssd(g, li, not last)
            phase_mlstm(g, li, not last)
            phase_lru(g, li, not last)
            phase_gla(g, li, not last)
            if "stop_after_mix" in DEBUG and DEBUG["stop_after_mix"] == li:
                break
            phase_wout(g, li, not last)
            phase_ffn_up(g, li, not last)
            phase_ffn_down(g, li, not last)
        phase_final(g)
        K.barrier()
    return nc


MIX_PARAM_SHAPES = {
    "pA_tm": [DEPTH, 128, 1056], "pA_conv": [DEPTH, 128, 8, 6],
    "pB_tm": [DEPTH, 128, 528], "pB_conv": [DEPTH, 128, 8, 6],
    "pC_col": [DEPTH, 128, 4, 12], "pC_w": [DEPTH, 16, 128, 128],
    "pD_tm": [DEPTH, 128, 1024], "pD_wg2": [DEPTH, 32, 512],
}


def next_ps(g, lo=0):
    n = len(g.psum) - lo
    t = g.psum[lo + g.ps_i % n]
    g.ps_i += 1
    return t


def next_psb(g):
    t = g.psb[g.psb_i % len(g.psb)]
    g.psb_i += 1
    return t


def build_consts(g):
    nc, K = g.nc, g.K
    cst = g.I_const = nc.dram_tensor("consts", [8, 128, 128], F32, kind="ExternalInput").ap()
    with ExitStack() as es:
        tmp = Tile(es.enter_context(nc.sbuf_tensor("cst_tmp", [128, 8, 128], F32)), "cst_tmp")
        K.dma("sp", tmp[:], cst.rearrange("c p n -> p c n"), tmp.b, writes=[tmp.b])
        K.op("dve", lambda e: e.tensor_copy(g.ident_f[:], tmp[:, 0, :]), [tmp.b], [g.ident_f.b])
        K.op("dve", lambda e: e.tensor_copy(g.ident_b[:], tmp[:, 0, :]), [tmp.b], [g.ident_b.b])
        K.op("dve", lambda e: e.tensor_copy(g.ones_b[:], tmp[:, 1, :]), [tmp.b], [g.ones_b.b])
        K.op("dve", lambda e: e.tensor_copy(g.ones_f[:], tmp[:, 1, :]), [tmp.b], [g.ones_f.b])
        K.op("dve", lambda e: e.tensor_copy(g.triU_f[:], tmp[:, 2, :]), [tmp.b], [g.triU_f.b])
        K.op("dve", lambda e: e.tensor_copy(g.triL_f[:], tmp[:, 3, :]), [tmp.b], [g.triL_f.b])
        K.op("dve", lambda e: e.tensor_copy(g.negF_f[:], tmp[:, 4, :]), [tmp.b], [g.negF_f.b])
        K.op("dve", lambda e: e.tensor_copy(g.negB_f[:], tmp[:, 5, :]), [tmp.b], [g.negB_f.b])
        K.op("dve", lambda e: e.tensor_copy(g.mskF_b[:], tmp[:, 2, :]), [tmp.b], [g.mskF_b.b])
        K.op("dve", lambda e: e.tensor_copy(g.mskB_b[:], tmp[:, 3, :]), [tmp.b], [g.mskB_b.b])
        K.op("dve", lambda e: e.tensor_copy(g.mid_f[0][:], tmp[:, 6, :]), [tmp.b], [g.mid_f[0].b])
        K.op("dve", lambda e: e.tensor_copy(g.mid_f[1][:], tmp[:, 7, :]), [tmp.b], [g.mid_f[1].b])
        K.op("dve", lambda e: e.memset(g.epsc[:], EPS), [], [g.epsc.b])
        K.dma("sp", g.gcol[:], g.I["gcol"], g.gcol.b, writes=[g.gcol.b])
        K.barrier()


def host_consts():
    c = np.zeros((8, 128, 128), np.float32)
    k = np.arange(128)[:, None]
    t = np.arange(128)[None, :]
    c[0] = np.eye(128)
    c[1] = 1.0
    c[2] = (k <= t)
    c[3] = (k >= t)
    c[4] = np.where(k > t, NEG, 0.0)
    c[5] = np.where(k < t, NEG, 0.0)
    c[6] = (k <= t).astype(np.float32) - (k <= 63)
    c[7] = (k >= t).astype(np.float32) - (k >= 64)
    return c


def phase_mod(g):
    nc, K, I = g.nc, g.K, g.I
    NB = 256
    with ExitStack() as es:
        def sb(name, shape, dt):
            return Tile(es.enter_context(nc.sbuf_tensor(uniq(name), list(shape), dt)), name)
        sT = sb("m_sT", [128, KC, 2], F32)
        craw = sb("m_craw", [128, 2, KC], F32)
        bcol = sb("m_bcol", [128, DEPTH, 96], F32)
        wt = [sb("m_w%d" % i, [128, KC, NB], F32) for i in range(2)]
        K.dma("sp", craw[:], I["ccol"], craw.b, writes=[craw.b])
        K.dma("sp", bcol[:], I["bmodcol"], bcol.b, writes=[bcol.b])
        for s in range(2):
            K.op("act", lambda e, s=s: e.activation(out=sT[:, :, s], in_=craw[:, s, :], func=AF.Silu),
                 [craw.b], [sT.b])
        nblk = 6 * D // NB
        for li in range(DEPTH):
            ps = next_ps(g)
            for nb in range(nblk):
                w = wt[nb % 2]
                K.dma("sp" if nb % 2 == 0 else "act", w[:],
                      I["w_mod"][li][:, nb * NB:(nb + 1) * NB].rearrange("(k p) n -> p k n", p=128),
                      w.b, writes=[w.b])
                for jt in range(NB // 128):
                    j = nb * (NB // 128) + jt
                    for kc in range(KC):
                        K.op("pe", lambda e, kc=kc, jt=jt, j=j, w=w: e.matmul(
                            ps[:, 2 * j:2 * j + 2], w[:, kc, jt * 128:(jt + 1) * 128], sT[:, kc, :],
                            start=(kc == 0), stop=(kc == KC - 1)), [w.b, sT.b], [ps.b])
            for s in range(2):
                K.op("dve", lambda e, s=s, li=li: e.tensor_tensor(
                    g.mod[:, li * 2 + s, :], ps[:, s:192:2], bcol[:, li, :], ALU.add), [ps.b, bcol.b], [g.mod.b])
        for li in range(DEPTH):
            for s in range(2):
                for (mi, gj) in ((1, li), (4, 2 + li)):
                    K.op("dve", lambda e, li=li, s=s, mi=mi, gj=gj: e.scalar_tensor_tensor(
                        out=g.mod[:, li * 2 + s, mi * 16:(mi + 1) * 16], in0=g.mod[:, li * 2 + s, mi * 16:(mi + 1) * 16],
                        scalar=1.0, in1=g.gcol[:, gj, :], op0=ALU.add, op1=ALU.mult), [g.mod.b, g.gcol.b], [g.mod.b])
        if "s_MOD" in DEBUG.get("outs", ()):
            K.dma("sp", g.S["MOD"].rearrange("a p j -> p a j"), g.mod[:], g.mod.b, reads=[g.mod.b])
        K.barrier()


def phase_x0(g):
    nc, K, I, S = g.nc, g.K, g.I, g.S
    with ExitStack() as es:
        def sb(name, shape, dt):
            return Tile(es.enter_context(nc.sbuf_tensor(uniq(name), list(shape), dt)), name)
        xin = [sb("x0_in%d" % i, [128, D], F32) for i in range(2)]
        xo = [sb("x0_o%d" % i, [128, KC, 128], F32) for i in range(2)]
        for ti in range(NCH):
            src = I["ctx"][ti * 128:(ti + 1) * 128, :] if ti < 2 else I["x"][(ti - 2) * 128:(ti - 1) * 128, :]
            xi = xin[ti % 2]
            o = xo[ti % 2]
            K.dma("sp", xi[:], src, xi.b, writes=[xi.b])
            for q in range(4):
                ps = next_ps(g)
                for j in range(4):
                    kc = q * 4 + j
                    K.op("pe", lambda e, kc=kc, j=j, ps=ps, xi=xi: e.transpose(
                        ps[:, j * 128:(j + 1) * 128], xi[:, kc * 128:(kc + 1) * 128], g.ident_f[:]),
                        [xi.b, g.ident_f.b], [ps.b])
                eng = "act" if q % 2 == 0 else "dve"
                if eng == "act":
                    K.op("act", lambda e, q=q, ps=ps, o=o: e.copy(
                        o[:, q * 4:(q + 1) * 4, :], ps[:].rearrange("p (a b) -> p a b", a=4)), [ps.b], [o.b])
                else:
                    K.op("dve", lambda e, q=q, ps=ps, o=o: e.tensor_copy(
                        o[:, q * 4:(q + 1) * 4, :], ps[:].rearrange("p (a b) -> p a b", a=4)), [ps.b], [o.b])
            K.dma("sp", S["xT"].rearrange("(k p) t -> p k t", p=128)[:, :, ti * 128:(ti + 1) * 128], o[:], o.b,
                  reads=[o.b], writes=[g.dbuf("xT", ti)])
        K.barrier()


def lat_slices(lo, hi):
    return [(CTX + 512 * i, 512, 0) for i in range(lo, hi)]


def superblocks(with_ctx, per):
    sbs = []
    for i in range(0, 8, per):
        sl = lat_slices(i, min(8, i + per))
        sbs.append(sl)
    if with_ctx:
        sbs[0] = [(0, CTX, 1)] + sbs[0]
    return sbs


def bc_mid(ap2d, n):
    return ap2d.unsqueeze(1).to_broadcast([ap2d.shape[0], n, ap2d.shape[1]])


def bc_last(ap2d, n):
    return ap2d.unsqueeze(2).to_broadcast([ap2d.shape[0], ap2d.shape[1], n])


def norm_into(g, R, li, which, pos, n, stream, dst, dst_off):
    nc, K, S = g.nc, g.K, g.S
    mi_scale, mi_shift = (1, 0) if which == 1 else (4, 3)
    ms = li * 2 + stream
    xTv = S["xT"].rearrange("(k p) t -> p k t", p=128)
    for o in range(0, n, 256):
        xin = R["xin"][R["i"] % 2]
        R["i"] += 1
        sq, rstd = R["sq"], R["rstd"]
        K.dma("sp", xin[:], xTv[:, :, pos + o:pos + o + 256], xin.b,
              reads=[g.dbuf("xT", (pos + o) // 128), g.dbuf("xT", (pos + o) // 128 + 1)], writes=[xin.b])
        K.op("act", lambda e: e.activation(out=sq[:], in_=xin[:], func=AF.Square), [xin.b], [sq.b])
        ps = next_ps(g)
        for kc in range(KC):
            K.op("pe", lambda e, kc=kc: e.matmul(ps[:, :256], g.ones_b[:], sq[:, kc, :], start=(kc == 0),
                                                 stop=(kc == KC - 1)), [sq.b, g.ones_b.b], [ps.b])
        K.op("act", lambda e: e.activation(out=rstd[:], in_=ps[:, :256], func=AF.Sqrt, scale=1.0 / D,
                                           bias=g.epsc[:, 0:1]), [ps.b, g.epsc.b], [rstd.b])
        K.op("dve", lambda e: e.reciprocal(rstd[:], rstd[:]), [rstd.b], [rstd.b])
        K.op("dve", lambda e: e.tensor_tensor(xin[:], xin[:], bc_mid(rstd[:], KC), ALU.mult), [xin.b, rstd.b], [xin.b])
        for kc in range(KC):
            if kc % 2 == 0:
                K.op("act", lambda e, kc=kc: e.activation(
                    out=dst[:, kc, dst_off + o:dst_off + o + 256], in_=xin[:, kc, :], func=AF.Identity,
                    scale=g.mod[:, ms, mi_scale * 16 + kc:mi_scale * 16 + kc + 1],
                    bias=g.mod[:, ms, mi_shift * 16 + kc:mi_shift * 16 + kc + 1]), [xin.b, g.mod.b], [dst.b])
            else:
                K.op("dve", lambda e, kc=kc: e.tensor_scalar(
                    dst[:, kc, dst_off + o:dst_off + o + 256], xin[:, kc, :],
                    g.mod[:, ms, mi_scale * 16 + kc:mi_scale * 16 + kc + 1],
                    g.mod[:, ms, mi_shift * 16 + kc:mi_shift * 16 + kc + 1], ALU.mult, ALU.add), [xin.b, g.mod.b], [dst.b2])


def colmajor_row_dmas(pos):
    t0 = pos - CTX
    r0 = t0 // 64
    return [(0, CTX + r0, 64), (64, CTX + r0 + 1, 64)]


def phase_norm_proj(g, li):
    nc, K, I, S = g.nc, g.K, g.I, g.S
    jobs = []
    jobs.append((CA, 512, "UA", 0, False, False, False))
    for c0 in range(512, 1536, 512):
        jobs.append((CA + c0, 512, "UAf", c0 - 512, False, False, True))
    jobs.append((CA + 1536, 16, "GA", 0, True, False, False))
    for c0 in range(0, 2048, 512):
        jobs.append((CB + c0, 512, "UB", c0, False, True, False))
    jobs.append((CB + 2048, 16, "GB", 0, True, True, False))
    for c0 in range(0, 1024, 512):
        jobs.append((CC + c0, 512, "UCf", c0, False, False, True))
    for c0 in range(0, 1536, 512):
        jobs.append((CD + c0, 512, "UD", c0, False, True, False))
    jobs.append((CD + 1536, 32, "GD", 0, True, True, False))
    with ExitStack() as es:
        def sb(name, shape, dt):
            return Tile(es.enter_context(nc.sbuf_tensor(uniq(name), list(shape), dt)), name)
        hT = sb("np_hT", [128, KC, 2304], BF16)
        hT.b2 = Buf("np_hT_b2")
        R = {"xin": [sb("np_xin%d" % i, [128, KC, 256], F32) for i in range(2)],
             "sq": sb("np_sq", [128, KC, 256], BF16), "rstd": sb("np_rstd", [128, 256], F32), "i": 0}
        wt = [sb("np_w%d" % i, [128, KC, 512], BF16) for i in range(2)]
        stg = [sb("np_stg%d" % i, [128, 512], BF16) for i in range(3)]
        stf = [sb("np_stf%d" % i, [128, 32], F32) for i in range(2)]
        wi = 0
        si = 0
        for sbk in superblocks(True, 4):
            off = 0
            tiles = []
            sl = []
            for (pos, n, stream) in sbk:
                norm_into(g, R, li, 1, pos, n, stream, hT, off)
                for a in range(0, n, 128):
                    tiles.append((off + a, pos + a, stream))
                sl.append((off, pos, n))
                off += n
            for (c0, ncol, sname, dc, isf, cm, isfm) in jobs:
                w = wt[wi % 2]
                wi += 1
                K.dma("pool", w[:, :, :ncol], I["w_in"][li][:, c0:c0 + ncol].rearrange("(k p) n -> p k n", p=128),
                      w.b, writes=[w.b])
                dst = S[sname]
                W = dst.shape[1]
                if isfm:
                    for j in range(ncol // 128):
                        for (soff, pos, n) in sl:
                            ps = next_ps(g)
                            for kc in range(KC):
                                K.op("pe", lambda e, kc=kc, ps=ps, w=w, soff=soff, n=n, j=j: e.matmul(
                                    ps[:, :n], w[:, kc, j * 128:(j + 1) * 128], hT[:, kc, soff:soff + n], start=(kc == 0),
                                    stop=(kc == KC - 1)), [hT.b, hT.b2, w.b], [ps.b])
                            st = stg[si % 3]
                            si += 1
                            if si % 2 == 0:
                                K.op("act", lambda e, st=st, ps=ps, n=n: e.copy(st[:, :n], ps[:, :n]), [ps.b], [st.b])
                            else:
                                K.op("dve", lambda e, st=st, ps=ps, n=n: e.tensor_copy(st[:, :n], ps[:, :n]), [ps.b], [st.b])
                            K.dma("sp", dst[dc + j * 128:dc + (j + 1) * 128, pos:pos + n], st[:, :n], st.b, reads=[st.b],
                                  writes=[g.dbuf(sname, (dc, j, pos))])
                    continue
                for (toff, pos, stream) in tiles:
                    ps = next_ps(g)
                    for kc in range(KC):
                        K.op("pe", lambda e, kc=kc, ps=ps, w=w, toff=toff, ncol=ncol: e.matmul(
                            ps[:, :ncol], hT[:, kc, toff:toff + 128], w[:, kc, :ncol], start=(kc == 0),
                            stop=(kc == KC - 1)), [hT.b, hT.b2, w.b], [ps.b])
                    if isf:
                        st = stf[si % 2]
                    else:
                        st = stg[si % 3]
                    si += 1
                    if si % 2 == 0:
                        K.op("act", lambda e, st=st, ps=ps, ncol=ncol: e.copy(st[:, :ncol], ps[:, :ncol]), [ps.b], [st.b])
                    else:
                        K.op("dve", lambda e, st=st, ps=ps, ncol=ncol: e.tensor_copy(st[:, :ncol], ps[:, :ncol]), [ps.b], [st.b])
                    if cm and stream == 0:
                        for (plo, row0, rs) in colmajor_row_dmas(pos):
                            K.dma("sp", dram_ap(dst, row0 * W + dc, [[rs * W, 64], [1, ncol]]), st[plo:plo + 64, :ncol],
                                  st.b, reads=[st.b], writes=[g.dbuf(sname, (c0, pos, plo))])
                    else:
                        K.dma("sp", dst[pos:pos + 128, dc:dc + ncol], st[:, :ncol], st.b, reads=[st.b],
                              writes=[g.dbuf(sname, (c0, pos))])
        K.barrier()


def load_wtile_fm(g, w, wsrc, ot, kcn):
    g.K.dma("pool", w[:, :kcn, :], wsrc[:, ot * 128:(ot + 1) * 128].rearrange("(k p) n -> p k n", p=128),
            w.b, writes=[w.b])


def load_wgroup_fm(g, w, wsrc, og, kcn):
    g.K.dma("pool", w[:, :kcn, :], wsrc[:, og * 512:(og + 1) * 512].rearrange("(k p) n -> p k n", p=128),
            w.b, writes=[w.b])


def phase_wout(g, li, with_ctx):
    nc, K, I, S = g.nc, g.K, g.I, g.S
    xTv = S["xT"]
    with ExitStack() as es:
        def sb(name, shape, dt):
            return Tile(es.enter_context(nc.sbuf_tensor(uniq(name), list(shape), dt)), name)
        yT = sb("wo_yT", [128, KC, 2304], BF16)
        yin = [sb("wo_yin%d" % i, [128, D], BF16) for i in range(2)]
        wt = [sb("wo_w%d" % i, [128, KC, 512], BF16) for i in range(2)]
        xt = [sb("wo_xt%d" % i, [128, 512], F32) for i in range(3)]
        yi = wi = xi = 0
        for sbk in superblocks(with_ctx, 4):
            off = 0
            sl = []
            for (pos, n, stream) in sbk:
                for a in range(0, n, 128):
                    y = yin[yi % 2]
                    yi += 1
                    K.dma("sp", y[:], S["Y"][pos + a:pos + a + 128, :], y.b, reads=[g.dbuf("Y")], writes=[y.b])
                    for h in range(2):
                        pb = next_psb(g)
                        for j in range(8):
                            kc = h * 8 + j
                            K.op("pe", lambda e, kc=kc, j=j, pb=pb, y=y: e.transpose(
                                pb[:, j * 128:(j + 1) * 128], y[:, kc * 128:(kc + 1) * 128], g.ident_b[:]),
                                [y.b, g.ident_b.b], [pb.b])
                        dstv = yT[:, h * 8:(h + 1) * 8, off + a:off + a + 128]
                        srcv = pb[:].rearrange("p (a b) -> p a b", a=8)
                        if h == 0:
                            K.op("act", lambda e, dstv=dstv, srcv=srcv: e.copy(dstv, srcv), [pb.b], [yT.b])
                        else:
                            K.op("dve", lambda e, dstv=dstv, srcv=srcv: e.tensor_copy(dstv, srcv), [pb.b], [yT.b])
                sl.append((off, pos, n, stream))
                off += n
            for ot in range(KC):
                if ot % 4 == 0:
                    w = wt[wi % 2]
                    wi += 1
                    load_wgroup_fm(g, w, I["w_out"][li], ot // 4, KC)
                wc = (ot % 4) * 128
                for (soff, pos, n, stream) in sl:
                    x = xt[xi % 3]
                    xi += 1
                    blks = [g.dbuf("xT", (pos + a) // 128) for a in range(0, n, 128)]
                    K.dma("sp", x[:, :n], xTv[ot * 128:(ot + 1) * 128, pos:pos + n], x.b, reads=blks, writes=[x.b])
                    ps = next_ps(g)
                    for kc in range(KC):
                        K.op("pe", lambda e, kc=kc, ps=ps, w=w, soff=soff, n=n, wc=wc: e.matmul(
                            ps[:, :n], w[:, kc, wc:wc + 128], yT[:, kc, soff:soff + n], start=(kc == 0), stop=(kc == KC - 1)),
                            [w.b, yT.b], [ps.b])
                    gate = g.mod[:, li * 2 + stream, 2 * 16 + ot:2 * 16 + ot + 1]
                    K.op("dve", lambda e, x=x, ps=ps, n=n, gate=gate: e.scalar_tensor_tensor(
                        out=x[:, :n], in0=ps[:, :n], scalar=gate, in1=x[:, :n], op0=ALU.mult, op1=ALU.add),
                        [ps.b, x.b, g.mod.b], [x.b])
                    K.dma("sp", xTv[ot * 128:(ot + 1) * 128, pos:pos + n], x[:, :n], x.b, reads=[x.b], writes=blks)
        K.barrier()


def phase_ffn_up(g, li, with_ctx):
    nc, K, I, S = g.nc, g.K, g.I, g.S
    with ExitStack() as es:
        def sb(name, shape, dt):
            return Tile(es.enter_context(nc.sbuf_tensor(uniq(name), list(shape), dt)), name)
        hT = sb("fu_hT", [128, KC, 2304], BF16)
        hT.b2 = Buf("fu_hT_b2")
        R = {"xin": [sb("fu_xin%d" % i, [128, KC, 256], F32) for i in range(2)],
             "sq": sb("fu_sq", [128, KC, 256], BF16), "rstd": sb("fu_rstd", [128, 256], F32), "i": 0}
        wg = [sb("fu_wg%d" % i, [128, KC, 512], BF16) for i in range(2)]
        wu = [sb("fu_wu%d" % i, [128, KC, 512], BF16) for i in range(2)]
        sg = [sb("fu_sg%d" % i, [128, 512], F32) for i in range(2)]
        strip = [sb("fu_strip%d" % i, [128, 2304], BF16) for i in range(2)]
        wi = gi = 0
        for sbk in superblocks(with_ctx, 4):
            off = 0
            sl = []
            for (pos, n, stream) in sbk:
                norm_into(g, R, li, 2, pos, n, stream, hT, off)
                sl.append((off, pos, n, stream))
                off += n
            ntok = off
            p0 = sbk[0][0]
            for ft in range(KF):
                if ft % 4 == 0:
                    a, b = wg[wi % 2], wu[wi % 2]
                    wi += 1
                    load_wgroup_fm(g, a, I["w_gate"][li], ft // 4, KC)
                    load_wgroup_fm(g, b, I["w_up"][li], ft // 4, KC)
                wc = (ft % 4) * 128
                st = strip[ft % 2]
                for (soff, pos, n, stream) in sl:
                    pg = next_ps(g)
                    for kc in range(KC):
                        K.op("pe", lambda e, kc=kc, pg=pg, a=a, soff=soff, n=n, wc=wc: e.matmul(
                            pg[:, :n], a[:, kc, wc:wc + 128], hT[:, kc, soff:soff + n], start=(kc == 0), stop=(kc == KC - 1)),
                            [a.b, hT.b, hT.b2], [pg.b])
                    pu = next_ps(g)
                    for kc in range(KC):
                        K.op("pe", lambda e, kc=kc, pu=pu, b=b, soff=soff, n=n, wc=wc: e.matmul(
                            pu[:, :n], b[:, kc, wc:wc + 128], hT[:, kc, soff:soff + n], start=(kc == 0), stop=(kc == KC - 1)),
                            [b.b, hT.b, hT.b2], [pu.b])
                    s_ = sg[gi % 2]
                    gi += 1
                    K.op("act", lambda e, s_=s_, pg=pg, n=n: e.activation(out=s_[:, :n], in_=pg[:, :n], func=AF.Silu),
                         [pg.b], [s_.b])
                    K.op("dve", lambda e, s_=s_, pu=pu, n=n, st=st, soff=soff: e.tensor_tensor(
                        st[:, soff:soff + n], s_[:, :n], pu[:, :n], ALU.mult), [s_.b, pu.b], [st.b])
                K.dma("sp", S["HID"][ft * 128:(ft + 1) * 128, p0:p0 + ntok], st[:, :ntok], st.b, reads=[st.b],
                      writes=[g.dbuf("HID", (ft, p0))])
        K.barrier()


def phase_ffn_down(g, li, with_ctx):
    nc, K, I, S = g.nc, g.K, g.I, g.S
    xTv = S["xT"]
    sbs = [lat_slices(i, i + 2) for i in range(0, 8, 2)]
    if with_ctx:
        sbs = [[(0, CTX, 1)]] + sbs
    with ExitStack() as es:
        def sb(name, shape, dt):
            return Tile(es.enter_context(nc.sbuf_tensor(uniq(name), list(shape), dt)), name)
        hid = sb("fd_hid", [128, KF, 1024], BF16)
        hidq = [Buf("fd_hidq%d" % q) for q in range(4)]
        wt = [sb("fd_w%d" % i, [128, KF, 512], BF16) for i in range(2)]
        xt = [sb("fd_xt%d" % i, [128, 512], F32) for i in range(3)]
        wi = xi = 0
        for sbk in sbs:
            p0 = sbk[0][0]
            ntok = sum(s[1] for s in sbk)
            hv = S["HID"].rearrange("(k p) t -> p k t", p=128)
            for q in range(4):
                K.dma("sp", hid[:, q * 11:(q + 1) * 11, :ntok], hv[:, q * 11:(q + 1) * 11, p0:p0 + ntok], hidq[q],
                      reads=[g.dbuf("HID")], writes=[hidq[q]])
            for ot in range(KC):
                if ot % 4 == 0:
                    w = wt[wi % 2]
                    wi += 1
                    load_wgroup_fm(g, w, I["w_down"][li], ot // 4, KF)
                wc = (ot % 4) * 128
                for (pos, n, stream) in sbk:
                    soff = pos - p0
                    x = xt[xi % 3]
                    xi += 1
                    blks = [g.dbuf("xT", (pos + a) // 128) for a in range(0, n, 128)]
                    K.dma("sp", x[:, :n], xTv[ot * 128:(ot + 1) * 128, pos:pos + n], x.b, reads=blks, writes=[x.b])
                    ps = next_ps(g)
                    for kc in range(KF):
                        K.op("pe", lambda e, kc=kc, ps=ps, w=w, soff=soff, n=n, wc=wc: e.matmul(
                            ps[:, :n], w[:, kc, wc:wc + 128], hid[:, kc, soff:soff + n], start=(kc == 0), stop=(kc == KF - 1)),
                            [w.b, hidq[kc // 11]], [ps.b])
                    gate = g.mod[:, li * 2 + stream, 5 * 16 + ot:5 * 16 + ot + 1]
                    K.op("dve", lambda e, x=x, ps=ps, n=n, gate=gate: e.scalar_tensor_tensor(
                        out=x[:, :n], in0=ps[:, :n], scalar=gate, in1=x[:, :n], op0=ALU.mult, op1=ALU.add),
                        [ps.b, x.b, g.mod.b], [x.b])
                    K.dma("sp", xTv[ot * 128:(ot + 1) * 128, pos:pos + n], x[:, :n], x.b, reads=[x.b], writes=blks)
        K.barrier()


def phase_final(g):
    nc, K, I, S = g.nc, g.K, g.I, g.S
    xTv = S["xT"].rearrange("(k p) t -> p k t", p=128)
    with ExitStack() as es:
        def sb(name, shape, dt):
            return Tile(es.enter_context(nc.sbuf_tensor(uniq(name), list(shape), dt)), name)
        xin = [sb("fn_xin%d" % i, [128, KC, 128], F32) for i in range(2)]
        sq = sb("fn_sq", [128, KC, 128], BF16)
        rstd = sb("fn_rstd", [128, 128], F32)
        ot = [sb("fn_o%d" % i, [128, D], F32) for i in range(2)]
        for ti in range(L // 128):
            pos = CTX + ti * 128
            x = xin[ti % 2]
            o = ot[ti % 2]
            K.dma("sp", x[:], xTv[:, :, pos:pos + 128], x.b, reads=[g.dbuf("xT", pos // 128)], writes=[x.b])
            K.op("act", lambda e, x=x: e.activation(out=sq[:], in_=x[:], func=AF.Square), [x.b], [sq.b])
            ps = next_ps(g)
            for kc in range(KC):
                K.op("pe", lambda e, kc=kc, ps=ps: e.matmul(ps[:, :128], g.ones_b[:], sq[:, kc, :], start=(kc == 0),
                                                            stop=(kc == KC - 1)), [sq.b, g.ones_b.b], [ps.b])
            K.op("act", lambda e, ps=ps: e.activation(out=rstd[:], in_=ps[:, :128], func=AF.Sqrt, scale=1.0 / D,
                                                      bias=g.epsc[:, 0:1]), [ps.b, g.epsc.b], [rstd.b])
            K.op("dve", lambda e: e.reciprocal(rstd[:], rstd[:]), [rstd.b], [rstd.b])
            K.op("dve", lambda e, x=x: e.tensor_tensor(x[:], x[:], bc_mid(rstd[:], KC), ALU.mult), [x.b, rstd.b], [x.b])
            K.op("dve", lambda e, x=x: e.tensor_tensor(x[:], x[:], bc_last(g.gcol[:, 4, :], 128), ALU.mult),
                 [x.b, g.gcol.b], [x.b])
            for q in range(4):
                pt = next_ps(g)
                for j in range(4):
                    kc = q * 4 + j
                    K.op("pe", lambda e, kc=kc, j=j, pt=pt, x=x: e.transpose(
                        pt[:, j * 128:(j + 1) * 128], x[:, kc, :], g.ident_f[:]), [x.b, g.ident_f.b], [pt.b])
                if q % 2 == 0:
                    K.op("act", lambda e, q=q, pt=pt, o=o: e.copy(o[:, q * 512:(q + 1) * 512], pt[:]), [pt.b], [o.b])
                else:
                    K.op("dve", lambda e, q=q, pt=pt, o=o: e.tensor_copy(o[:, q * 512:(q + 1) * 512], pt[:]), [pt.b], [o.b])
            K.dma("sp", g.out[ti * 128:(ti + 1) * 128, :], o[:], o.b, reads=[o.b], writes=[g.dbuf("out", ti)])
        K.barrier()


def make_in_maps(inp, cores):
    f = lambda a: np.ascontiguousarray(np.asarray(a, dtype=np.float32))
    shared = {k: f(inp[k]) for k in ("w_mod", "w_in", "w_out", "w_gate", "w_up", "w_down")}
    shared["consts"] = host_consts()
    col = lambda v: f(v).reshape(-1, 128).T
    shared["gcol"] = f(np.stack([col(inp["norm1_g"][0]), col(inp["norm1_g"][1]), col(inp["norm2_g"][0]),
                                 col(inp["norm2_g"][1]), col(inp["final_g"])], 1))
    shared["bmodcol"] = f(np.stack([col(inp["b_mod"][i]) for i in range(DEPTH)], 1))
    shared.update(host_mixer_params(inp))
    maps = []
    for b in cores:
        m = dict(shared)
        m["x"] = f(inp["x"][b])
        m["ctx"] = f(inp["ctx"][b])
        m["ccol"] = f(np.stack([col(inp["c"][b]), col(inp["c_ctx"])], 1))
        maps.append(m)
    return maps


def host_mixer_params(inp):
    f = lambda a: np.asarray(a, dtype=np.float32)
    bc = lambda v: np.broadcast_to(f(v).reshape(1, -1), (128, f(v).size))
    out = {k: np.zeros(shp, np.float32) for k, shp in MIX_PARAM_SHAPES.items()}

    def convcols(w, b, nt):
        o = np.zeros((128, nt, 6), np.float32)
        o[:, :, 0:5] = f(w).T.reshape(nt, 128, 5).transpose(1, 0, 2)
        o[:, :, 5] = f(b).reshape(nt, 128).T
        return o
    for i in range(DEPTH):
        out["pA_tm"][i] = np.concatenate([bc(inp["ssd_dt_bias"][i]), bc(inp["ssd_a_log"][i]),
                                          bc(np.repeat(f(inp["ssd_d"][i]), 64)), bc(inp["ssd_norm_g"][i])], 1)
        out["pA_conv"][i] = convcols(inp["ssd_conv_w"][i], inp["ssd_conv_b"][i], 8)
        out["pB_tm"][i] = np.concatenate([bc(inp["ml_igate_b"][i]), bc(inp["ml_fgate_b"][i]), bc(inp["ml_norm_g"][i])], 1)
        out["pB_conv"][i] = convcols(inp["ml_conv_w"][i], inp["ml_conv_b"][i], 8)
        pc = np.zeros((128, 4, 12), np.float32)
        pc[:, :, 0:6] = convcols(inp["lru_conv_w"][i], inp["lru_conv_b"][i], 4)
        for d in range(2):
            pc[:, :, 6 + d] = f(inp["lru_ba"][i][d]).reshape(4, 128).T
            pc[:, :, 8 + d] = f(inp["lru_bx"][i][d]).reshape(4, 128).T
            pc[:, :, 10 + d] = f(inp["lru_lambda"][i][d]).reshape(4, 128).T
        out["pC_col"][i] = pc
        for d in range(2):
            for which, nm in enumerate(("lru_wa", "lru_wx")):
                w = f(inp[nm][i][d])
                for ct in range(4):
                    for nl in range(2):
                        out["pC_w"][i, (d * 2 + which) * 4 + ct, nl * 64:(nl + 1) * 64, nl * 64:(nl + 1) * 64] = w[2 * ct + nl]
        out["pD_tm"][i] = np.concatenate([bc(f(inp["gla_bg"][i]).reshape(-1)), bc(inp["gla_norm_g"][i])], 1)
        for d in range(2):
            out["pD_wg2"][i, d * 16:(d + 1) * 16, d * 256:(d + 1) * 256] = f(inp["gla_wg2"][i][d])
    return out


def kernel(**inputs):
    nc = build_program()
    cores = [0, 1, 2, 3]
    maps = make_in_maps(inputs, cores)
    res = run_bass_kernel_spmd(nc, maps, core_ids=list(range(len(cores))))
    return np.stack([np.asarray(r["out"], dtype=np.float32) for r in res.results], 0)


def load_fm_from_tm(g, sb, U, col0, ntile, name):
    K = g.K
    fm = sb(name, [128, ntile, T], BF16)
    tin = [sb(name + "_in%d" % i, [128, ntile * 128], BF16) for i in range(2)]
    ei = 0
    for c in range(NCH):
        t = tin[c % 2]
        K.dma("sp", t[:], U[c * 128:(c + 1) * 128, col0:col0 + ntile * 128], t.b, writes=[t.b])
        for h0 in range(0, ntile, 8):
            nb = min(8, ntile - h0)
            pb = next_psb(g)
            for j in range(nb):
                K.op("pe", lambda e, j=j, pb=pb, t=t, h0=h0: e.transpose(
                    pb[:, j * 128:(j + 1) * 128], t[:, (h0 + j) * 128:(h0 + j + 1) * 128], g.ident_b[:]),
                    [t.b, g.ident_b.b], [pb.b])
            dstv = fm[:, h0:h0 + nb, c * 128:(c + 1) * 128]
            srcv = pb[:, :nb * 128].rearrange("p (a b) -> p a b", a=nb)
            ei += 1
            if ei % 2 == 0:
                K.op("act", lambda e, dstv=dstv, srcv=srcv: e.copy(dstv, srcv), [pb.b], [fm.b])
            else:
                K.op("dve", lambda e, dstv=dstv, srcv=srcv: e.tensor_copy(dstv, srcv), [pb.b], [fm.b])
    return fm


def conv_fm(g, eng, src, srcb, dst, cw, ct):
    K = g.K
    for (a, b) in ((0, CTX), (CTX, T)):
        K.op(eng, lambda e, a=a, b=b: e.tensor_scalar(dst[:, a:b], src[:, a:b], cw[:, ct, 2:3], cw[:, ct, 5:6],
                                                       ALU.mult, ALU.add), [srcb, cw.b], [dst.b])
        for j in (0, 1, 3, 4):
            d = j - 2
            lo = a + max(0, -d)
            hi = b - max(0, d)
            K.op(eng, lambda e, lo=lo, hi=hi, d=d, j=j: e.scalar_tensor_tensor(
                out=dst[:, lo:hi], in0=src[:, lo + d:hi + d], scalar=cw[:, ct, j:j + 1], in1=dst[:, lo:hi],
                op0=ALU.mult, op1=ALU.add), [srcb, cw.b, dst.b], [dst.b])


CONV_BLOCKS = [(0, CTX, 0, CTX)] + [(CTX + 512 * i, 512, CTX, T) for i in range(8)]


def build_diag(g, sb, cw, ntile, name):
    K = g.K
    dg = sb(name, [128, ntile * 5, 128], BF16)
    for ct in range(ntile):
        for j in range(5):
            K.op("dve", lambda e, ct=ct, j=j: e.tensor_scalar_mul(dg[:, ct * 5 + j, :], g.ident_f[:], cw[:, ct, j:j + 1]),
                 [g.ident_f.b, cw.b], [dg.b])
    return dg


def conv_pe(g, src, srcb, dg, dct, cw, cct, func, out_ap_fn, dstb):
    K = g.K
    for (t0, n, a, b) in CONV_BLOCKS:
        ps = next_ps(g)
        for idx, j in enumerate((2, 0, 1, 3, 4)):
            d = j - 2
            lo = max(t0, a - d)
            hi = min(t0 + n, b - d)
            K.op("pe", lambda e, ps=ps, lo=lo, hi=hi, d=d, j=j, t0=t0, idx=idx: e.matmul(
                ps[:, lo - t0:hi - t0], dg[:, dct * 5 + j, :], src[:, lo + d:hi + d], start=(idx == 0), stop=(idx == 4)),
                [dg.b, srcb], [ps.b])
        K.op("act", lambda e, ps=ps, t0=t0, n=n: e.activation(out=out_ap_fn(t0, n), in_=ps[:, :n], func=func,
                                                              bias=cw[:, cct, 5:6]), [ps.b, cw.b], [dstb])


def rstd_cols(g, out, in_, scale, n_reads, out_b):
    K = g.K
    K.op("act", lambda e: e.activation(out=out, in_=in_, func=AF.Ln, scale=scale, bias=g.epsc[:, 0:1]),
         n_reads + [g.epsc.b], [out_b])
    K.op("act", lambda e: e.activation(out=out, in_=out, func=AF.Exp, scale=-0.5), [out_b], [out_b])


def cm_store(g, dst, dcol, ncol, c, st, W=D):
    K = g.K
    if c < 2:
        K.dma("sp", dst[c * 128:(c + 1) * 128, dcol:dcol + ncol], st[:, :ncol], st.b, reads=[st.b],
              writes=[g.dbuf("Ycm", (dcol, c))])
        return
    for hf in range(2):
        col = 2 * (c - 2) + hf
        K.dma("sp", dram_ap(dst, (CTX + col) * W + dcol, [[64 * W, 64], [1, ncol]]), st[hf * 64:(hf + 1) * 64, :ncol],
              st.b, reads=[st.b], writes=[g.dbuf("Ycm", (dcol, c, hf))])


def chunk_order(d):
    return [0, 1] + list(range(2, NCH)) if d == 0 else [1, 0] + list(range(NCH - 1, 1, -1))


def phase_ssd(g, li, need_ctx):
    nc, K, I, S = g.nc, g.K, g.I, g.S
    P_intra, P_upd, P_inter = g.psum[0], g.psum[1], g.psum[2]
    with ExitStack() as es:
        def sb(name, shape, dt):
            return Tile(es.enter_context(nc.sbuf_tensor(uniq(name), list(shape), dt)), name)
        ptm = sb("a_ptm", [128, 1056], F32)
        cw = sb("a_cw", [128, 8, 6], F32)
        K.dma("sp", ptm[:], I["pA_tm"][li], ptm.b, writes=[ptm.b])
        K.dma("sp", cw[:], I["pA_conv"][li], cw.b, writes=[cw.b])
        fm = sb("a_fm", [128, 8, T], BF16)
        with ExitStack() as es2:
            def sb2(name, shape, dt):
                return Tile(es2.enter_context(nc.sbuf_tensor(uniq(name), list(shape), dt)), name)
            fpre = sb2("a_fpre", [128, 8, T], BF16)
            for ct in range(8):
                K.dma("sp", fpre[:, ct, :], S["UAf"][ct * 128:(ct + 1) * 128, :], fpre.b, writes=[fpre.b])
            dg = build_diag(g, sb2, cw, 8, "a_dg")
            for ct in range(8):
                conv_pe(g, fpre[:, ct, :], fpre.b, dg, ct, cw, ct, AF.Silu, lambda t0, n, ct=ct: fm[:, ct, t0:t0 + n], fm.b)
            K.barrier()
        ga = sb("a_ga", [128, NCH, 16], F32)
        dt = sb("a_dt", [128, NCH, 16], F32)
        loga = sb("a_loga", [128, NCH, 16], F32)
        lndt = sb("a_lndt", [128, NCH, 16], F32)
        nega = sb("a_nega", [128, 16], F32)
        K.dma("sp", ga[:], S["GA"].rearrange("(c p) j -> p c j", p=128), ga.b, writes=[ga.b])
        K.op("dve", lambda e: e.tensor_tensor(ga[:], ga[:], bc_mid(ptm[:, 0:16], NCH), ALU.add), [ga.b, ptm.b], [ga.b])
        K.op("act", lambda e: e.activation(out=dt[:], in_=ga[:], func=AF.Exp), [ga.b], [dt.b])
        K.op("act", lambda e: e.activation(out=dt[:], in_=dt[:], func=AF.Ln, bias=g.ones_f[:, 0:1]), [dt.b, g.ones_f.b], [dt.b])
        K.op("act", lambda e: e.activation(out=lndt[:], in_=dt[:], func=AF.Ln), [dt.b], [lndt.b])
        K.op("act", lambda e: e.activation(out=nega[:], in_=ptm[:, 16:32], func=AF.Exp), [ptm.b], [nega.b])
        K.op("dve", lambda e: e.tensor_scalar_mul(nega[:], nega[:], -1.0), [nega.b], [nega.b])
        K.op("dve", lambda e: e.tensor_tensor(loga[:], dt[:], bc_mid(nega[:], NCH), ALU.mult), [dt.b, nega.b], [loga.b])

        zA = sb("a_zA", [128, NCH, 512], BF16)
        K.dma("sp", zA[:], S["UA"][:, 0:512].rearrange("(c p) n -> p c n", p=128), zA.b, writes=[zA.b])
        for (c0_, c1_) in ((0, 17), (17, NCH)):
            K.op("act", lambda e, c0_=c0_, c1_=c1_: e.activation(out=zA[:, c0_:c1_, :], in_=zA[:, c0_:c1_, :], func=AF.Silu), [zA.b], [zA.b])
        ST = sb("a_ST", [128, 512], F32)
        STb = sb("a_STb", [128, 512], BF16)
        xs = [sb("a_xs%d" % i, [128, 768], BF16) for i in range(2)]
        xw = [sb("a_xw%d" % i, [128, 512], BF16) for i in range(2)]
        bc = sb("a_bc", [128, 8], F32)
        expb = sb("a_expb", [128, 8], F32)
        biasc = sb("a_biasc", [128, 8], F32)
        ebl = sb("a_ebl", [128, 8], F32)
        cbs = sb("a_cbs", [128, 256], F32)
        Rt = [sb("a_R%d" % i, [128, 8, 128], F32) for i in range(2)]
        neg4 = [sb("a_neg4_%d" % i, [128, 4, 128], BF16) for i in range(2)]
        for i, nm in enumerate((g.negF_f, g.negB_f)):
            K.op("dve", lambda e, i=i, nm=nm: e.tensor_copy(neg4[i][:], bc_mid(nm[:], 4)), [nm.b], [neg4[i].b])
        PD = [g.psum[3], g.psum[4]]
        E = [sb("a_E%d" % i, [128, 128], F32) for i in range(2)]
        wT = [sb("a_wT%d" % i, [128, 128], BF16) for i in range(2)]
        yint = sb("a_yint", [128, 512], F32)
        t1 = sb("a_t1", [128, 512], F32)
        t2 = sb("a_t2", [128, 512], F32)
        y = [sb("a_y%d" % i, [128, 512], F32) for i in range(2)]
        yf = [sb("a_yf%d" % i, [128, 512], F32) for i in range(2)]
        zt = [sb("a_z%d" % i, [128, 512], BF16) for i in range(2)]
        szs = [sb("a_sz%d" % i, [128, 512], F32) for i in range(2)]
        ss = sb("a_ss", [128, 1], F32)
        ob = [sb("a_ob%d" % i, [128, 512], BF16) for i in range(2)]
        YF = nc.dram_tensor("s_YFA_%d" % li, [T, 512], F32, kind="Internal").ap()
        iters = [(d_, c_) for d_ in range(2) for c_ in chunk_order(d_)]

        def pre(j):
            if j > len(iters):
                return
            d_, c_ = iters[j - 1]
            if d_ == 1 and not (c_ < 2 and not need_ctx):
                K.dma("sp", yf[j % 2][:], YF[c_ * 128:(c_ + 1) * 128, :], yf[j % 2].b, reads=[g.dbuf("YFA", c_)],
                      writes=[yf[j % 2].b])
        bcs = [sb("a_bcs%d" % i, [128, 8], F32) for i in range(2)]
        expbs = [sb("a_expbs%d" % i, [128, 8], F32) for i in range(2)]
        biascs = [sb("a_biascs%d" % i, [128, 8], F32) for i in range(2)]
        cbss = [sb("a_cbss%d" % i, [128, 256], F32) for i in range(2)]
        dirc = [(g.triU_f, 127), (g.triL_f, 0)]

        def prep(j):
            if j > len(iters):
                return
            d_, c_ = iters[j - 1]
            q0 = c_ * 128
            par = j % 2
            tri_ = dirc[d_][0]
            xsb_ = xs[par]
            pb = next_psb(g)
            for jj in range(6):
                K.op("pe", lambda e, jj=jj, pb=pb: e.transpose(pb[:, jj * 128:(jj + 1) * 128], fm[:, jj, q0:q0 + 128],
                                                               g.ident_b[:]), [fm.b, g.ident_b.b], [pb.b])
            K.op("act", lambda e, pb=pb: e.copy(xsb_[:], pb[:, :768]), [pb.b], [xsb_.b])
            pq = next_ps(g, 5)
            K.op("pe", lambda e, pq=pq: e.matmul(pq[:, 0:8], tri_[:], loga[:, c_, d_ * 8:(d_ + 1) * 8], start=True,
                                                 stop=True), [tri_.b, loga.b], [pq.b])
            K.op("dve", lambda e, pq=pq: e.tensor_copy(bcs[par][:], pq[:, 0:8]), [pq.b], [bcs[par].b])
            K.op("act", lambda e, pq=pq: e.activation(out=expbs[par][:], in_=pq[:, 0:8], func=AF.Exp), [pq.b], [expbs[par].b])
            K.op("dve", lambda e: e.tensor_tensor(biascs[par][:], lndt[:, c_, d_ * 8:(d_ + 1) * 8], bcs[par][:], ALU.subtract),
                 [lndt.b, bcs[par].b], [biascs[par].b])
            pcb = next_ps(g, 5)
            for gi in range(2):
                K.op("pe", lambda e, gi=gi, pcb=pcb: e.matmul(
                    pcb[:, gi * 128:(gi + 1) * 128], fm[:, 4 + gi, q0:q0 + 128], fm[:, 6 + gi, q0:q0 + 128],
                    start=True, stop=True), [fm.b], [pcb.b])
            K.op("act", lambda e, pcb=pcb: e.copy(cbss[par][:], pcb[:, :256]), [pcb.b], [cbss[par].b])
            Rb = Rt[par]
            K.op("dve", lambda e: e.tensor_tensor(Rb[:], bc_mid(tri_[:], 8), bc_last(loga[:, c_, d_ * 8:(d_ + 1) * 8], 128), ALU.mult),
                 [tri_.b, loga.b], [Rb.b])
            for hf in range(2):
                K.op("pe", lambda e, hf=hf: e.matmul(PD[hf][:], g.ones_f[:], Rb[:, hf * 4:(hf + 1) * 4, :].rearrange("p a b -> p (a b)"),
                                                     start=True, stop=False), [g.ones_f.b, Rb.b], [PD[hf].b])
                K.op("pe", lambda e, hf=hf: e.matmul(PD[hf][:], g.ident_b[:], neg4[d_][:].rearrange("p a b -> p (a b)"),
                                                     start=False, stop=True), [g.ident_b.b, neg4[d_].b], [PD[hf].b])

        late_fn = None
        pre(1)
        prep(1)
        for ci, (d, c) in enumerate(iters, 1):
            p0 = c * 128
            last = dirc[d][1]
            par = ci % 2
            if ci == 1 or ci == NCH + 1:
                K.op("dve", lambda e: e.memset(ST[:], 0.0), [], [ST.b])
                K.op("dve", lambda e: e.memset(STb[:], 0.0), [], [STb.b])
            pre(ci + 1)
            xsb, xwb, yb = xs[par], xw[par], y[par]
            biasc, expb, cbs = biascs[par], expbs[par], cbss[par]
            for gi in range(2):
                K.op("pe", lambda e, gi=gi: e.matmul(
                    P_inter[:, gi * 256:(gi + 1) * 256], fm[:, 6 + gi, p0:p0 + 128], STb[:, gi * 256:(gi + 1) * 256],
                    start=True, stop=True), [fm.b, STb.b], [P_inter.b])
            for h in range(8):
                gi = h // 4
                eh, wh = E[h % 2], wT[h % 2]
                hs = slice(h * 64, (h + 1) * 64)
                pD = PD[h // 4]
                dc = (h % 4) * 128
                K.op("act", lambda e, pD=pD, eh=eh, h=h, dc=dc: e.activation(out=eh[:], in_=pD[:, dc:dc + 128], func=AF.Exp,
                                                                            bias=biasc[:, h:h + 1]), [pD.b, biasc.b], [eh.b])
                K.op("act", lambda e, pD=pD, h=h, dc=dc: e.activation(out=ebl[:, h:h + 1], in_=pD[:, dc + last:dc + last + 1],
                                                                     func=AF.Exp), [pD.b], [ebl.b])
                K.op("dve", lambda e, wh=wh, eh=eh, gi=gi: e.tensor_tensor(wh[:], cbs[:, gi * 128:(gi + 1) * 128], eh[:],
                                                                          ALU.mult), [cbs.b, eh.b], [wh.b])
                K.op("pe", lambda e, wh=wh, hs=hs: e.matmul(P_intra[:, hs], wh[:], xsb[:, hs], start=True, stop=True),
                     [wh.b, xsb.b], [P_intra.b])
                K.op("dve", lambda e, eh=eh, hs=hs: e.tensor_scalar_mul(xwb[:, hs], xsb[:, hs], eh[:, last:last + 1]),
                     [xsb.b, eh.b], [xwb.b])
                K.op("pe", lambda e, gi=gi, hs=hs: e.matmul(
                    P_upd[:, hs], xsb[:, 512 + gi * 128:512 + (gi + 1) * 128], xwb[:, hs], start=True, stop=True),
                    [xsb.b, xwb.b], [P_upd.b])
            prep(ci + 1)
            K.op("act", lambda e: e.copy(yint[:], P_intra[:]), [P_intra.b], [yint.b])
            K.op("dve", lambda e: e.tensor_tensor(t1[:].rearrange("p (h q) -> p h q", h=8),
                                                  P_inter[:].rearrange("p (h q) -> p h q", h=8), bc_last(expb[:], 64), ALU.mult),
                 [P_inter.b, expb.b], [t1.b])
            K.op("dve", lambda e: e.tensor_tensor(yb[:], t1[:], yint[:], ALU.add), [t1.b, yint.b], [yb.b])
            K.op("dve", lambda e: e.tensor_tensor(ST[:].rearrange("p (h q) -> p h q", h=8),
                                                  ST[:].rearrange("p (h q) -> p h q", h=8), bc_last(ebl[:], 64), ALU.mult),
                 [ST.b, ebl.b], [ST.b])
            K.op("dve", lambda e: e.tensor_tensor(ST[:], ST[:], P_upd[:], ALU.add), [ST.b, P_upd.b], [ST.b])
            K.op("act", lambda e: e.copy(STb[:], ST[:]), [ST.b], [STb.b])
            if late_fn is not None:
                late_fn()
                late_fn = None
            if c < 2 and not need_ctx:
                continue
            if d == 0:
                K.dma("sp", YF[p0:p0 + 128, :], yb[:], yb.b, reads=[yb.b], writes=[g.dbuf("YFA", c)])
                continue
            yfb, o = yf[par], ob[par]
            K.op("pool", lambda e: e.tensor_tensor(yb[:], yb[:], yfb[:], ALU.add), [yb.b, yfb.b], [yb.b])
            K.op("pool", lambda e: e.tensor_tensor(t2[:], xsb[:, 0:512], ptm[:, 32:544], ALU.mult), [xsb.b, ptm.b], [t2.b])
            K.op("pool", lambda e: e.tensor_tensor(yb[:], yb[:], t2[:], ALU.add), [yb.b, t2.b], [yb.b])
            K.op("pool", lambda e: e.tensor_tensor(yb[:], yb[:], zA[:, c, :], ALU.mult), [yb.b, zA.b], [yb.b])
            K.op("pool", lambda e: e.tensor_tensor(t2[:], yb[:], yb[:], ALU.mult), [yb.b], [t2.b])

            def _late(yb=yb, o=o, p0=p0, c=c):
                K.op("dve", lambda e: e.tensor_reduce(ss[:], t2[:], AX.X, ALU.add), [t2.b], [ss.b])
                rstd_cols(g, ss[:], ss[:], 1.0 / 512, [ss.b], ss.b)
                K.op("dve", lambda e: e.scalar_tensor_tensor(out=o[:], in0=yb[:], scalar=ss[:, 0:1], in1=ptm[:, 544:1056],
                                                             op0=ALU.mult, op1=ALU.mult), [yb.b, ss.b, ptm.b], [o.b])
                K.dma("sp", S["Y"][p0:p0 + 128, 0:512], o[:], o.b, reads=[o.b], writes=[g.dbuf("Y", ("A", c))])
            late_fn = _late
        if late_fn is not None:
            late_fn()
        K.barrier()


def phase_mlstm(g, li, need_ctx):
    nc, K, I, S = g.nc, g.K, g.I, g.S
    P_intra, P_upd, P_inter = g.psum[0], g.psum[1], g.psum[2]
    LNS = math.log(128.0 ** -0.5)
    with ExitStack() as es:
        def sb(name, shape, dt):
            return Tile(es.enter_context(nc.sbuf_tensor(uniq(name), list(shape), dt)), name)
        ptm = sb("b_ptm", [128, 528], F32)
        cw = sb("b_cw", [128, 8, 6], F32)
        K.dma("sp", ptm[:], I["pB_tm"][li], ptm.b, writes=[ptm.b])
        K.dma("sp", cw[:], I["pB_conv"][li], cw.b, writes=[cw.b])
        fm = sb("b_fm", [128, 8, T], BF16)
        with ExitStack() as es2:
            def sb2(name, shape, dt):
                return Tile(es2.enter_context(nc.sbuf_tensor(uniq(name), list(shape), dt)), name)
            fpre = load_fm_from_tm(g, sb2, S["UB"], 0, 8, "b_fpre")
            dg = build_diag(g, sb2, cw, 8, "b_dg")
            for ct in range(8):
                conv_pe(g, fpre[:, ct, :], fpre.b, dg, ct, cw, ct, AF.Silu, lambda t0, n, ct=ct: fm[:, ct, t0:t0 + n], fm.b)
            K.barrier()
        gb = sb("b_gb", [128, NCH, 16], F32)
        igb = sb("b_igb", [128, NCH, 8], F32)
        logf = sb("b_logf", [128, NCH, 8], F32)
        K.dma("sp", gb[:], S["GB"].rearrange("(c p) j -> p c j", p=128), gb.b, writes=[gb.b])
        K.op("dve", lambda e: e.tensor_tensor(gb[:], gb[:], bc_mid(ptm[:, 0:16], NCH), ALU.add), [gb.b, ptm.b], [gb.b])
        K.op("dve", lambda e: e.tensor_scalar_add(igb[:], gb[:, :, 0:8], LNS), [gb.b], [igb.b])
        K.op("act", lambda e: e.activation(out=logf[:], in_=gb[:, :, 8:16], func=AF.Exp, scale=-1.0), [gb.b], [logf.b])
        K.op("act", lambda e: e.activation(out=logf[:], in_=logf[:], func=AF.Ln, bias=g.ones_f[:, 0:1]), [logf.b, g.ones_f.b], [logf.b])
        K.op("dve", lambda e: e.tensor_scalar_mul(logf[:], logf[:], -1.0), [logf.b], [logf.b])

        oA = sb("b_oA", [128, NCH, 512], BF16)
        K.dma("sp", oA[:], S["UB"][:, 1536:2048].rearrange("(c p) n -> p c n", p=128), oA.b, writes=[oA.b])
        for (c0_, c1_) in ((0, 17), (17, NCH)):
            K.op("act", lambda e, c0_=c0_, c1_=c1_: e.activation(out=oA[:, c0_:c1_, :], in_=oA[:, c0_:c1_, :], func=AF.Sigmoid), [oA.b], [oA.b])
        Cst = sb("b_C", [128, 4, 129], F32)
        Cb = sb("b_Cb", [128, 4, 129], BF16)
        ktm = [sb("b_ktm%d" % i, [128, 512], BF16) for i in range(2)]
        vp = [sb("b_vp%d" % i, [128, 4, 129], BF16) for i in range(2)]
        vpw = [sb("b_vpw%d" % i, [128, 129], BF16) for i in range(2)]
        bc = sb("b_bc", [128, 4], F32)
        expb = sb("b_expb", [128, 4], F32)
        biasc = sb("b_biasc", [128, 4], F32)
        ebl = sb("b_ebl", [128, 4], F32)
        Rt = [sb("b_R%d" % i, [128, 4, 128], F32) for i in range(2)]
        neg4 = [sb("b_neg4_%d" % i, [128, 4, 128], BF16) for i in range(2)]
        for i, nm in enumerate((g.negF_f, g.negB_f)):
            K.op("dve", lambda e, i=i, nm=nm: e.tensor_copy(neg4[i][:], bc_mid(nm[:], 4)), [nm.b], [neg4[i].b])
        PD = g.psum[3]
        E4 = [sb("b_E4_%d" % i, [128, 4, 128], F32) for i in range(2)]
        wT4 = [sb("b_wT4_%d" % i, [128, 4, 128], BF16) for i in range(2)]
        vw4 = [sb("b_vw4_%d" % i, [128, 4, 128], BF16) for i in range(2)]
        ws4 = [sb("b_ws4_%d" % i, [128, 4], BF16) for i in range(2)]
        yint4 = sb("b_yint4", [128, 512], F32)
        dsb = sb("b_dsb", [128, 12], F32)
        t4 = sb("b_t4", [128, 4], F32)
        PK = g.psum[4]
        y = [sb("b_y%d" % i, [128, 4, 129], F32) for i in range(2)]
        yf = [sb("b_yf%d" % i, [128, 4, 129], F32) for i in range(2)]
        ot = [sb("b_o%d" % i, [128, 512], BF16) for i in range(2)]
        sos = [sb("b_so%d" % i, [128, 512], F32) for i in range(2)]
        yns = [sb("b_yn%d" % i, [128, 4, 128], F32) for i in range(2)]
        tq = sb("b_tq", [128, 4, 128], F32)
        t1 = sb("b_t1", [128, 4, 128], F32)
        den = sb("b_den", [128, 4], F32)
        ss = sb("b_ss", [128, 4], F32)
        ob = [sb("b_ob%d" % i, [128, 512], BF16) for i in range(2)]
        for v_ in vp:
            K.op("dve", lambda e, v_=v_: e.memset(v_[:], 1.0), [], [v_.b])
        YF = nc.dram_tensor("s_YFB_%d" % li, [T, 4 * 129], F32, kind="Internal").ap()
        iters = [(d_, c_) for d_ in range(2) for c_ in chunk_order(d_)]

        def pre(j):
            if j > len(iters):
                return
            d_, c_ = iters[j - 1]
            q0 = c_ * 128
            K.dma("sp", vp[j % 2][:, :, 0:128], S["UB"][q0:q0 + 128, 1024:1536].rearrange("p (h q) -> p h q", h=4), vp[j % 2].b,
                  writes=[vp[j % 2].b])
            if d_ == 1 and not (c_ < 2 and not need_ctx):
                K.dma("sp", yf[j % 2][:, :, 0:128], YF[q0:q0 + 128, 0:512].rearrange("p (h q) -> p h q", h=4), yf[j % 2].b,
                      reads=[g.dbuf("YFB", c_)], writes=[yf[j % 2].b])
        bcs = [sb("b_bcs%d" % i, [128, 4], F32) for i in range(2)]
        expbs = [sb("b_expbs%d" % i, [128, 4], F32) for i in range(2)]
        biascs = [sb("b_biascs%d" % i, [128, 4], F32) for i in range(2)]

        def prep(j):
            if j > len(iters):
                return
            d_, c_ = iters[j - 1]
            q0 = c_ * 128
            par = j % 2
            tri_ = g.triU_f if d_ == 0 else g.triL_f
            kt_ = ktm[par]
            pb = next_psb(g)
            for jj in range(4):
                K.op("pe", lambda e, jj=jj: e.transpose(pb[:, jj * 128:(jj + 1) * 128], fm[:, 4 + jj, q0:q0 + 128],
                                                        g.ident_b[:]), [fm.b, g.ident_b.b], [pb.b])
            K.op("act", lambda e: e.copy(kt_[:], pb[:, :512]), [pb.b], [kt_.b])
            pq = next_ps(g, 5)
            K.op("pe", lambda e: e.matmul(pq[:, 0:4], tri_[:], logf[:, c_, d_ * 4:(d_ + 1) * 4], start=True, stop=True),
                 [tri_.b, logf.b], [pq.b])
            K.op("dve", lambda e: e.tensor_copy(bcs[par][:], pq[:, 0:4]), [pq.b], [bcs[par].b])
            K.op("act", lambda e: e.activation(out=expbs[par][:], in_=pq[:, 0:4], func=AF.Exp), [pq.b], [expbs[par].b])
            K.op("dve", lambda e: e.tensor_tensor(biascs[par][:], igb[:, c_, d_ * 4:(d_ + 1) * 4], bcs[par][:], ALU.subtract),
                 [igb.b, bcs[par].b], [biascs[par].b])
            Rb = Rt[par]
            K.op("dve", lambda e: e.tensor_tensor(Rb[:], bc_mid(tri_[:], 4), bc_last(logf[:, c_, d_ * 4:(d_ + 1) * 4], 128), ALU.mult),
                 [tri_.b, logf.b], [Rb.b])
            K.op("pe", lambda e: e.matmul(PD[:], g.ones_f[:], Rb[:].rearrange("p a b -> p (a b)"), start=True, stop=False),
                 [g.ones_f.b, Rb.b], [PD.b])
            K.op("pe", lambda e: e.matmul(PD[:], g.ident_b[:], neg4[d_][:].rearrange("p a b -> p (a b)"), start=False, stop=True),
                 [g.ident_b.b, neg4[d_].b], [PD.b])
        ci = 0
        late_fn = None
        for d in range(2):
            tri = g.triU_f if d == 0 else g.triL_f
            negm = g.negF_f if d == 0 else g.negB_f
            last = 127 if d == 0 else 0
            K.op("dve", lambda e: e.memset(Cst[:], 0.0), [], [Cst.b])
            K.op("dve", lambda e: e.memset(Cb[:], 0.0), [], [Cb.b])
            for c in chunk_order(d):
                p0 = c * 128
                ci += 1
                if ci == 1:
                    pre(1)
                    prep(1)
                pre(ci + 1)
                kt, vb, yb = ktm[ci % 2], vp[ci % 2], y[ci % 2]
                so, o_ = sos[ci % 2], ot[ci % 2]
                biasc, expb = biascs[ci % 2], expbs[ci % 2]
                Eb, wTb, vwb, wsb = E4[ci % 2], wT4[ci % 2], vw4[ci % 2], ws4[ci % 2]
                for h in range(4):
                    K.op("act", lambda e, h=h, Eb=Eb: e.activation(out=Eb[:, h, :], in_=PD[:, h * 128:(h + 1) * 128], func=AF.Exp,
                                                                   bias=biasc[:, h:h + 1]), [PD.b, biasc.b], [Eb.b])
                K.op("act", lambda e: e.activation(out=ebl[:], in_=PD[:, last:512:128], func=AF.Exp), [PD.b], [ebl.b])
                for h in range(4):
                    K.op("pe", lambda e, h=h: e.matmul(PK[:, h * 128:(h + 1) * 128], fm[:, 4 + h, p0:p0 + 128], fm[:, h, p0:p0 + 128],
                                                       start=True, stop=True), [fm.b], [PK.b])
                K.op("dve", lambda e, wTb=wTb, Eb=Eb: e.tensor_tensor(wTb[:], PK[:].rearrange("p (a b) -> p a b", a=4), Eb[:], ALU.mult),
                     [PK.b, Eb.b], [wTb.b])
                K.op("dve", lambda e, vwb=vwb, Eb=Eb, vb=vb: e.tensor_tensor(vwb[:], vb[:, :, 0:128], bc_last(Eb[:, :, last], 128), ALU.mult),
                     [vb.b, Eb.b], [vwb.b])
                K.op("dve", lambda e, wsb=wsb, Eb=Eb: e.tensor_copy(wsb[:], Eb[:, :, last]), [Eb.b], [wsb.b])
                PN = next_ps(g, 5)
                for h in range(4):
                    hs = slice(h * 128, (h + 1) * 128)
                    K.op("pe", lambda e, h=h, hs=hs, wTb=wTb, vb=vb: e.matmul(P_intra[:, hs], wTb[:, h, :], vb[:, h, 0:128], start=True, stop=True),
                         [wTb.b, vb.b], [P_intra.b])
                    K.op("pe", lambda e, h=h, hs=hs: e.matmul(P_inter[:, hs], fm[:, h, p0:p0 + 128], Cb[:, h, 0:128], start=True, stop=True),
                         [fm.b, Cb.b], [P_inter.b])
                    K.op("pe", lambda e, h=h, hs=hs, kt=kt, vwb=vwb: e.matmul(P_upd[:, hs], kt[:, hs], vwb[:, h, :], start=True, stop=True),
                         [kt.b, vwb.b], [P_upd.b])
                    K.op("pe", lambda e, h=h, PN=PN, wTb=wTb: e.matmul(PN[:, h:h + 1], wTb[:, h, :], g.ones_b[:, 0:1], start=True, stop=True),
                         [wTb.b, g.ones_b.b], [PN.b])
                    K.op("pe", lambda e, h=h, PN=PN: e.matmul(PN[:, 4 + h:5 + h], fm[:, h, p0:p0 + 128], Cb[:, h, 128:129], start=True, stop=True),
                         [fm.b, Cb.b], [PN.b])
                    K.op("pe", lambda e, h=h, hs=hs, PN=PN, kt=kt, wsb=wsb: e.matmul(PN[:, 8 + h:9 + h], kt[:, hs], wsb[:, h:h + 1], start=True, stop=True),
                         [kt.b, wsb.b], [PN.b])
                K.op("act", lambda e: e.copy(yint4[:], P_intra[:]), [P_intra.b], [yint4.b])
                K.op("act", lambda e, PN=PN: e.copy(dsb[:], PN[:, 0:12]), [PN.b], [dsb.b])
                prep(ci + 1)
                K.op("dve", lambda e: e.tensor_tensor(t1[:], P_inter[:].rearrange("p (a b) -> p a b", a=4), bc_last(expb[:], 128), ALU.mult),
                     [P_inter.b, expb.b], [t1.b])
                K.op("dve", lambda e, yb=yb: e.tensor_tensor(yb[:, :, 0:128], t1[:], yint4[:].rearrange("p (a b) -> p a b", a=4), ALU.add),
                     [t1.b, yint4.b], [yb.b])
                K.op("dve", lambda e: e.tensor_tensor(t4[:], dsb[:, 4:8], expb[:], ALU.mult), [dsb.b, expb.b], [t4.b])
                K.op("dve", lambda e, yb=yb: e.tensor_tensor(yb[:, :, 128], t4[:], dsb[:, 0:4], ALU.add), [t4.b, dsb.b], [yb.b])
                K.op("dve", lambda e: e.tensor_tensor(Cst[:, :, 0:128], Cst[:, :, 0:128], bc_last(ebl[:], 128), ALU.mult), [Cst.b, ebl.b], [Cst.b])
                K.op("dve", lambda e: e.tensor_tensor(Cst[:, :, 0:128], Cst[:, :, 0:128], P_upd[:].rearrange("p (a b) -> p a b", a=4), ALU.add),
                     [Cst.b, P_upd.b], [Cst.b])
                K.op("dve", lambda e: e.tensor_tensor(Cst[:, :, 128], Cst[:, :, 128], ebl[:], ALU.mult), [Cst.b, ebl.b], [Cst.b])
                K.op("dve", lambda e: e.tensor_tensor(Cst[:, :, 128], Cst[:, :, 128], dsb[:, 8:12], ALU.add), [Cst.b, dsb.b], [Cst.b])
                K.op("act", lambda e: e.copy(Cb[:], Cst[:]), [Cst.b], [Cb.b])
                if late_fn is not None:
                    late_fn()
                    late_fn = None
                if c < 2 and not need_ctx:
                    continue
                yn = yns[ci % 2]
                YFv = YF[p0:p0 + 128, 0:512].rearrange("p (h q) -> p h q", h=4)
                K.op("act", lambda e, yb=yb: e.activation(out=den[:], in_=yb[:, :, 128], func=AF.Abs), [yb.b], [den.b])
                K.op("dve", lambda e: e.tensor_scalar_max(den[:], den[:], 1.0), [den.b], [den.b])
                K.op("dve", lambda e: e.reciprocal(den[:], den[:]), [den.b], [den.b])
                K.op("dve", lambda e, yb=yb, yn=yn: e.tensor_tensor(yn[:], yb[:, :, 0:128], bc_last(den[:], 128), ALU.mult), [yb.b, den.b], [yn.b])
                if d == 0:
                    K.dma("sp", YFv, yn[:], yn.b, reads=[yn.b], writes=[g.dbuf("YFB", c)])
                    continue
                yfb, o2 = yf[ci % 2], ob[ci % 2]
                K.op("pool", lambda e, yfb=yfb, yn=yn: e.tensor_tensor(yn[:], yn[:], yfb[:, :, 0:128], ALU.add), [yn.b, yfb.b], [yn.b])
                K.op("pool", lambda e, yn=yn: e.tensor_tensor(tq[:], yn[:], yn[:], ALU.mult), [yn.b], [tq.b])

                def _late(yn=yn, o2=o2, so=so, c=c):
                    K.op("dve", lambda e: e.tensor_reduce(ss[:], tq[:], AX.X, ALU.add), [tq.b], [ss.b])
                    rstd_cols(g, ss[:], ss[:], 1.0 / 128, [ss.b], ss.b)
                    K.op("dve", lambda e: e.tensor_tensor(yn[:], yn[:], bc_last(ss[:], 128), ALU.mult), [yn.b, ss.b], [yn.b])
                    K.op("pool", lambda e: e.tensor_tensor(yn[:].rearrange("p h q -> p (h q)"), yn[:].rearrange("p h q -> p (h q)"),
                                                           ptm[:, 16:528], ALU.mult), [yn.b, ptm.b], [yn.b])
                    K.op("pool", lambda e: e.tensor_tensor(o2[:], yn[:].rearrange("p h q -> p (h q)"), oA[:, c, :], ALU.mult),
                         [yn.b, oA.b], [o2.b])
                    cm_store(g, S["Y"], 512, 512, c, o2)
                late_fn = _late
        if late_fn is not None:
            late_fn()
        K.barrier()


def phase_lru(g, li, need_ctx):
    nc, K, I, S = g.nc, g.K, g.I, g.S
    GC = 1.5957691216057308
    with ExitStack() as es:
        def sb(name, shape, dt):
            return Tile(es.enter_context(nc.sbuf_tensor(uniq(name), list(shape), dt)), name)
        pc = sb("c_pc", [128, 4, 12], F32)
        Wb = sb("c_W", [128, 16, 128], BF16)
        c8 = sb("c_c8", [128, 4, 2], F32)
        K.dma("sp", pc[:], I["pC_col"][li], pc.b, writes=[pc.b])
        K.dma("pool", Wb[:], I["pC_w"][li].rearrange("w p n -> p w n"), Wb.b, writes=[Wb.b])
        K.op("act", lambda e: e.activation(out=c8[:], in_=pc[:, :, 10:12], func=AF.Exp, scale=-1.0), [pc.b], [c8.b])
        K.op("act", lambda e: e.activation(out=c8[:], in_=c8[:], func=AF.Ln, bias=g.ones_f[:, 0:1]), [c8.b, g.ones_f.b], [c8.b])
        K.op("dve", lambda e: e.tensor_scalar_mul(c8[:], c8[:], -8.0), [c8.b], [c8.b])
        fm = sb("c_fm", [128, 8, T], BF16)
        for ct in range(8):
            K.dma("sp", fm[:, ct, :], S["UCf"][ct * 128:(ct + 1) * 128, :], fm.b, writes=[fm.b])
        xf = sb("c_xf", [128, T], F32)
        xfb = sb("c_xfb", [128, T], BF16)
        A = sb("c_A", [128, T], F32)
        Bx = sb("c_Bx", [128, T], F32)
        H = [sb("c_H%d" % i, [128, T], F32) for i in range(2)]
        blocks = [(t0, min(512, T - t0)) for t0 in range(0, T, 512)]
        dg = build_diag(g, sb, pc, 4, "c_dg")
        for ct in range(4):
            conv_pe(g, fm[:, 4 + ct, :], fm.b, dg, ct, pc, ct, AF.Identity, lambda t0, n: xf[:, t0:t0 + n], xf.b)
            K.op("dve", lambda e: e.tensor_copy(xfb[:], xf[:]), [xf.b], [xfb.b])
            for d in range(2):
                for (which, dst, bcol) in ((0, A, 6 + d), (1, Bx, 8 + d)):
                    wi = (d * 2 + which) * 4 + ct
                    for (t0, n) in blocks:
                        ps = next_ps(g)
                        K.op("pe", lambda e, ps=ps, wi=wi, t0=t0, n=n: e.matmul(ps[:, :n], Wb[:, wi, :], xfb[:, t0:t0 + n],
                                                                                start=True, stop=True), [Wb.b, xfb.b], [ps.b])
                        K.op("act", lambda e, ps=ps, dst=dst, t0=t0, n=n, bcol=bcol, ct=ct: e.activation(
                            out=dst[:, t0:t0 + n], in_=ps[:, :n], func=AF.Sigmoid, bias=pc[:, ct, bcol:bcol + 1]),
                            [ps.b, pc.b], [dst.b])
                Hd = H[d]
                K.op("act", lambda e, ct=ct, d=d: e.activation(out=A[:], in_=A[:], func=AF.Exp, scale=c8[:, ct, d:d + 1]),
                     [A.b, c8.b], [A.b])
                K.op("dve", lambda e, Hd=Hd: e.tensor_tensor(Hd[:], A[:], A[:], ALU.mult), [A.b], [Hd.b])
                K.op("dve", lambda e, Hd=Hd: e.tensor_scalar(Hd[:], Hd[:], -1.0, 1.0, ALU.mult, ALU.add), [Hd.b], [Hd.b])
                K.op("act", lambda e, Hd=Hd: e.activation(out=Hd[:], in_=Hd[:], func=AF.Sqrt), [Hd.b], [Hd.b])
                K.op("dve", lambda e, Hd=Hd: e.tensor_tensor(Bx[:], Bx[:], Hd[:], ALU.mult), [Bx.b, Hd.b], [Bx.b])
                K.op("dve", lambda e: e.tensor_tensor(Bx[:], Bx[:], xf[:], ALU.mult), [Bx.b, xf.b], [Bx.b])
                if d == 0:
                    K.op("dve", lambda e, Hd=Hd: e.tensor_tensor_scan(Hd[:], A[:], Bx[:], 0.0, ALU.mult, ALU.add),
                         [A.b, Bx.b], [Hd.b])
                else:
                    K.op("dve", lambda e, Hd=Hd: e.tensor_tensor_scan(Hd[:, 0:CTX][:, ::-1], A[:, 0:CTX][:, ::-1],
                                                                     Bx[:, 0:CTX][:, ::-1], 0.0, ALU.mult, ALU.add),
                         [A.b, Bx.b], [Hd.b])
                    K.op("dve", lambda e, Hd=Hd: e.tensor_tensor_scan(Hd[:, CTX:T][:, ::-1], A[:, CTX:T][:, ::-1],
                                                                     Bx[:, CTX:T][:, ::-1], Hd[:, 0:1], ALU.mult, ALU.add),
                         [A.b, Bx.b, Hd.b], [Hd.b])
            gt = fm[:, ct, :]
            K.op("dve", lambda e: e.tensor_tensor(H[0][:], H[0][:], H[1][:], ALU.add), [H[0].b, H[1].b], [H[0].b])
            K.op("dve", lambda e, gt=gt: e.tensor_tensor(A[:], gt, gt, ALU.mult), [fm.b], [A.b])
            K.op("dve", lambda e: e.tensor_scalar(A[:], A[:], 0.044715, 1.0, ALU.mult, ALU.add), [A.b], [A.b])
            K.op("dve", lambda e, gt=gt: e.tensor_tensor(A[:], A[:], gt, ALU.mult), [A.b, fm.b], [A.b])
            K.op("act", lambda e: e.activation(out=A[:], in_=A[:], func=AF.Sigmoid, scale=GC), [A.b], [A.b])
            K.op("dve", lambda e, gt=gt: e.tensor_tensor(H[0][:], H[0][:], gt, ALU.mult), [H[0].b, fm.b], [H[0].b])
            K.op("dve", lambda e, ct=ct: e.tensor_tensor(fm[:, 4 + ct, :], H[0][:], A[:], ALU.mult), [H[0].b, A.b], [fm.b])
        st = [sb("c_st%d" % i, [128, 512], BF16) for i in range(2)]
        for c in range(NCH):
            if c < 2 and not need_ctx:
                continue
            pb = next_psb(g)
            for j in range(4):
                K.op("pe", lambda e, j=j, pb=pb, c=c: e.transpose(pb[:, j * 128:(j + 1) * 128], fm[:, 4 + j, c * 128:(c + 1) * 128],
                                                                  g.ident_b[:]), [fm.b, g.ident_b.b], [pb.b])
            s_ = st[c % 2]
            K.op("act", lambda e, pb=pb, s_=s_: e.copy(s_[:], pb[:, :512]), [pb.b], [s_.b])
            K.dma("sp", S["Y"][c * 128:(c + 1) * 128, 1024:1536], s_[:], s_.b, reads=[s_.b], writes=[g.dbuf("Y", ("C", c))])
        K.barrier()


def phase_gla(g, li, need_ctx):
    nc, K, I, S = g.nc, g.K, g.I, g.S
    P_o, P_upd = g.psum[0], g.psum[1]
    with ExitStack() as es:
        def sb(name, shape, dt):
            return Tile(es.enter_context(nc.sbuf_tensor(uniq(name), list(shape), dt)), name)
        ptm = sb("d_ptm", [128, 1024], F32)
        wg2 = sb("d_wg2", [32, 512], F32)
        lnq = sb("d_lnq", [128, 1], F32)
        K.dma("sp", ptm[:], I["pD_tm"][li], ptm.b, writes=[ptm.b])
        K.dma("sp", wg2[:], I["pD_wg2"][li], wg2.b, writes=[wg2.b])
        K.op("dve", lambda e: e.memset(lnq[:], math.log(0.125)), [], [lnq.b])
        g1 = sb("d_g1", [128, NCH, 32], F32)
        K.dma("sp", g1[:], S["GD"].rearrange("(c p) j -> p c j", p=128), g1.b, writes=[g1.b])
        rA = sb("d_rA", [128, NCH, 512], BF16)
        K.dma("sp", rA[:], S["UD"][:, 1024:1536].rearrange("(c p) n -> p c n", p=128), rA.b, writes=[rA.b])
        for (c0_, c1_) in ((0, 17), (17, NCH)):
            K.op("act", lambda e, c0_=c0_, c1_=c1_: e.activation(out=rA[:, c0_:c1_, :], in_=rA[:, c0_:c1_, :], func=AF.Silu), [rA.b], [rA.b])
        Sst = sb("d_S", [128, 2, 128], F32)
        Smid = sb("d_Smid", [128, 2, 128], F32)
        Smb = sb("d_Smb", [128, 2, 128], BF16)
        qk = [sb("d_qk%d" % i, [128, 512], BF16) for i in range(2)]
        vt = [sb("d_v%d" % i, [128, 512], BF16) for i in range(2)]
        ektm = sb("d_ektm", [128, 256], F32)
        eqT = sb("d_eqT", [128, 2, 128], F32)
        ekT = sb("d_ekT", [128, 2, 128], F32)
        ebm = sb("d_ebm", [128, 2], F32)
        ebr = sb("d_ebr", [128, 2], F32)
        qtT = sb("d_qtT", [128, 2, 128], BF16)
        ktT = sb("d_ktT", [128, 2, 128], BF16)
        ktm = sb("d_ktm", [128, 256], BF16)
        att = [sb("d_att%d" % i, [128, 128], BF16) for i in range(2)]
        tmp = sb("d_tmp", [128, 128], F32)
        o = [sb("d_o%d" % i, [128, 512], F32) for i in range(2)]
        of = [sb("d_of%d" % i, [128, 512], F32) for i in range(2)]
        rt = [sb("d_r%d" % i, [128, 512], BF16) for i in range(2)]
        srs = [sb("d_sr%d" % i, [128, 512], F32) for i in range(2)]
        t1 = sb("d_t1", [128, 512], F32)
        ss = sb("d_ss", [128, 4], F32)
        ob = [sb("d_ob%d" % i, [128, 512], BF16) for i in range(2)]
        YF = nc.dram_tensor("s_YFD_%d" % li, [T, 512], F32, kind="Internal").ap()
        laA = sb("d_laA", [128, NCH, 512], F32)
        g1Ts = [sb("d_g1T%d" % i, [32, 128], F32) for i in range(2)]
        for c_ in range(NCH):
            gT = g1Ts[c_ % 2]
            pg = next_ps(g, 2)
            K.op("pe", lambda e, pg=pg, c_=c_: e.transpose(pg[:32, :128], g1[:, c_, :], g.ident_f[:]), [g1.b, g.ident_f.b], [pg.b])
            K.op("act", lambda e, pg=pg, gT=gT: e.copy(gT[:], pg[:32, :128]), [pg.b], [gT.b])
            pl = next_ps(g, 2)
            K.op("pe", lambda e, pl=pl, gT=gT: e.matmul(pl[:], gT[:], wg2[:], start=True, stop=True), [gT.b, wg2.b], [pl.b])
            K.op("dve", lambda e, pl=pl, c_=c_: e.tensor_tensor(laA[:, c_, :], pl[:], ptm[:, 0:512], ALU.add), [pl.b, ptm.b], [laA.b])
        for (c0_, c1_) in ((0, 17), (17, NCH)):
            K.op("act", lambda e, c0_=c0_, c1_=c1_: e.activation(out=laA[:, c0_:c1_, :], in_=laA[:, c0_:c1_, :], func=AF.Exp, scale=-1.0),
                 [laA.b], [laA.b])
            K.op("act", lambda e, c0_=c0_, c1_=c1_: e.activation(out=laA[:, c0_:c1_, :], in_=laA[:, c0_:c1_, :], func=AF.Ln,
                                                                 bias=g.ones_f[:, 0:1]), [laA.b, g.ones_f.b], [laA.b])
            K.op("dve", lambda e, c0_=c0_, c1_=c1_: e.tensor_scalar_mul(laA[:, c0_:c1_, :], laA[:, c0_:c1_, :], -1.0 / 16.0), [laA.b], [laA.b])

        iters = [(d_, c_) for d_ in range(2) for c_ in chunk_order(d_)]

        def pre(j):
            if j > len(iters):
                return
            d_, c_ = iters[j - 1]
            q0 = c_ * 128
            K.dma("sp", qk[j % 2][:], S["UD"][q0:q0 + 128, 0:512], qk[j % 2].b, writes=[qk[j % 2].b])
            K.dma("sp", vt[j % 2][:], S["UD"][q0:q0 + 128, 512:1024], vt[j % 2].b, writes=[vt[j % 2].b])
            if d_ == 1 and not (c_ < 2 and not need_ctx):
                K.dma("sp", of[j % 2][:], YF[q0:q0 + 128, :], of[j % 2].b, reads=[g.dbuf("YFD", c_)], writes=[of[j % 2].b])
        ci = 0
        late_fn = None
        for d in range(2):
            tri = g.triU_f if d == 0 else g.triL_f
            mid = g.mid_f[d]
            midcol = g.triU_f[:, 63:64] if d == 0 else g.triL_f[:, 64:65]
            msk = g.mskF_b if d == 0 else g.mskB_b
            last = 127 if d == 0 else 0
            K.op("dve", lambda e: e.memset(Sst[:], 0.0), [], [Sst.b])
            for c in chunk_order(d):
                p0 = c * 128
                ci += 1
                if ci == 1:
                    pre(1)
                pre(ci + 1)
                qkb, vb, ob_ = qk[ci % 2], vt[ci % 2], o[ci % 2]
                sr, rb = srs[ci % 2], rt[ci % 2]
                la = Sub(laA[:, c, d * 256:(d + 1) * 256], laA.b)
                p1 = next_ps(g, 2)
                K.op("pe", lambda e, p1=p1: e.matmul(p1[:, :256], mid[:], la[:], start=True, stop=True), [mid.b, la.b], [p1.b])
                K.op("act", lambda e, p1=p1: e.activation(out=ektm[:], in_=p1[:, :256], func=AF.Exp, scale=-1.0), [p1.b], [ektm.b])
                p2 = next_ps(g, 2)
                for ct in range(2):
                    K.op("pe", lambda e, p2=p2, ct=ct: e.matmul(p2[:, ct * 129:ct * 129 + 128], la[:, ct * 128:(ct + 1) * 128], mid[:],
                                                                start=True, stop=True), [la.b, mid.b], [p2.b])
                    K.op("pe", lambda e, p2=p2, ct=ct: e.matmul(p2[:, ct * 129 + 128:ct * 129 + 129], la[:, ct * 128:(ct + 1) * 128], midcol,
                                                                start=True, stop=True), [la.b, tri.b], [p2.b])
                p2v = p2[:, :258].rearrange("p (a b) -> p a b", a=2)
                K.op("act", lambda e, p2v=p2v: e.activation(out=eqT[:], in_=p2v[:, :, 0:128], func=AF.Exp, bias=lnq[:, 0:1]),
                     [p2.b, lnq.b], [eqT.b])
                K.op("act", lambda e, p2v=p2v: e.activation(out=ekT[:], in_=p2v[:, :, 0:128], func=AF.Exp, scale=-1.0), [p2.b], [ekT.b])
                K.op("act", lambda e, p2v=p2v: e.activation(out=ebm[:], in_=p2v[:, :, 128], func=AF.Exp), [p2.b], [ebm.b])
                K.op("act", lambda e, p2v=p2v: e.activation(out=ebr[:], in_=p2v[:, :, last], func=AF.Exp), [p2.b], [ebr.b])
                pb = next_psb(g)
                for j in range(4):
                    K.op("pe", lambda e, j=j, pb=pb, qkb=qkb: e.transpose(pb[:, j * 128:(j + 1) * 128], qkb[:, j * 128:(j + 1) * 128],
                                                                          g.ident_b[:]), [qkb.b, g.ident_b.b], [pb.b])
                K.op("dve", lambda e, pb=pb: e.tensor_tensor(qtT[:], pb[:, 0:256].rearrange("p (a b) -> p a b", a=2), eqT[:], ALU.mult),
                     [pb.b, eqT.b], [qtT.b])
                K.op("dve", lambda e, pb=pb: e.tensor_tensor(ktT[:], pb[:, 256:512].rearrange("p (a b) -> p a b", a=2), ekT[:], ALU.mult),
                     [pb.b, ekT.b], [ktT.b])
                K.op("dve", lambda e, qkb=qkb: e.tensor_tensor(ktm[:], qkb[:, 256:512], ektm[:], ALU.mult), [qkb.b, ektm.b], [ktm.b])
                for ct in range(2):
                    K.op("dve", lambda e, ct=ct: e.tensor_scalar_mul(Smid[:, ct, :], Sst[:, ct, :], ebm[:, ct:ct + 1]),
                         [Sst.b, ebm.b], [Smid.b])
                K.op("act", lambda e: e.copy(Smb[:], Smid[:]), [Smid.b], [Smb.b])
                for h in range(4):
                    ct, hp = h // 2, (h % 2) * 64
                    at = att[h % 2]
                    pa = next_ps(g, 2)
                    K.op("pe", lambda e, pa=pa, ct=ct, hp=hp: e.matmul(pa[:, :128], ktT[hp:hp + 64, ct, :], qtT[hp:hp + 64, ct, :],
                                                                       start=True, stop=True), [ktT.b, qtT.b], [pa.b])
                    K.op("dve", lambda e, pa=pa, at=at: e.tensor_tensor(at[:], pa[:, :128], msk[:], ALU.mult), [pa.b, msk.b], [at.b])
                    K.op("pe", lambda e, at=at, h=h, vb=vb: e.matmul(P_o[:, h * 128:(h + 1) * 128], at[:], vb[:, h * 128:(h + 1) * 128],
                                                                     start=True, stop=False), [at.b, vb.b], [P_o.b])
                    K.op("pe", lambda e, h=h, ct=ct, hp=hp: e.matmul(P_o[:, h * 128:(h + 1) * 128], qtT[hp:hp + 64, ct, :],
                                                                     Smb[hp:hp + 64, ct, :], start=False, stop=True),
                         [qtT.b, Smb.b], [P_o.b])
                    K.op("pe", lambda e, h=h, ct=ct, hp=hp, vb=vb: e.matmul(P_upd[hp:hp + 64, ct * 128:(ct + 1) * 128],
                                                                            ktm[:, h * 64:(h + 1) * 64], vb[:, h * 128:(h + 1) * 128],
                                                                            start=True, stop=True), [ktm.b, vb.b], [P_upd.b])
                for ct in range(2):
                    K.op("dve", lambda e, ct=ct: e.tensor_tensor(tmp[:], Smid[:, ct, :], P_upd[:, ct * 128:(ct + 1) * 128], ALU.add),
                         [Smid.b, P_upd.b], [tmp.b])
                    K.op("dve", lambda e, ct=ct: e.tensor_scalar_mul(Sst[:, ct, :], tmp[:], ebr[:, ct:ct + 1]), [tmp.b, ebr.b], [Sst.b])
                if late_fn is not None:
                    late_fn()
                    late_fn = None
                if c < 2 and not need_ctx:
                    continue
                if d == 0:
                    K.op("act", lambda e, ob_=ob_: e.copy(ob_[:], P_o[:]), [P_o.b], [ob_.b])
                    K.dma("sp", YF[p0:p0 + 128, :], ob_[:], ob_.b, reads=[ob_.b], writes=[g.dbuf("YFD", c)])
                    continue
                ofb, o2 = of[ci % 2], ob[ci % 2]
                K.op("dve", lambda e, ob_=ob_, ofb=ofb: e.tensor_tensor(ob_[:], P_o[:], ofb[:], ALU.add), [P_o.b, ofb.b], [ob_.b])
                K.op("pool", lambda e, ob_=ob_: e.tensor_tensor(t1[:], ob_[:], ob_[:], ALU.mult), [ob_.b], [t1.b])

                def _late(ob_=ob_, o2=o2, sr=sr, c=c):
                    K.op("dve", lambda e: e.tensor_reduce(ss[:], t1[:].rearrange("p (h q) -> p h q", h=4), AX.X, ALU.add), [t1.b], [ss.b])
                    rstd_cols(g, ss[:], ss[:], 1.0 / 128, [ss.b], ss.b)
                    K.op("dve", lambda e: e.tensor_tensor(ob_[:].rearrange("p (h q) -> p h q", h=4),
                                                          ob_[:].rearrange("p (h q) -> p h q", h=4), bc_last(ss[:], 128), ALU.mult),
                         [ob_.b, ss.b], [ob_.b])
                    K.op("pool", lambda e: e.tensor_tensor(ob_[:], ob_[:], ptm[:, 512:1024], ALU.mult), [ob_.b, ptm.b], [ob_.b])
                    K.op("pool", lambda e: e.tensor_tensor(o2[:], ob_[:], rA[:, c, :], ALU.mult), [ob_.b, rA.b], [o2.b])
                    cm_store(g, S["Y"], 1536, 512, c, o2)
                late_fn = _late
        if late_fn is not None:
            late_fn()
        K.barrier()
```

```python
import numpy as np
import math
from contextlib import ExitStack
import concourse.bass as bass
import concourse.mybir as mybir
from concourse.bass_utils import run_bass_kernel_spmd

F32, BF16 = mybir.dt.float32, mybir.dt.bfloat16
AF = mybir.ActivationFunctionType
ALU = mybir.AluOpType
AX = mybir.AxisListType

D = 2048
L = 4096
CTX = 256
T = L + CTX
NCH = T // 128
DFF = 5632
KC = D // 128
KF = DFF // 128
PIN = 6208
EPS = 1e-6
DEPTH = 2
CA, CB, CC, CD = 0, 1552, 3616, 4640
NEG = -30000.0
N_ACTIVE = 4

DEBUG = {}


class Buf:
    __slots__ = ("name", "w", "r", "dsem", "dcnt", "dlast", "ep")

    def __init__(self, name):
        self.name = name
        self.w = []
        self.r = {}
        self.dsem = None
        self.dcnt = 0
        self.dlast = None
        self.ep = 0


class Trk:
    SEM_CAP = 30000

    def __init__(self, nc):
        self.nc = nc
        self.eng = {"pe": nc.tensor, "act": nc.scalar, "dve": nc.vector, "pool": nc.gpsimd, "sp": nc.sync}
        self.esem = {}
        self.ecnt = {}
        self.seen = {e: {} for e in self.eng}
        self.nsem = 0
        self.ep = 0
        self.dma_slots = {}
        self.free_dsems = []
        self.n_ins = 0
        for e in self.eng:
            self.esem[e] = self._newsem("e_" + e)
            self.ecnt[e] = 0

    def _newsem(self, name):
        self.nsem += 1
        return self.nc.alloc_semaphore("%s_%d" % (name, self.nsem))

    def _fresh(self, b):
        if b.ep != self.ep:
            b.w = []
            b.r = {}
            b.dlast = None
            b.ep = self.ep

    def _wait(self, e, deps):
        seen = self.seen[e]
        own = self.esem[e]
        for (sem, val) in deps:
            k = id(sem)
            if seen.get(k, 0) >= val:
                continue
            if e == "pe" and sem is own:
                continue
            self.eng[e].wait_ge(sem, val)
            seen[k] = val

    def _deps(self, reads, writes):
        deps = []
        for b in reads:
            self._fresh(b)
            deps += b.w
        for b in writes:
            self._fresh(b)
            deps += b.w
            deps += list(b.r.values())
        return deps

    def _record(self, ev, reads, writes):
        k = id(ev[0])
        for b in reads:
            b.r[k] = ev
        for b in writes:
            b.w = [ev]
            b.r = {}

    def op(self, e, fn, reads=(), writes=()):
        self._wait(e, self._deps(reads, writes))
        ins = fn(self.eng[e])
        if self.ecnt[e] >= self.SEM_CAP:
            self.esem[e] = self._newsem("e_" + e)
            self.ecnt[e] = 0
        self.ecnt[e] += 1
        sem = self.esem[e]
        ins.then_inc(sem, 1)
        self._record((sem, self.ecnt[e]), reads, writes)
        self.n_ins += 1

    def dma(self, q, out, in_, slot, reads=(), writes=()):
        deps = self._deps(reads, writes)
        self._fresh(slot)
        if slot.dlast is not None:
            deps.append(slot.dlast)
        self._wait(q, deps)
        if slot.dsem is not None and slot.dcnt + 16 > self.SEM_CAP:
            slot.dsem = None
        if slot.dsem is None:
            while self.free_dsems:
                sem, cnt = self.free_dsems.pop()
                if cnt + 16 <= self.SEM_CAP:
                    slot.dsem, slot.dcnt = sem, cnt
                    break
            else:
                slot.dsem = self._newsem("d")
                slot.dcnt = 0
        slot.dcnt += 16
        self.eng[q].dma_start(out=out, in_=in_).then_inc(slot.dsem, 16)
        ev = (slot.dsem, slot.dcnt)
        slot.dlast = ev
        self.dma_slots[id(slot)] = slot
        self._record(ev, reads, writes)
        self.n_ins += 1

    def barrier(self):
        evs = [(self.esem[e], self.ecnt[e]) for e in self.eng if self.ecnt[e] > 0]
        for s in self.dma_slots.values():
            if s.ep == self.ep and s.dlast is not None:
                evs.append(s.dlast)
        for e in self.eng:
            seen = self.seen[e]
            for (sem, val) in evs:
                if sem is self.esem[e]:
                    continue
                if seen.get(id(sem), 0) >= val:
                    continue
                self.eng[e].wait_ge(sem, val)
                seen[id(sem)] = val
        for s in self.dma_slots.values():
            if s.dsem is not None:
                self.free_dsems.append((s.dsem, s.dcnt))
                s.dsem = None
        self.ep += 1
        self.dma_slots = {}


class Tile:
    def __init__(self, t, name):
        self.t = t
        self.b = Buf(name)

    def __getitem__(self, k):
        return self.t[k]


class Sub:
    def __init__(self, ap, b):
        self.t = ap
        self.b = b

    def __getitem__(self, k):
        return self.t[k]


class Ctx:
    pass


_UNIQ = [0]


def uniq(name):
    _UNIQ[0] += 1
    return "%s_u%d" % (name, _UNIQ[0])


def dram_ap(t, offset, pattern):
    return bass.AP(t.tensor, offset, pattern)


def build_program(debug_outs=()):
    nc = bass.Bass("TRN2", target_bir_lowering=False)
    K = Trk(nc)
    g = Ctx()
    g.nc, g.K = nc, K

    def din(name, shape, dt=F32):
        return nc.dram_tensor(name, list(shape), dt, kind="ExternalInput").ap()

    def dscr(name, shape, dt):
        kind = "ExternalOutput" if name in debug_outs else "Internal"
        return nc.dram_tensor(name, list(shape), dt, kind=kind).ap()

    I = {}
    I["x"] = din("x", [L, D])
    I["ctx"] = din("ctx", [CTX, D])
    I["ccol"] = din("ccol", [128, 2, KC])
    I["gcol"] = din("gcol", [128, 5, KC])
    I["bmodcol"] = din("bmodcol", [128, DEPTH, 96])
    I["w_mod"] = din("w_mod", [DEPTH, D, 6 * D])
    I["w_in"] = din("w_in", [DEPTH, D, PIN])
    I["w_out"] = din("w_out", [DEPTH, D, D])
    I["w_gate"] = din("w_gate", [DEPTH, D, DFF])
    I["w_up"] = din("w_up", [DEPTH, D, DFF])
    I["w_down"] = din("w_down", [DEPTH, DFF, D])
    for nm, shp in MIX_PARAM_SHAPES.items():
        I[nm] = din(nm, shp)
    out = nc.dram_tensor("out", [L, D], F32, kind="ExternalOutput").ap()

    S = {}
    S["xT"] = dscr("s_xT", [D, T], F32)
    S["UA"] = dscr("s_UA", [T, 1536], BF16)
    S["GA"] = dscr("s_GA", [T, 16], F32)
    S["UB"] = dscr("s_UB", [T, 2048], BF16)
    S["GB"] = dscr("s_GB", [T, 16], F32)
    S["UC"] = dscr("s_UC", [T, 1024], BF16)
    S["UD"] = dscr("s_UD", [T, 1536], BF16)
    S["GD"] = dscr("s_GD", [T, 32], F32)
    S["UAf"] = dscr("s_UAf", [1024, T], BF16)
    S["UCf"] = dscr("s_UCf", [1024, T], BF16)
    S["Y"] = dscr("s_Y", [T, D], BF16)
    S["HID"] = dscr("s_HID", [DFF, T], BF16)
    S["MOD"] = dscr("s_MOD", [DEPTH * 2, 128, 96], F32)
    g.I, g.S, g.out = I, S, out
    g.dbufs = {}

    def dbuf(name, blk=0):
        k = (name, blk)
        if k not in g.dbufs:
            g.dbufs[k] = Buf("%s_%s" % (name, blk))
        return g.dbufs[k]
    g.dbuf = dbuf

    with ExitStack() as top:
        def sb(name, shape, dt):
            return Tile(top.enter_context(nc.sbuf_tensor(name, list(shape), dt)), name)

        g.ident_f = sb("ident_f", [128, 128], F32)
        g.ident_b = sb("ident_b", [128, 128], BF16)
        g.ones_b = sb("ones_b", [128, 128], BF16)
        g.ones_f = sb("ones_f", [128, 128], F32)
        g.triU_f = sb("triU_f", [128, 128], F32)
        g.triL_f = sb("triL_f", [128, 128], F32)
        g.negF_f = sb("negF_f", [128, 128], F32)
        g.negB_f = sb("negB_f", [128, 128], F32)
        g.mskF_b = sb("mskF_b", [128, 128], BF16)
        g.mskB_b = sb("mskB_b", [128, 128], BF16)
        g.mid_f = [sb("midF_f", [128, 128], F32), sb("midB_f", [128, 128], F32)]
        g.mod = sb("modcols", [128, DEPTH * 2, 96], F32)
        g.gcol = sb("gcols", [128, 5, KC], F32)
        g.epsc = sb("epsc", [128, 1], F32)
        g.psum = []
        for i in range(6):
            g.psum.append(Tile(top.enter_context(nc.psum_tensor("ps%d" % i, [128, 512], F32)), "ps%d" % i))
        g.psb = []
        for i in range(2):
            g.psb.append(Tile(top.enter_context(nc.psum_tensor("psb%d" % i, [128, 1024], BF16)), "psb%d" % i))
        g.ps_i = 0
        g.psb_i = 0

        build_consts(g)
        phase_mod(g)
        phase_x0(g)
        for li in range(DEPTH):
            last = li == DEPTH - 1
            phase_norm_proj(g, li)
            if "stop_after_proj" in DEBUG and DEBUG["stop_after_proj"] == li:
                break
            phase_ssd(g, li, not last)
            phase_mlstm(g, li, not last)
            phase_lru(g, li, not last)
            phase_gla(g, li, not last)
            if "stop_after_mix" in DEBUG and DEBUG["stop_after_mix"] == li:
                break
            phase_wout(g, li, not last)
            phase_ffn_up(g, li, not last)
            phase_ffn_down(g, li, not last)
        phase_final(g)
        K.barrier()
    return nc


MIX_PARAM_SHAPES = {
    "pA_tm": [DEPTH, 128, 1056], "pA_conv": [DEPTH, 128, 8, 6],
    "pB_tm": [DEPTH, 128, 528], "pB_conv": [DEPTH, 128, 8, 6],
    "pC_col": [DEPTH, 128, 4, 12], "pC_w": [DEPTH, 16, 128, 128],
    "pD_tm": [DEPTH, 128, 1024], "pD_wg2": [DEPTH, 32, 512],
}


def next_ps(g, lo=0):
    n = len(g.psum) - lo
    t = g.psum[lo + g.ps_i % n]
    g.ps_i += 1
    return t


def next_psb(g):
    t = g.psb[g.psb_i % len(g.psb)]
    g.psb_i += 1
    return t


def build_consts(g):
    nc, K = g.nc, g.K
    cst = g.I_const = nc.dram_tensor("consts", [8, 128, 128], F32, kind="ExternalInput").ap()
    with ExitStack() as es:
        tmp = Tile(es.enter_context(nc.sbuf_tensor("cst_tmp", [128, 8, 128], F32)), "cst_tmp")
        K.dma("sp", tmp[:], cst.rearrange("c p n -> p c n"), tmp.b, writes=[tmp.b])
        K.op("dve", lambda e: e.tensor_copy(g.ident_f[:], tmp[:, 0, :]), [tmp.b], [g.ident_f.b])
        K.op("dve", lambda e: e.tensor_copy(g.ident_b[:], tmp[:, 0, :]), [tmp.b], [g.ident_b.b])
        K.op("dve", lambda e: e.tensor_copy(g.ones_b[:], tmp[:, 1, :]), [tmp.b], [g.ones_b.b])
        K.op("dve", lambda e: e.tensor_copy(g.ones_f[:], tmp[:, 1, :]), [tmp.b], [g.ones_f.b])
        K.op("dve", lambda e: e.tensor_copy(g.triU_f[:], tmp[:, 2, :]), [tmp.b], [g.triU_f.b])
        K.op("dve", lambda e: e.tensor_copy(g.triL_f[:], tmp[:, 3, :]), [tmp.b], [g.triL_f.b])
        K.op("dve", lambda e: e.tensor_copy(g.negF_f[:], tmp[:, 4, :]), [tmp.b], [g.negF_f.b])
        K.op("dve", lambda e: e.tensor_copy(g.negB_f[:], tmp[:, 5, :]), [tmp.b], [g.negB_f.b])
        K.op("dve", lambda e: e.tensor_copy(g.mskF_b[:], tmp[:, 2, :]), [tmp.b], [g.mskF_b.b])
        K.op("dve", lambda e: e.tensor_copy(g.mskB_b[:], tmp[:, 3, :]), [tmp.b], [g.mskB_b.b])
        K.op("dve", lambda e: e.tensor_copy(g.mid_f[0][:], tmp[:, 6, :]), [tmp.b], [g.mid_f[0].b])
        K.op("dve", lambda e: e.tensor_copy(g.mid_f[1][:], tmp[:, 7, :]), [tmp.b], [g.mid_f[1].b])
        K.op("dve", lambda e: e.memset(g.epsc[:], EPS), [], [g.epsc.b])
        K.dma("sp", g.gcol[:], g.I["gcol"], g.gcol.b, writes=[g.gcol.b])
        K.barrier()


def host_consts():
    c = np.zeros((8, 128, 128), np.float32)
    k = np.arange(128)[:, None]
    t = np.arange(128)[None, :]
    c[0] = np.eye(128)
    c[1] = 1.0
    c[2] = (k <= t)
    c[3] = (k >= t)
    c[4] = np.where(k > t, NEG, 0.0)
    c[5] = np.where(k < t, NEG, 0.0)
    c[6] = (k <= t).astype(np.float32) - (k <= 63)
    c[7] = (k >= t).astype(np.float32) - (k >= 64)
    return c


def phase_mod(g):
    nc, K, I = g.nc, g.K, g.I
    NB = 256
    with ExitStack() as es:
        def sb(name, shape, dt):
            return Tile(es.enter_context(nc.sbuf_tensor(uniq(name), list(shape), dt)), name)
        sT = sb("m_sT", [128, KC, 2], F32)
        craw = sb("m_craw", [128, 2, KC], F32)
        bcol = sb("m_bcol", [128, DEPTH, 96], F32)
        wt = [sb("m_w%d" % i, [128, KC, NB], F32) for i in range(2)]
        K.dma("sp", craw[:], I["ccol"], craw.b, writes=[craw.b])
        K.dma("sp", bcol[:], I["bmodcol"], bcol.b, writes=[bcol.b])
        for s in range(2):
            K.op("act", lambda e, s=s: e.activation(out=sT[:, :, s], in_=craw[:, s, :], func=AF.Silu),
                 [craw.b], [sT.b])
        nblk = 6 * D // NB
        for li in range(DEPTH):
            ps = next_ps(g)
            for nb in range(nblk):
                w = wt[nb % 2]
                K.dma("sp" if nb % 2 == 0 else "act", w[:],
                      I["w_mod"][li][:, nb * NB:(nb + 1) * NB].rearrange("(k p) n -> p k n", p=128),
                      w.b, writes=[w.b])
                for jt in range(NB // 128):
                    j = nb * (NB // 128) + jt
                    for kc in range(KC):
                        K.op("pe", lambda e, kc=kc, jt=jt, j=j, w=w: e.matmul(
                            ps[:, 2 * j:2 * j + 2], w[:, kc, jt * 128:(jt + 1) * 128], sT[:, kc, :],
                            start=(kc == 0), stop=(kc == KC - 1)), [w.b, sT.b], [ps.b])
            for s in range(2):
                K.op("dve", lambda e, s=s, li=li: e.tensor_tensor(
                    g.mod[:, li * 2 + s, :], ps[:, s:192:2], bcol[:, li, :], ALU.add), [ps.b, bcol.b], [g.mod.b])
        for li in range(DEPTH):
            for s in range(2):
                for (mi, gj) in ((1, li), (4, 2 + li)):
                    K.op("dve", lambda e, li=li, s=s, mi=mi, gj=gj: e.scalar_tensor_tensor(
                        out=g.mod[:, li * 2 + s, mi * 16:(mi + 1) * 16], in0=g.mod[:, li * 2 + s, mi * 16:(mi + 1) * 16],
                        scalar=1.0, in1=g.gcol[:, gj, :], op0=ALU.add, op1=ALU.mult), [g.mod.b, g.gcol.b], [g.mod.b])
        if "s_MOD" in DEBUG.get("outs", ()):
            K.dma("sp", g.S["MOD"].rearrange("a p j -> p a j"), g.mod[:], g.mod.b, reads=[g.mod.b])
        K.barrier()


def phase_x0(g):
    nc, K, I, S = g.nc, g.K, g.I, g.S
    with ExitStack() as es:
        def sb(name, shape, dt):
            return Tile(es.enter_context(nc.sbuf_tensor(uniq(name), list(shape), dt)), name)
        xin = [sb("x0_in%d" % i, [128, D], F32) for i in range(2)]
        xo = [sb("x0_o%d" % i, [128, KC, 128], F32) for i in range(2)]
        for ti in range(NCH):
            src = I["ctx"][ti * 128:(ti + 1) * 128, :] if ti < 2 else I["x"][(ti - 2) * 128:(ti - 1) * 128, :]
            xi = xin[ti % 2]
            o = xo[ti % 2]
            K.dma("sp", xi[:], src, xi.b, writes=[xi.b])
            for q in range(4):
                ps = next_ps(g)
                for j in range(4):
                    kc = q * 4 + j
                    K.op("pe", lambda e, kc=kc, j=j, ps=ps, xi=xi: e.transpose(
                        ps[:, j * 128:(j + 1) * 128], xi[:, kc * 128:(kc + 1) * 128], g.ident_f[:]),
                        [xi.b, g.ident_f.b], [ps.b])
                eng = "act" if q % 2 == 0 else "dve"
                if eng == "act":
                    K.op("act", lambda e, q=q, ps=ps, o=o: e.copy(
                        o[:, q * 4:(q + 1) * 4, :], ps[:].rearrange("p (a b) -> p a b", a=4)), [ps.b], [o.b])
                else:
                    K.op("dve", lambda e, q=q, ps=ps, o=o: e.tensor_copy(
                        o[:, q * 4:(q + 1) * 4, :], ps[:].rearrange("p (a b) -> p a b", a=4)), [ps.b], [o.b])
            K.dma("sp", S["xT"].rearrange("(k p) t -> p k t", p=128)[:, :, ti * 128:(ti + 1) * 128], o[:], o.b,
                  reads=[o.b], writes=[g.dbuf("xT", ti)])
        K.barrier()


def lat_slices(lo, hi):
    return [(CTX + 512 * i, 512, 0) for i in range(lo, hi)]


def superblocks(with_ctx, per):
    sbs = []
    for i in range(0, 8, per):
        sl = lat_slices(i, min(8, i + per))
        sbs.append(sl)
    if with_ctx:
        sbs[0] = [(0, CTX, 1)] + sbs[0]
    return sbs


def bc_mid(ap2d, n):
    return ap2d.unsqueeze(1).to_broadcast([ap2d.shape[0], n, ap2d.shape[1]])


def bc_last(ap2d, n):
    return ap2d.unsqueeze(2).to_broadcast([ap2d.shape[0], ap2d.shape[1], n])


def norm_into(g, R, li, which, pos, n, stream, dst, dst_off):
    nc, K, S = g.nc, g.K, g.S
    mi_scale, mi_shift = (1, 0) if which == 1 else (4, 3)
    ms = li * 2 + stream
    xTv = S["xT"].rearrange("(k p) t -> p k t", p=128)
    for o in range(0, n, 256):
        xin = R["xin"][R["i"] % 2]
        R["i"] += 1
        sq, rstd = R["sq"], R["rstd"]
        K.dma("sp", xin[:], xTv[:, :, pos + o:pos + o + 256], xin.b,
              reads=[g.dbuf("xT", (pos + o) // 128), g.dbuf("xT", (pos + o) // 128 + 1)], writes=[xin.b])
        K.op("act", lambda e: e.activation(out=sq[:], in_=xin[:], func=AF.Square), [xin.b], [sq.b])
        ps = next_ps(g)
        for kc in range(KC):
            K.op("pe", lambda e, kc=kc: e.matmul(ps[:, :256], g.ones_b[:], sq[:, kc, :], start=(kc == 0),
                                                 stop=(kc == KC - 1)), [sq.b, g.ones_b.b], [ps.b])
        K.op("act", lambda e: e.activation(out=rstd[:], in_=ps[:, :256], func=AF.Sqrt, scale=1.0 / D,
                                           bias=g.epsc[:, 0:1]), [ps.b, g.epsc.b], [rstd.b])
        K.op("dve", lambda e: e.reciprocal(rstd[:], rstd[:]), [rstd.b], [rstd.b])
        K.op("dve", lambda e: e.tensor_tensor(xin[:], xin[:], bc_mid(rstd[:], KC), ALU.mult), [xin.b, rstd.b], [xin.b])
        for kc in range(KC):
            K.op("act", lambda e, kc=kc: e.activation(
                out=dst[:, kc, dst_off + o:dst_off + o + 256], in_=xin[:, kc, :], func=AF.Identity,
                scale=g.mod[:, ms, mi_scale * 16 + kc:mi_scale * 16 + kc + 1],
                bias=g.mod[:, ms, mi_shift * 16 + kc:mi_shift * 16 + kc + 1]), [xin.b, g.mod.b], [dst.b])


def colmajor_row_dmas(pos):
    t0 = pos - CTX
    r0 = t0 // 64
    return [(0, CTX + r0, 64), (64, CTX + r0 + 1, 64)]


def phase_norm_proj(g, li):
    nc, K, I, S = g.nc, g.K, g.I, g.S
    jobs = []
    jobs.append((CA, 512, "UA", 0, False, False, False))
    for c0 in range(512, 1536, 512):
        jobs.append((CA + c0, 512, "UAf", c0 - 512, False, False, True))
    jobs.append((CA + 1536, 16, "GA", 0, True, False, False))
    for c0 in range(0, 2048, 512):
        jobs.append((CB + c0, 512, "UB", c0, False, True, False))
    jobs.append((CB + 2048, 16, "GB", 0, True, True, False))
    for c0 in range(0, 1024, 512):
        jobs.append((CC + c0, 512, "UCf", c0, False, False, True))
    for c0 in range(0, 1536, 512):
        jobs.append((CD + c0, 512, "UD", c0, False, True, False))
    jobs.append((CD + 1536, 32, "GD", 0, True, True, False))
    with ExitStack() as es:
        def sb(name, shape, dt):
            return Tile(es.enter_context(nc.sbuf_tensor(uniq(name), list(shape), dt)), name)
        hT = sb("np_hT", [128, KC, 2304], BF16)
        R = {"xin": [sb("np_xin%d" % i, [128, KC, 256], F32) for i in range(2)],
             "sq": sb("np_sq", [128, KC, 256], BF16), "rstd": sb("np_rstd", [128, 256], F32), "i": 0}
        wt = [sb("np_w%d" % i, [128, KC, 512], BF16) for i in range(2)]
        stg = [sb("np_stg%d" % i, [128, 512], BF16) for i in range(3)]
        stf = [sb("np_stf%d" % i, [128, 32], F32) for i in range(2)]
        wi = 0
        si = 0
        for sbk in superblocks(True, 4):
            off = 0
            tiles = []
            sl = []
            for (pos, n, stream) in sbk:
                norm_into(g, R, li, 1, pos, n, stream, hT, off)
                for a in range(0, n, 128):
                    tiles.append((off + a, pos + a, stream))
                sl.append((off, pos, n))
                off += n
            for (c0, ncol, sname, dc, isf, cm, isfm) in jobs:
                w = wt[wi % 2]
                wi += 1
                K.dma("pool", w[:, :, :ncol], I["w_in"][li][:, c0:c0 + ncol].rearrange("(k p) n -> p k n", p=128),
                      w.b, writes=[w.b])
                dst = S[sname]
                W = dst.shape[1]
                if isfm:
                    for j in range(ncol // 128):
                        for (soff, pos, n) in sl:
                            ps = next_ps(g)
                            for kc in range(KC):
                                K.op("pe", lambda e, kc=kc, ps=ps, w=w, soff=soff, n=n, j=j: e.matmul(
                                    ps[:, :n], w[:, kc, j * 128:(j + 1) * 128], hT[:, kc, soff:soff + n], start=(kc == 0),
                                    stop=(kc == KC - 1)), [hT.b, w.b], [ps.b])
                            st = stg[si % 3]
                            si += 1
                            if si % 2 == 0:
                                K.op("act", lambda e, st=st, ps=ps, n=n: e.copy(st[:, :n], ps[:, :n]), [ps.b], [st.b])
                            else:
                                K.op("dve", lambda e, st=st, ps=ps, n=n: e.tensor_copy(st[:, :n], ps[:, :n]), [ps.b], [st.b])
                            K.dma("sp", dst[dc + j * 128:dc + (j + 1) * 128, pos:pos + n], st[:, :n], st.b, reads=[st.b],
                                  writes=[g.dbuf(sname, (dc, j, pos))])
                    continue
                for (toff, pos, stream) in tiles:
                    ps = next_ps(g)
                    for kc in range(KC):
                        K.op("pe", lambda e, kc=kc, ps=ps, w=w, toff=toff, ncol=ncol: e.matmul(
                            ps[:, :ncol], hT[:, kc, toff:toff + 128], w[:, kc, :ncol], start=(kc == 0),
                            stop=(kc == KC - 1)), [hT.b, w.b], [ps.b])
                    if isf:
                        st = stf[si % 2]
                    else:
                        st = stg[si % 3]
                    si += 1
                    if si % 2 == 0:
                        K.op("act", lambda e, st=st, ps=ps, ncol=ncol: e.copy(st[:, :ncol], ps[:, :ncol]), [ps.b], [st.b])
                    else:
                        K.op("dve", lambda e, st=st, ps=ps, ncol=ncol: e.tensor_copy(st[:, :ncol], ps[:, :ncol]), [ps.b], [st.b])
                    if cm and stream == 0:
                        for (plo, row0, rs) in colmajor_row_dmas(pos):
                            K.dma("sp", dram_ap(dst, row0 * W + dc, [[rs * W, 64], [1, ncol]]), st[plo:plo + 64, :ncol],
                                  st.b, reads=[st.b], writes=[g.dbuf(sname, (c0, pos, plo))])
                    else:
                        K.dma("sp", dst[pos:pos + 128, dc:dc + ncol], st[:, :ncol], st.b, reads=[st.b],
                              writes=[g.dbuf(sname, (c0, pos))])
        K.barrier()


def load_wtile_fm(g, w, wsrc, ot, kcn):
    g.K.dma("pool", w[:, :kcn, :], wsrc[:, ot * 128:(ot + 1) * 128].rearrange("(k p) n -> p k n", p=128),
            w.b, writes=[w.b])


def load_wgroup_fm(g, w, wsrc, og, kcn):
    g.K.dma("pool", w[:, :kcn, :], wsrc[:, og * 512:(og + 1) * 512].rearrange("(k p) n -> p k n", p=128),
            w.b, writes=[w.b])


def phase_wout(g, li, with_ctx):
    nc, K, I, S = g.nc, g.K, g.I, g.S
    xTv = S["xT"]
    with ExitStack() as es:
        def sb(name, shape, dt):
            return Tile(es.enter_context(nc.sbuf_tensor(uniq(name), list(shape), dt)), name)
        yTs = [sb("wo_yT0", [128, KC, 2304], BF16), sb("wo_yT1", [128, KC, 2048], BF16)]
        yin = [sb("wo_yin%d" % i, [128, D], BF16) for i in range(2)]
        wt = [sb("wo_w%d" % i, [128, KC, 512], BF16) for i in range(2)]
        xt = [sb("wo_xt%d" % i, [128, 512], F32) for i in range(3)]
        cnt = {"yi": 0}
        sbs = superblocks(with_ctx, 4)

        def tile_job(yT, tpos, toff):
            def run():
                y = yin[cnt["yi"] % 2]
                cnt["yi"] += 1
                K.dma("sp", y[:], S["Y"][tpos:tpos + 128, :], y.b, reads=[g.dbuf("Y")], writes=[y.b])
                for h in range(2):
                    pb = next_psb(g)
                    for j in range(8):
                        kc = h * 8 + j
                        K.op("pe", lambda e, kc=kc, j=j, pb=pb: e.transpose(
                            pb[:, j * 128:(j + 1) * 128], y[:, kc * 128:(kc + 1) * 128], g.ident_b[:]),
                            [y.b, g.ident_b.b], [pb.b])
                    dstv = yT[:, h * 8:(h + 1) * 8, toff:toff + 128]
                    srcv = pb[:].rearrange("p (a b) -> p a b", a=8)
                    if h == 0:
                        K.op("act", lambda e: e.copy(dstv, srcv), [pb.b], [yT.b])
                    else:
                        K.op("dve", lambda e: e.tensor_copy(dstv, srcv), [pb.b], [yT.b])
            return run

        plans = []
        for k, sbk in enumerate(sbs):
            yT = yTs[k % 2]
            off = 0
            sl, jobs = [], []
            for (pos, n, stream) in sbk:
                for a_ in range(0, n, 128):
                    jobs.append(tile_job(yT, pos + a_, off + a_))
                sl.append((off, pos, n, stream))
                off += n
            plans.append((yT, sl, jobs))
        for job in plans[0][2]:
            job()
        wi = xi = 0
        for k, (yT, sl, _) in enumerate(plans):
            nxt = list(plans[k + 1][2]) if k + 1 < len(plans) else []
            per = -(-len(nxt) // KC) if nxt else 0
            for ot in range(KC):
                if ot % 4 == 0:
                    w = wt[wi % 2]
                    wi += 1
                    load_wgroup_fm(g, w, I["w_out"][li], ot // 4, KC)
                wc = (ot % 4) * 128
                for (soff, pos, n, stream) in sl:
                    x = xt[xi % 3]
                    xi += 1
                    blks = [g.dbuf("xT", (pos + a_) // 128) for a_ in range(0, n, 128)]
                    K.dma("sp", x[:, :n], xTv[ot * 128:(ot + 1) * 128, pos:pos + n], x.b, reads=blks, writes=[x.b])
                    ps = next_ps(g)
                    for kc in range(KC):
                        K.op("pe", lambda e, kc=kc, ps=ps, w=w, soff=soff, n=n, wc=wc, yT=yT: e.matmul(
                            ps[:, :n], w[:, kc, wc:wc + 128], yT[:, kc, soff:soff + n], start=(kc == 0), stop=(kc == KC - 1)),
                            [w.b, yT.b], [ps.b])
                    gate = g.mod[:, li * 2 + stream, 2 * 16 + ot:2 * 16 + ot + 1]
                    K.op("dve", lambda e, x=x, ps=ps, n=n, gate=gate: e.scalar_tensor_tensor(
                        out=x[:, :n], in0=ps[:, :n], scalar=gate, in1=x[:, :n], op0=ALU.mult, op1=ALU.add),
                        [ps.b, x.b, g.mod.b], [x.b])
                    K.dma("sp", xTv[ot * 128:(ot + 1) * 128, pos:pos + n], x[:, :n], x.b, reads=[x.b], writes=blks)
                for _ in range(per):
                    if nxt:
                        nxt.pop(0)()
            while nxt:
                nxt.pop(0)()
        K.barrier()


def phase_ffn_up(g, li, with_ctx):
    nc, K, I, S = g.nc, g.K, g.I, g.S
    with ExitStack() as es:
        def sb(name, shape, dt):
            return Tile(es.enter_context(nc.sbuf_tensor(uniq(name), list(shape), dt)), name)
        hT = sb("fu_hT", [128, KC, 2304], BF16)
        R = {"xin": [sb("fu_xin%d" % i, [128, KC, 256], F32) for i in range(2)],
             "sq": sb("fu_sq", [128, KC, 256], BF16), "rstd": sb("fu_rstd", [128, 256], F32), "i": 0}
        wg = [sb("fu_wg%d" % i, [128, KC, 512], BF16) for i in range(2)]
        wu = [sb("fu_wu%d" % i, [128, KC, 512], BF16) for i in range(2)]
        sg = [sb("fu_sg%d" % i, [128, 512], F32) for i in range(2)]
        strip = [sb("fu_strip%d" % i, [128, 2304], BF16) for i in range(2)]
        wi = gi = 0
        for sbk in superblocks(with_ctx, 4):
            off = 0
            sl = []
            for (pos, n, stream) in sbk:
                norm_into(g, R, li, 2, pos, n, stream, hT, off)
                sl.append((off, pos, n, stream))
                off += n
            ntok = off
            p0 = sbk[0][0]
            for ft in range(KF):
                if ft % 4 == 0:
                    a, b = wg[wi % 2], wu[wi % 2]
                    wi += 1
                    load_wgroup_fm(g, a, I["w_gate"][li], ft // 4, KC)
                    load_wgroup_fm(g, b, I["w_up"][li], ft // 4, KC)
                wc = (ft % 4) * 128
                st = strip[ft % 2]
                for (soff, pos, n, stream) in sl:
                    pg = next_ps(g)
                    for kc in range(KC):
                        K.op("pe", lambda e, kc=kc, pg=pg, a=a, soff=soff, n=n, wc=wc: e.matmul(
                            pg[:, :n], a[:, kc, wc:wc + 128], hT[:, kc, soff:soff + n], start=(kc == 0), stop=(kc == KC - 1)),
                            [a.b, hT.b], [pg.b])
                    pu = next_ps(g)
                    for kc in range(KC):
                        K.op("pe", lambda e, kc=kc, pu=pu, b=b, soff=soff, n=n, wc=wc: e.matmul(
                            pu[:, :n], b[:, kc, wc:wc + 128], hT[:, kc, soff:soff + n], start=(kc == 0), stop=(kc == KC - 1)),
                            [b.b, hT.b], [pu.b])
                    s_ = sg[gi % 2]
                    gi += 1
                    K.op("act", lambda e, s_=s_, pg=pg, n=n: e.activation(out=s_[:, :n], in_=pg[:, :n], func=AF.Silu),
                         [pg.b], [s_.b])
                    K.op("dve", lambda e, s_=s_, pu=pu, n=n, st=st, soff=soff: e.tensor_tensor(
                        st[:, soff:soff + n], s_[:, :n], pu[:, :n], ALU.mult), [s_.b, pu.b], [st.b])
                K.dma("sp", S["HID"][ft * 128:(ft + 1) * 128, p0:p0 + ntok], st[:, :ntok], st.b, reads=[st.b],
                      writes=[g.dbuf("HID", (ft, p0))])
        K.barrier()


def phase_ffn_down(g, li, with_ctx):
    nc, K, I, S = g.nc, g.K, g.I, g.S
    xTv = S["xT"]
    sbs = [lat_slices(i, i + 2) for i in range(0, 8, 2)]
    if with_ctx:
        sbs = [[(0, CTX, 1)]] + sbs
    with ExitStack() as es:
        def sb(name, shape, dt):
            return Tile(es.enter_context(nc.sbuf_tensor(uniq(name), list(shape), dt)), name)
        hid = sb("fd_hid", [128, KF, 1024], BF16)
        hidq = [Buf("fd_hidq%d" % q) for q in range(4)]
        wt = [sb("fd_w%d" % i, [128, KF, 512], BF16) for i in range(2)]
        xt = [sb("fd_xt%d" % i, [128, 512], F32) for i in range(3)]
        wi = xi = 0
        for sbk in sbs:
            p0 = sbk[0][0]
            ntok = sum(s[1] for s in sbk)
            hv = S["HID"].rearrange("(k p) t -> p k t", p=128)
            for q in range(4):
                K.dma("sp", hid[:, q * 11:(q + 1) * 11, :ntok], hv[:, q * 11:(q + 1) * 11, p0:p0 + ntok], hidq[q],
                      reads=[g.dbuf("HID")], writes=[hidq[q]])
            for ot in range(KC):
                if ot % 4 == 0:
                    w = wt[wi % 2]
                    wi += 1
                    load_wgroup_fm(g, w, I["w_down"][li], ot // 4, KF)
                wc = (ot % 4) * 128
                for (pos, n, stream) in sbk:
                    soff = pos - p0
                    x = xt[xi % 3]
                    xi += 1
                    blks = [g.dbuf("xT", (pos + a) // 128) for a in range(0, n, 128)]
                    K.dma("sp", x[:, :n], xTv[ot * 128:(ot + 1) * 128, pos:pos + n], x.b, reads=blks, writes=[x.b])
                    ps = next_ps(g)
                    for kc in range(KF):
                        K.op("pe", lambda e, kc=kc, ps=ps, w=w, soff=soff, n=n, wc=wc: e.matmul(
                            ps[:, :n], w[:, kc, wc:wc + 128], hid[:, kc, soff:soff + n], start=(kc == 0), stop=(kc == KF - 1)),
                            [w.b, hidq[kc // 11]], [ps.b])
                    gate = g.mod[:, li * 2 + stream, 5 * 16 + ot:5 * 16 + ot + 1]
                    K.op("dve", lambda e, x=x, ps=ps, n=n, gate=gate: e.scalar_tensor_tensor(
                        out=x[:, :n], in0=ps[:, :n], scalar=gate, in1=x[:, :n], op0=ALU.mult, op1=ALU.add),
                        [ps.b, x.b, g.mod.b], [x.b])
                    K.dma("sp", xTv[ot * 128:(ot + 1) * 128, pos:pos + n], x[:, :n], x.b, reads=[x.b], writes=blks)
        K.barrier()


def phase_final(g):
    nc, K, I, S = g.nc, g.K, g.I, g.S
    xTv = S["xT"].rearrange("(k p) t -> p k t", p=128)
    with ExitStack() as es:
        def sb(name, shape, dt):
            return Tile(es.enter_context(nc.sbuf_tensor(uniq(name), list(shape), dt)), name)
        xin = [sb("fn_xin%d" % i, [128, KC, 128], F32) for i in range(2)]
        sq = sb("fn_sq", [128, KC, 128], BF16)
        rstd = sb("fn_rstd", [128, 128], F32)
        ot = [sb("fn_o%d" % i, [128, D], F32) for i in range(2)]
        for ti in range(L // 128):
            pos = CTX + ti * 128
            x = xin[ti % 2]
            o = ot[ti % 2]
            K.dma("sp", x[:], xTv[:, :, pos:pos + 128], x.b, reads=[g.dbuf("xT", pos // 128)], writes=[x.b])
            K.op("act", lambda e, x=x: e.activation(out=sq[:], in_=x[:], func=AF.Square), [x.b], [sq.b])
            ps = next_ps(g)
            for kc in range(KC):
                K.op("pe", lambda e, kc=kc, ps=ps: e.matmul(ps[:, :128], g.ones_b[:], sq[:, kc, :], start=(kc == 0),
                                                            stop=(kc == KC - 1)), [sq.b, g.ones_b.b], [ps.b])
            K.op("act", lambda e, ps=ps: e.activation(out=rstd[:], in_=ps[:, :128], func=AF.Sqrt, scale=1.0 / D,
                                                      bias=g.epsc[:, 0:1]), [ps.b, g.epsc.b], [rstd.b])
            K.op("dve", lambda e: e.reciprocal(rstd[:], rstd[:]), [rstd.b], [rstd.b])
            K.op("dve", lambda e, x=x: e.tensor_tensor(x[:], x[:], bc_mid(rstd[:], KC), ALU.mult), [x.b, rstd.b], [x.b])
            K.op("dve", lambda e, x=x: e.tensor_tensor(x[:], x[:], bc_last(g.gcol[:, 4, :], 128), ALU.mult),
                 [x.b, g.gcol.b], [x.b])
            for q in range(4):
                pt = next_ps(g)
                for j in range(4):
                    kc = q * 4 + j
                    K.op("pe", lambda e, kc=kc, j=j, pt=pt, x=x: e.transpose(
                        pt[:, j * 128:(j + 1) * 128], x[:, kc, :], g.ident_f[:]), [x.b, g.ident_f.b], [pt.b])
                if q % 2 == 0:
                    K.op("act", lambda e, q=q, pt=pt, o=o: e.copy(o[:, q * 512:(q + 1) * 512], pt[:]), [pt.b], [o.b])
                else:
                    K.op("dve", lambda e, q=q, pt=pt, o=o: e.tensor_copy(o[:, q * 512:(q + 1) * 512], pt[:]), [pt.b], [o.b])
            K.dma("sp", g.out[ti * 128:(ti + 1) * 128, :], o[:], o.b, reads=[o.b], writes=[g.dbuf("out", ti)])
        K.barrier()


def make_in_maps(inp, cores):
    f = lambda a: np.ascontiguousarray(np.asarray(a, dtype=np.float32))
    shared = {k: f(inp[k]) for k in ("w_mod", "w_in", "w_out", "w_gate", "w_up", "w_down")}
    shared["consts"] = host_consts()
    col = lambda v: f(v).reshape(-1, 128).T
    shared["gcol"] = f(np.stack([col(inp["norm1_g"][0]), col(inp["norm1_g"][1]), col(inp["norm2_g"][0]),
                                 col(inp["norm2_g"][1]), col(inp["final_g"])], 1))
    shared["bmodcol"] = f(np.stack([col(inp["b_mod"][i]) for i in range(DEPTH)], 1))
    shared.update(host_mixer_params(inp))
    maps = []
    for b in cores:
        m = dict(shared)
        m["x"] = f(inp["x"][b])
        m["ctx"] = f(inp["ctx"][b])
        m["ccol"] = f(np.stack([col(inp["c"][b]), col(inp["c_ctx"])], 1))
        maps.append(m)
    return maps


def host_mixer_params(inp):
    f = lambda a: np.asarray(a, dtype=np.float32)
    bc = lambda v: np.broadcast_to(f(v).reshape(1, -1), (128, f(v).size))
    out = {k: np.zeros(shp, np.float32) for k, shp in MIX_PARAM_SHAPES.items()}

    def convcols(w, b, nt):
        o = np.zeros((128, nt, 6), np.float32)
        o[:, :, 0:5] = f(w).T.reshape(nt, 128, 5).transpose(1, 0, 2)
        o[:, :, 5] = f(b).reshape(nt, 128).T
        return o
    for i in range(DEPTH):
        out["pA_tm"][i] = np.concatenate([bc(inp["ssd_dt_bias"][i]), bc(inp["ssd_a_log"][i]),
                                          bc(np.repeat(f(inp["ssd_d"][i]), 64)), bc(inp["ssd_norm_g"][i])], 1)
        out["pA_conv"][i] = convcols(inp["ssd_conv_w"][i], inp["ssd_conv_b"][i], 8)
        out["pB_tm"][i] = np.concatenate([bc(inp["ml_igate_b"][i]), bc(inp["ml_fgate_b"][i]), bc(inp["ml_norm_g"][i])], 1)
        out["pB_conv"][i] = convcols(inp["ml_conv_w"][i], inp["ml_conv_b"][i], 8)
        pc = np.zeros((128, 4, 12), np.float32)
        pc[:, :, 0:6] = convcols(inp["lru_conv_w"][i], inp["lru_conv_b"][i], 4)
        for d in range(2):
            pc[:, :, 6 + d] = f(inp["lru_ba"][i][d]).reshape(4, 128).T
            pc[:, :, 8 + d] = f(inp["lru_bx"][i][d]).reshape(4, 128).T
            pc[:, :, 10 + d] = f(inp["lru_lambda"][i][d]).reshape(4, 128).T
        out["pC_col"][i] = pc
        for d in range(2):
            for which, nm in enumerate(("lru_wa", "lru_wx")):
                w = f(inp[nm][i][d])
                for ct in range(4):
                    for nl in range(2):
                        out["pC_w"][i, (d * 2 + which) * 4 + ct, nl * 64:(nl + 1) * 64, nl * 64:(nl + 1) * 64] = w[2 * ct + nl]
        out["pD_tm"][i] = np.concatenate([bc(f(inp["gla_bg"][i]).reshape(-1)), bc(inp["gla_norm_g"][i])], 1)
        for d in range(2):
            out["pD_wg2"][i, d * 16:(d + 1) * 16, d * 256:(d + 1) * 256] = f(inp["gla_wg2"][i][d])
    return out


def kernel(**inputs):
    nc = build_program()
    cores = [0, 1, 2, 3]
    maps = make_in_maps(inputs, cores)
    res = run_bass_kernel_spmd(nc, maps, core_ids=list(range(len(cores))))
    return np.stack([np.asarray(r["out"], dtype=np.float32) for r in res.results], 0)


def load_fm_from_tm(g, sb, U, col0, ntile, name):
    K = g.K
    fm = sb(name, [128, ntile, T], BF16)
    tin = [sb(name + "_in%d" % i, [128, ntile * 128], BF16) for i in range(2)]
    ei = 0
    for c in range(NCH):
        t = tin[c % 2]
        K.dma("sp", t[:], U[c * 128:(c + 1) * 128, col0:col0 + ntile * 128], t.b, writes=[t.b])
        for h0 in range(0, ntile, 8):
            nb = min(8, ntile - h0)
            pb = next_psb(g)
            for j in range(nb):
                K.op("pe", lambda e, j=j, pb=pb, t=t, h0=h0: e.transpose(
                    pb[:, j * 128:(j + 1) * 128], t[:, (h0 + j) * 128:(h0 + j + 1) * 128], g.ident_b[:]),
                    [t.b, g.ident_b.b], [pb.b])
            dstv = fm[:, h0:h0 + nb, c * 128:(c + 1) * 128]
            srcv = pb[:, :nb * 128].rearrange("p (a b) -> p a b", a=nb)
            ei += 1
            if ei % 2 == 0:
                K.op("act", lambda e, dstv=dstv, srcv=srcv: e.copy(dstv, srcv), [pb.b], [fm.b])
            else:
                K.op("dve", lambda e, dstv=dstv, srcv=srcv: e.tensor_copy(dstv, srcv), [pb.b], [fm.b])
    return fm


def conv_fm(g, eng, src, srcb, dst, cw, ct):
    K = g.K
    for (a, b) in ((0, CTX), (CTX, T)):
        K.op(eng, lambda e, a=a, b=b: e.tensor_scalar(dst[:, a:b], src[:, a:b], cw[:, ct, 2:3], cw[:, ct, 5:6],
                                                       ALU.mult, ALU.add), [srcb, cw.b], [dst.b])
        for j in (0, 1, 3, 4):
            d = j - 2
            lo = a + max(0, -d)
            hi = b - max(0, d)
            K.op(eng, lambda e, lo=lo, hi=hi, d=d, j=j: e.scalar_tensor_tensor(
                out=dst[:, lo:hi], in0=src[:, lo + d:hi + d], scalar=cw[:, ct, j:j + 1], in1=dst[:, lo:hi],
                op0=ALU.mult, op1=ALU.add), [srcb, cw.b, dst.b], [dst.b])


CONV_BLOCKS = [(0, CTX, 0, CTX)] + [(CTX + 512 * i, 512, CTX, T) for i in range(8)]


def build_diag(g, sb, cw, ntile, name):
    K = g.K
    dg = sb(name, [128, ntile * 5, 128], BF16)
    for ct in range(ntile):
        for j in range(5):
            K.op("dve", lambda e, ct=ct, j=j: e.tensor_scalar_mul(dg[:, ct * 5 + j, :], g.ident_f[:], cw[:, ct, j:j + 1]),
                 [g.ident_f.b, cw.b], [dg.b])
    return dg


def conv_pe(g, src, srcb, dg, dct, cw, cct, func, out_ap_fn, dstb):
    K = g.K
    for (t0, n, a, b) in CONV_BLOCKS:
        ps = next_ps(g)
        for idx, j in enumerate((2, 0, 1, 3, 4)):
            d = j - 2
            lo = max(t0, a - d)
            hi = min(t0 + n, b - d)
            K.op("pe", lambda e, ps=ps, lo=lo, hi=hi, d=d, j=j, t0=t0, idx=idx: e.matmul(
                ps[:, lo - t0:hi - t0], dg[:, dct * 5 + j, :], src[:, lo + d:hi + d], start=(idx == 0), stop=(idx == 4)),
                [dg.b, srcb], [ps.b])
        K.op("act", lambda e, ps=ps, t0=t0, n=n: e.activation(out=out_ap_fn(t0, n), in_=ps[:, :n], func=func,
                                                              bias=cw[:, cct, 5:6]), [ps.b, cw.b], [dstb])


def rstd_cols(g, out, in_, scale, n_reads, out_b):
    K = g.K
    K.op("act", lambda e: e.activation(out=out, in_=in_, func=AF.Ln, scale=scale, bias=g.epsc[:, 0:1]),
         n_reads + [g.epsc.b], [out_b])
    K.op("act", lambda e: e.activation(out=out, in_=out, func=AF.Exp, scale=-0.5), [out_b], [out_b])


def cm_store(g, dst, dcol, ncol, c, st, W=D):
    K = g.K
    if c < 2:
        K.dma("sp", dst[c * 128:(c + 1) * 128, dcol:dcol + ncol], st[:, :ncol], st.b, reads=[st.b],
              writes=[g.dbuf("Ycm", (dcol, c))])
        return
    for hf in range(2):
        col = 2 * (c - 2) + hf
        K.dma("sp", dram_ap(dst, (CTX + col) * W + dcol, [[64 * W, 64], [1, ncol]]), st[hf * 64:(hf + 1) * 64, :ncol],
              st.b, reads=[st.b], writes=[g.dbuf("Ycm", (dcol, c, hf))])


def chunk_order(d):
    return [0, 1] + list(range(2, NCH)) if d == 0 else [1, 0] + list(range(NCH - 1, 1, -1))


def phase_ssd(g, li, need_ctx):
    nc, K, I, S = g.nc, g.K, g.I, g.S
    P_intra, P_upd, P_inter = g.psum[0], g.psum[1], g.psum[2]
    with ExitStack() as es:
        def sb(name, shape, dt):
            return Tile(es.enter_context(nc.sbuf_tensor(uniq(name), list(shape), dt)), name)
        ptm = sb("a_ptm", [128, 1056], F32)
        cw = sb("a_cw", [128, 8, 6], F32)
        K.dma("sp", ptm[:], I["pA_tm"][li], ptm.b, writes=[ptm.b])
        K.dma("sp", cw[:], I["pA_conv"][li], cw.b, writes=[cw.b])
        fm = sb("a_fm", [128, 8, T], BF16)
        with ExitStack() as es2:
            def sb2(name, shape, dt):
                return Tile(es2.enter_context(nc.sbuf_tensor(uniq(name), list(shape), dt)), name)
            fpre = sb2("a_fpre", [128, 8, T], BF16)
            for ct in range(8):
                K.dma("sp", fpre[:, ct, :], S["UAf"][ct * 128:(ct + 1) * 128, :], fpre.b, writes=[fpre.b])
            dg = build_diag(g, sb2, cw, 8, "a_dg")
            for ct in range(8):
                conv_pe(g, fpre[:, ct, :], fpre.b, dg, ct, cw, ct, AF.Silu, lambda t0, n, ct=ct: fm[:, ct, t0:t0 + n], fm.b)
            K.barrier()
        ga = sb("a_ga", [128, NCH, 16], F32)
        dt = sb("a_dt", [128, NCH, 16], F32)
        loga = sb("a_loga", [128, NCH, 16], F32)
        lndt = sb("a_lndt", [128, NCH, 16], F32)
        nega = sb("a_nega", [128, 16], F32)
        K.dma("sp", ga[:], S["GA"].rearrange("(c p) j -> p c j", p=128), ga.b, writes=[ga.b])
        K.op("dve", lambda e: e.tensor_tensor(ga[:], ga[:], bc_mid(ptm[:, 0:16], NCH), ALU.add), [ga.b, ptm.b], [ga.b])
        K.op("act", lambda e: e.activation(out=dt[:], in_=ga[:], func=AF.Exp), [ga.b], [dt.b])
        K.op("act", lambda e: e.activation(out=dt[:], in_=dt[:], func=AF.Ln, bias=g.ones_f[:, 0:1]), [dt.b, g.ones_f.b], [dt.b])
        K.op("act", lambda e: e.activation(out=lndt[:], in_=dt[:], func=AF.Ln), [dt.b], [lndt.b])
        K.op("act", lambda e: e.activation(out=nega[:], in_=ptm[:, 16:32], func=AF.Exp), [ptm.b], [nega.b])
        K.op("dve", lambda e: e.tensor_scalar_mul(nega[:], nega[:], -1.0), [nega.b], [nega.b])
        K.op("dve", lambda e: e.tensor_tensor(loga[:], dt[:], bc_mid(nega[:], NCH), ALU.mult), [dt.b, nega.b], [loga.b])

        zA = sb("a_zA", [128, NCH, 512], BF16)
        K.dma("sp", zA[:], S["UA"][:, 0:512].rearrange("(c p) n -> p c n", p=128), zA.b, writes=[zA.b])
        for (c0_, c1_) in ((0, 17), (17, NCH)):
            K.op("act", lambda e, c0_=c0_, c1_=c1_: e.activation(out=zA[:, c0_:c1_, :], in_=zA[:, c0_:c1_, :], func=AF.Silu), [zA.b], [zA.b])
        ST = sb("a_ST", [128, 512], F32)
        STb = sb("a_STb", [128, 512], BF16)
        xs = [sb("a_xs%d" % i, [128, 768], BF16) for i in range(2)]
        xw = [sb("a_xw%d" % i, [128, 512], BF16) for i in range(2)]
        bc = sb("a_bc", [128, 8], F32)
        expb = sb("a_expb", [128, 8], F32)
        biasc = sb("a_biasc", [128, 8], F32)
        ebl = sb("a_ebl", [128, 8], F32)
        cbs = sb("a_cbs", [128, 256], F32)
        Rt = [sb("a_R%d" % i, [128, 8, 128], F32) for i in range(2)]
        neg4 = [sb("a_neg4_%d" % i, [128, 4, 128], BF16) for i in range(2)]
        for i, nm in enumerate((g.negF_f, g.negB_f)):
            K.op("dve", lambda e, i=i, nm=nm: e.tensor_copy(neg4[i][:], bc_mid(nm[:], 4)), [nm.b], [neg4[i].b])
        PD = [g.psum[3], g.psum[4]]
        E = [sb("a_E%d" % i, [128, 128], F32) for i in range(2)]
        wT = [sb("a_wT%d" % i, [128, 128], BF16) for i in range(2)]
        yint = sb("a_yint", [128, 512], F32)
        t1 = sb("a_t1", [128, 512], F32)
        t2 = sb("a_t2", [128, 512], F32)
        y = [sb("a_y%d" % i, [128, 512], F32) for i in range(2)]
        yf = [sb("a_yf%d" % i, [128, 512], F32) for i in range(2)]
        zt = [sb("a_z%d" % i, [128, 512], BF16) for i in range(2)]
        szs = [sb("a_sz%d" % i, [128, 512], F32) for i in range(2)]
        ss = sb("a_ss", [128, 1], F32)
        ob = [sb("a_ob%d" % i, [128, 512], BF16) for i in range(2)]
        YF = nc.dram_tensor("s_YFA_%d" % li, [T, 512], F32, kind="Internal").ap()
        iters = [(d_, c_) for d_ in range(2) for c_ in chunk_order(d_)]

        def pre(j):
            if j > len(iters):
                return
            d_, c_ = iters[j - 1]
            if d_ == 1 and not (c_ < 2 and not need_ctx):
                K.dma("sp", yf[j % 2][:], YF[c_ * 128:(c_ + 1) * 128, :], yf[j % 2].b, reads=[g.dbuf("YFA", c_)],
                      writes=[yf[j % 2].b])
        bcs = [sb("a_bcs%d" % i, [128, 8], F32) for i in range(2)]
        expbs = [sb("a_expbs%d" % i, [128, 8], F32) for i in range(2)]
        biascs = [sb("a_biascs%d" % i, [128, 8], F32) for i in range(2)]
        cbss = [sb("a_cbss%d" % i, [128, 256], F32) for i in range(2)]
        dirc = [(g.triU_f, 127), (g.triL_f, 0)]

        def prep(j):
            if j > len(iters):
                return
            d_, c_ = iters[j - 1]
            q0 = c_ * 128
            par = j % 2
            tri_ = dirc[d_][0]
            xsb_ = xs[par]
            pb = next_psb(g)
            for jj in range(6):
                K.op("pe", lambda e, jj=jj, pb=pb: e.transpose(pb[:, jj * 128:(jj + 1) * 128], fm[:, jj, q0:q0 + 128],
                                                               g.ident_b[:]), [fm.b, g.ident_b.b], [pb.b])
            K.op("act", lambda e, pb=pb: e.copy(xsb_[:], pb[:, :768]), [pb.b], [xsb_.b])
            pq = next_ps(g, 5)
            K.op("pe", lambda e, pq=pq: e.matmul(pq[:, 0:8], tri_[:], loga[:, c_, d_ * 8:(d_ + 1) * 8], start=True,
                                                 stop=True), [tri_.b, loga.b], [pq.b])
            K.op("dve", lambda e, pq=pq: e.tensor_copy(bcs[par][:], pq[:, 0:8]), [pq.b], [bcs[par].b])
            K.op("act", lambda e, pq=pq: e.activation(out=expbs[par][:], in_=pq[:, 0:8], func=AF.Exp), [pq.b], [expbs[par].b])
            K.op("dve", lambda e: e.tensor_tensor(biascs[par][:], lndt[:, c_, d_ * 8:(d_ + 1) * 8], bcs[par][:], ALU.subtract),
                 [lndt.b, bcs[par].b], [biascs[par].b])
            pcb = next_ps(g, 5)
            for gi in range(2):
                K.op("pe", lambda e, gi=gi, pcb=pcb: e.matmul(
                    pcb[:, gi * 128:(gi + 1) * 128], fm[:, 4 + gi, q0:q0 + 128], fm[:, 6 + gi, q0:q0 + 128],
                    start=True, stop=True), [fm.b], [pcb.b])
            K.op("act", lambda e, pcb=pcb: e.copy(cbss[par][:], pcb[:, :256]), [pcb.b], [cbss[par].b])
            Rb = Rt[par]
            K.op("dve", lambda e: e.tensor_tensor(Rb[:], bc_mid(tri_[:], 8), bc_last(loga[:, c_, d_ * 8:(d_ + 1) * 8], 128), ALU.mult),
                 [tri_.b, loga.b], [Rb.b])
            for hf in range(2):
                K.op("pe", lambda e, hf=hf: e.matmul(PD[hf][:], g.ones_f[:], Rb[:, hf * 4:(hf + 1) * 4, :].rearrange("p a b -> p (a b)"),
                                                     start=True, stop=False), [g.ones_f.b, Rb.b], [PD[hf].b])
                K.op("pe", lambda e, hf=hf: e.matmul(PD[hf][:], g.ident_b[:], neg4[d_][:].rearrange("p a b -> p (a b)"),
                                                     start=False, stop=True), [g.ident_b.b, neg4[d_].b], [PD[hf].b])

        late_fn = None
        pre(1)
        prep(1)
        for ci, (d, c) in enumerate(iters, 1):
            p0 = c * 128
            last = dirc[d][1]
            par = ci % 2
            if ci == 1 or ci == NCH + 1:
                K.op("dve", lambda e: e.memset(ST[:], 0.0), [], [ST.b])
                K.op("dve", lambda e: e.memset(STb[:], 0.0), [], [STb.b])
            pre(ci + 1)
            xsb, xwb, yb = xs[par], xw[par], y[par]
            biasc, expb, cbs = biascs[par], expbs[par], cbss[par]
            for gi in range(2):
                K.op("pe", lambda e, gi=gi: e.matmul(
                    P_inter[:, gi * 256:(gi + 1) * 256], fm[:, 6 + gi, p0:p0 + 128], STb[:, gi * 256:(gi + 1) * 256],
                    start=True, stop=True), [fm.b, STb.b], [P_inter.b])
            for h in range(8):
                gi = h // 4
                eh, wh = E[h % 2], wT[h % 2]
                hs = slice(h * 64, (h + 1) * 64)
                pD = PD[h // 4]
                dc = (h % 4) * 128
                K.op("act", lambda e, pD=pD, eh=eh, h=h, dc=dc: e.activation(out=eh[:], in_=pD[:, dc:dc + 128], func=AF.Exp,
                                                                            bias=biasc[:, h:h + 1]), [pD.b, biasc.b], [eh.b])
                K.op("act", lambda e, pD=pD, h=h, dc=dc: e.activation(out=ebl[:, h:h + 1], in_=pD[:, dc + last:dc + last + 1],
                                                                     func=AF.Exp), [pD.b], [ebl.b])
                K.op("dve", lambda e, wh=wh, eh=eh, gi=gi: e.tensor_tensor(wh[:], cbs[:, gi * 128:(gi + 1) * 128], eh[:],
                                                                          ALU.mult), [cbs.b, eh.b], [wh.b])
                K.op("pe", lambda e, wh=wh, hs=hs: e.matmul(P_intra[:, hs], wh[:], xsb[:, hs], start=True, stop=True),
                     [wh.b, xsb.b], [P_intra.b])
                K.op("dve", lambda e, eh=eh, hs=hs: e.tensor_scalar_mul(xwb[:, hs], xsb[:, hs], eh[:, last:last + 1]),
                     [xsb.b, eh.b], [xwb.b])
                K.op("pe", lambda e, gi=gi, hs=hs: e.matmul(
                    P_upd[:, hs], xsb[:, 512 + gi * 128:512 + (gi + 1) * 128], xwb[:, hs], start=True, stop=True),
                    [xsb.b, xwb.b], [P_upd.b])
            prep(ci + 1)
            K.op("act", lambda e: e.copy(yint[:], P_intra[:]), [P_intra.b], [yint.b])
            K.op("dve", lambda e: e.tensor_tensor(t1[:].rearrange("p (h q) -> p h q", h=8),
                                                  P_inter[:].rearrange("p (h q) -> p h q", h=8), bc_last(expb[:], 64), ALU.mult),
                 [P_inter.b, expb.b], [t1.b])
            K.op("dve", lambda e: e.tensor_tensor(yb[:], t1[:], yint[:], ALU.add), [t1.b, yint.b], [yb.b])
            K.op("dve", lambda e: e.tensor_tensor(ST[:].rearrange("p (h q) -> p h q", h=8),
                                                  ST[:].rearrange("p (h q) -> p h q", h=8), bc_last(ebl[:], 64), ALU.mult),
                 [ST.b, ebl.b], [ST.b])
            K.op("dve", lambda e: e.tensor_tensor(ST[:], ST[:], P_upd[:], ALU.add), [ST.b, P_upd.b], [ST.b])
            K.op("act", lambda e: e.copy(STb[:], ST[:]), [ST.b], [STb.b])
            if late_fn is not None:
                late_fn()
                late_fn = None
            if c < 2 and not need_ctx:
                continue
            if d == 0:
                K.dma("sp", YF[p0:p0 + 128, :], yb[:], yb.b, reads=[yb.b], writes=[g.dbuf("YFA", c)])
                continue
            yfb, o = yf[par], ob[par]
            K.op("pool", lambda e: e.tensor_tensor(yb[:], yb[:], yfb[:], ALU.add), [yb.b, yfb.b], [yb.b])
            K.op("pool", lambda e: e.tensor_tensor(t2[:], xsb[:, 0:512], ptm[:, 32:544], ALU.mult), [xsb.b, ptm.b], [t2.b])
            K.op("pool", lambda e: e.tensor_tensor(yb[:], yb[:], t2[:], ALU.add), [yb.b, t2.b], [yb.b])
            K.op("pool", lambda e: e.tensor_tensor(yb[:], yb[:], zA[:, c, :], ALU.mult), [yb.b, zA.b], [yb.b])
            K.op("pool", lambda e: e.tensor_tensor(t2[:], yb[:], yb[:], ALU.mult), [yb.b], [t2.b])

            def _late(yb=yb, o=o, p0=p0, c=c):
                K.op("dve", lambda e: e.tensor_reduce(ss[:], t2[:], AX.X, ALU.add), [t2.b], [ss.b])
                rstd_cols(g, ss[:], ss[:], 1.0 / 512, [ss.b], ss.b)
                K.op("dve", lambda e: e.scalar_tensor_tensor(out=o[:], in0=yb[:], scalar=ss[:, 0:1], in1=ptm[:, 544:1056],
                                                             op0=ALU.mult, op1=ALU.mult), [yb.b, ss.b, ptm.b], [o.b])
                K.dma("sp", S["Y"][p0:p0 + 128, 0:512], o[:], o.b, reads=[o.b], writes=[g.dbuf("Y", ("A", c))])
            late_fn = _late
        if late_fn is not None:
            late_fn()
        K.barrier()


def phase_mlstm(g, li, need_ctx):
    nc, K, I, S = g.nc, g.K, g.I, g.S
    P_intra, P_upd, P_inter = g.psum[0], g.psum[1], g.psum[2]
    LNS = math.log(128.0 ** -0.5)
    with ExitStack() as es:
        def sb(name, shape, dt):
            return Tile(es.enter_context(nc.sbuf_tensor(uniq(name), list(shape), dt)), name)
        ptm = sb("b_ptm", [128, 528], F32)
        cw = sb("b_cw", [128, 8, 6], F32)
        K.dma("sp", ptm[:], I["pB_tm"][li], ptm.b, writes=[ptm.b])
        K.dma("sp", cw[:], I["pB_conv"][li], cw.b, writes=[cw.b])
        fm = sb("b_fm", [128, 8, T], BF16)
        with ExitStack() as es2:
            def sb2(name, shape, dt):
                return Tile(es2.enter_context(nc.sbuf_tensor(uniq(name), list(shape), dt)), name)
            fpre = load_fm_from_tm(g, sb2, S["UB"], 0, 8, "b_fpre")
            dg = build_diag(g, sb2, cw, 8, "b_dg")
            for ct in range(8):
                conv_pe(g, fpre[:, ct, :], fpre.b, dg, ct, cw, ct, AF.Silu, lambda t0, n, ct=ct: fm[:, ct, t0:t0 + n], fm.b)
            K.barrier()
        gb = sb("b_gb", [128, NCH, 16], F32)
        igb = sb("b_igb", [128, NCH, 8], F32)
        logf = sb("b_logf", [128, NCH, 8], F32)
        K.dma("sp", gb[:], S["GB"].rearrange("(c p) j -> p c j", p=128), gb.b, writes=[gb.b])
        K.op("dve", lambda e: e.tensor_tensor(gb[:], gb[:], bc_mid(ptm[:, 0:16], NCH), ALU.add), [gb.b, ptm.b], [gb.b])
        K.op("dve", lambda e: e.tensor_scalar_add(igb[:], gb[:, :, 0:8], LNS), [gb.b], [igb.b])
        K.op("act", lambda e: e.activation(out=logf[:], in_=gb[:, :, 8:16], func=AF.Exp, scale=-1.0), [gb.b], [logf.b])
        K.op("act", lambda e: e.activation(out=logf[:], in_=logf[:], func=AF.Ln, bias=g.ones_f[:, 0:1]), [logf.b, g.ones_f.b], [logf.b])
        K.op("dve", lambda e: e.tensor_scalar_mul(logf[:], logf[:], -1.0), [logf.b], [logf.b])

        oA = sb("b_oA", [128, NCH, 512], BF16)
        K.dma("sp", oA[:], S["UB"][:, 1536:2048].rearrange("(c p) n -> p c n", p=128), oA.b, writes=[oA.b])
        for (c0_, c1_) in ((0, 17), (17, NCH)):
            K.op("act", lambda e, c0_=c0_, c1_=c1_: e.activation(out=oA[:, c0_:c1_, :], in_=oA[:, c0_:c1_, :], func=AF.Sigmoid), [oA.b], [oA.b])
        Cst = sb("b_C", [128, 4, 129], F32)
        Cb = sb("b_Cb", [128, 4, 129], BF16)
        ktm = [sb("b_ktm%d" % i, [128, 512], BF16) for i in range(2)]
        vp = [sb("b_vp%d" % i, [128, 4, 129], BF16) for i in range(2)]
        vpw = [sb("b_vpw%d" % i, [128, 129], BF16) for i in range(2)]
        bc = sb("b_bc", [128, 4], F32)
        expb = sb("b_expb", [128, 4], F32)
        biasc = sb("b_biasc", [128, 4], F32)
        ebl = sb("b_ebl", [128, 4], F32)
        Rt = [sb("b_R%d" % i, [128, 4, 128], F32) for i in range(2)]
        neg4 = [sb("b_neg4_%d" % i, [128, 4, 128], BF16) for i in range(2)]
        for i, nm in enumerate((g.negF_f, g.negB_f)):
            K.op("dve", lambda e, i=i, nm=nm: e.tensor_copy(neg4[i][:], bc_mid(nm[:], 4)), [nm.b], [neg4[i].b])
        PD = g.psum[3]
        E4 = [sb("b_E4_%d" % i, [128, 4, 128], F32) for i in range(2)]
        wT4 = [sb("b_wT4_%d" % i, [128, 4, 128], BF16) for i in range(2)]
        vw4 = [sb("b_vw4_%d" % i, [128, 4, 128], BF16) for i in range(2)]
        ws4 = [sb("b_ws4_%d" % i, [128, 4], BF16) for i in range(2)]
        yint4 = sb("b_yint4", [128, 512], F32)
        dsb = sb("b_dsb", [128, 12], F32)
        t4 = sb("b_t4", [128, 4], F32)
        PK = g.psum[4]
        y = [sb("b_y%d" % i, [128, 4, 129], F32) for i in range(2)]
        yf = [sb("b_yf%d" % i, [128, 4, 129], F32) for i in range(2)]
        ot = [sb("b_o%d" % i, [128, 512], BF16) for i in range(2)]
        sos = [sb("b_so%d" % i, [128, 512], F32) for i in range(2)]
        yns = [sb("b_yn%d" % i, [128, 4, 128], F32) for i in range(2)]
        tq = sb("b_tq", [128, 4, 128], F32)
        t1 = sb("b_t1", [128, 4, 128], F32)
        den = sb("b_den", [128, 4], F32)
        ss = sb("b_ss", [128, 4], F32)
        ob = [sb("b_ob%d" % i, [128, 512], BF16) for i in range(2)]
        for v_ in vp:
            K.op("dve", lambda e, v_=v_: e.memset(v_[:], 1.0), [], [v_.b])
        YF = nc.dram_tensor("s_YFB_%d" % li, [T, 4 * 129], F32, kind="Internal").ap()
        iters = [(d_, c_) for d_ in range(2) for c_ in chunk_order(d_)]

        def pre(j):
            if j > len(iters):
                return
            d_, c_ = iters[j - 1]
            q0 = c_ * 128
            K.dma("sp", vp[j % 2][:, :, 0:128], S["UB"][q0:q0 + 128, 1024:1536].rearrange("p (h q) -> p h q", h=4), vp[j % 2].b,
                  writes=[vp[j % 2].b])
            if d_ == 1 and not (c_ < 2 and not need_ctx):
                K.dma("sp", yf[j % 2][:, :, 0:128], YF[q0:q0 + 128, 0:512].rearrange("p (h q) -> p h q", h=4), yf[j % 2].b,
                      reads=[g.dbuf("YFB", c_)], writes=[yf[j % 2].b])
        bcs = [sb("b_bcs%d" % i, [128, 4], F32) for i in range(2)]
        expbs = [sb("b_expbs%d" % i, [128, 4], F32) for i in range(2)]
        biascs = [sb("b_biascs%d" % i, [128, 4], F32) for i in range(2)]

        def prep(j):
            if j > len(iters):
                return
            d_, c_ = iters[j - 1]
            q0 = c_ * 128
            par = j % 2
            tri_ = g.triU_f if d_ == 0 else g.triL_f
            kt_ = ktm[par]
            pb = next_psb(g)
            for jj in range(4):
                K.op("pe", lambda e, jj=jj: e.transpose(pb[:, jj * 128:(jj + 1) * 128], fm[:, 4 + jj, q0:q0 + 128],
                                                        g.ident_b[:]), [fm.b, g.ident_b.b], [pb.b])
            K.op("act", lambda e: e.copy(kt_[:], pb[:, :512]), [pb.b], [kt_.b])
            pq = next_ps(g, 5)
            K.op("pe", lambda e: e.matmul(pq[:, 0:4], tri_[:], logf[:, c_, d_ * 4:(d_ + 1) * 4], start=True, stop=True),
                 [tri_.b, logf.b], [pq.b])
            K.op("dve", lambda e: e.tensor_copy(bcs[par][:], pq[:, 0:4]), [pq.b], [bcs[par].b])
            K.op("act", lambda e: e.activation(out=expbs[par][:], in_=pq[:, 0:4], func=AF.Exp), [pq.b], [expbs[par].b])
            K.op("dve", lambda e: e.tensor_tensor(biascs[par][:], igb[:, c_, d_ * 4:(d_ + 1) * 4], bcs[par][:], ALU.subtract),
                 [igb.b, bcs[par].b], [biascs[par].b])
            Rb = Rt[par]
            K.op("dve", lambda e: e.tensor_tensor(Rb[:], bc_mid(tri_[:], 4), bc_last(logf[:, c_, d_ * 4:(d_ + 1) * 4], 128), ALU.mult),
                 [tri_.b, logf.b], [Rb.b])
            K.op("pe", lambda e: e.matmul(PD[:], g.ones_f[:], Rb[:].rearrange("p a b -> p (a b)"), start=True, stop=False),
                 [g.ones_f.b, Rb.b], [PD.b])
            K.op("pe", lambda e: e.matmul(PD[:], g.ident_b[:], neg4[d_][:].rearrange("p a b -> p (a b)"), start=False, stop=True),
                 [g.ident_b.b, neg4[d_].b], [PD.b])
        ci = 0
        late_fn = None
        for d in range(2):
            tri = g.triU_f if d == 0 else g.triL_f
            negm = g.negF_f if d == 0 else g.negB_f
            last = 127 if d == 0 else 0
            K.op("dve", lambda e: e.memset(Cst[:], 0.0), [], [Cst.b])
            K.op("dve", lambda e: e.memset(Cb[:], 0.0), [], [Cb.b])
            for c in chunk_order(d):
                p0 = c * 128
                ci += 1
                if ci == 1:
                    pre(1)
                    prep(1)
                pre(ci + 1)
                kt, vb, yb = ktm[ci % 2], vp[ci % 2], y[ci % 2]
                so, o_ = sos[ci % 2], ot[ci % 2]
                biasc, expb = biascs[ci % 2], expbs[ci % 2]
                Eb, wTb, vwb, wsb = E4[ci % 2], wT4[ci % 2], vw4[ci % 2], ws4[ci % 2]
                for h in range(4):
                    K.op("act", lambda e, h=h, Eb=Eb: e.activation(out=Eb[:, h, :], in_=PD[:, h * 128:(h + 1) * 128], func=AF.Exp,
                                                                   bias=biasc[:, h:h + 1]), [PD.b, biasc.b], [Eb.b])
                K.op("act", lambda e: e.activation(out=ebl[:], in_=PD[:, last:512:128], func=AF.Exp), [PD.b], [ebl.b])
                for h in range(4):
                    K.op("pe", lambda e, h=h: e.matmul(PK[:, h * 128:(h + 1) * 128], fm[:, 4 + h, p0:p0 + 128], fm[:, h, p0:p0 + 128],
                                                       start=True, stop=True), [fm.b], [PK.b])
                K.op("dve", lambda e, wTb=wTb, Eb=Eb: e.tensor_tensor(wTb[:], PK[:].rearrange("p (a b) -> p a b", a=4), Eb[:], ALU.mult),
                     [PK.b, Eb.b], [wTb.b])
                K.op("dve", lambda e, vwb=vwb, Eb=Eb, vb=vb: e.tensor_tensor(vwb[:], vb[:, :, 0:128], bc_last(Eb[:, :, last], 128), ALU.mult),
                     [vb.b, Eb.b], [vwb.b])
                K.op("dve", lambda e, wsb=wsb, Eb=Eb: e.tensor_copy(wsb[:], Eb[:, :, last]), [Eb.b], [wsb.b])
                PN = next_ps(g, 5)
                for h in range(4):
                    hs = slice(h * 128, (h + 1) * 128)
                    K.op("pe", lambda e, h=h, hs=hs, wTb=wTb, vb=vb: e.matmul(P_intra[:, hs], wTb[:, h, :], vb[:, h, 0:128], start=True, stop=True),
                         [wTb.b, vb.b], [P_intra.b])
                    K.op("pe", lambda e, h=h, hs=hs: e.matmul(P_inter[:, hs], fm[:, h, p0:p0 + 128], Cb[:, h, 0:128], start=True, stop=True),
                         [fm.b, Cb.b], [P_inter.b])
                    K.op("pe", lambda e, h=h, hs=hs, kt=kt, vwb=vwb: e.matmul(P_upd[:, hs], kt[:, hs], vwb[:, h, :], start=True, stop=True),
                         [kt.b, vwb.b], [P_upd.b])
                    K.op("pe", lambda e, h=h, PN=PN, wTb=wTb: e.matmul(PN[:, h:h + 1], wTb[:, h, :], g.ones_b[:, 0:1], start=True, stop=True),
                         [wTb.b, g.ones_b.b], [PN.b])
                    K.op("pe", lambda e, h=h, PN=PN: e.matmul(PN[:, 4 + h:5 + h], fm[:, h, p0:p0 + 128], Cb[:, h, 128:129], start=True, stop=True),
                         [fm.b, Cb.b], [PN.b])
                    K.op("pe", lambda e, h=h, hs=hs, PN=PN, kt=kt, wsb=wsb: e.matmul(PN[:, 8 + h:9 + h], kt[:, hs], wsb[:, h:h + 1], start=True, stop=True),
                         [kt.b, wsb.b], [PN.b])
                K.op("act", lambda e: e.copy(yint4[:], P_intra[:]), [P_intra.b], [yint4.b])
                K.op("act", lambda e, PN=PN: e.copy(dsb[:], PN[:, 0:12]), [PN.b], [dsb.b])
                prep(ci + 1)
                K.op("dve", lambda e: e.tensor_tensor(t1[:], P_inter[:].rearrange("p (a b) -> p a b", a=4), bc_last(expb[:], 128), ALU.mult),
                     [P_inter.b, expb.b], [t1.b])
                K.op("dve", lambda e, yb=yb: e.tensor_tensor(yb[:, :, 0:128], t1[:], yint4[:].rearrange("p (a b) -> p a b", a=4), ALU.add),
                     [t1.b, yint4.b], [yb.b])
                K.op("dve", lambda e: e.tensor_tensor(t4[:], dsb[:, 4:8], expb[:], ALU.mult), [dsb.b, expb.b], [t4.b])
                K.op("dve", lambda e, yb=yb: e.tensor_tensor(yb[:, :, 128], t4[:], dsb[:, 0:4], ALU.add), [t4.b, dsb.b], [yb.b])
                K.op("dve", lambda e: e.tensor_tensor(Cst[:, :, 0:128], Cst[:, :, 0:128], bc_last(ebl[:], 128), ALU.mult), [Cst.b, ebl.b], [Cst.b])
                K.op("dve", lambda e: e.tensor_tensor(Cst[:, :, 0:128], Cst[:, :, 0:128], P_upd[:].rearrange("p (a b) -> p a b", a=4), ALU.add),
                     [Cst.b, P_upd.b], [Cst.b])
                K.op("dve", lambda e: e.tensor_tensor(Cst[:, :, 128], Cst[:, :, 128], ebl[:], ALU.mult), [Cst.b, ebl.b], [Cst.b])
                K.op("dve", lambda e: e.tensor_tensor(Cst[:, :, 128], Cst[:, :, 128], dsb[:, 8:12], ALU.add), [Cst.b, dsb.b], [Cst.b])
                K.op("act", lambda e: e.copy(Cb[:], Cst[:]), [Cst.b], [Cb.b])
                if late_fn is not None:
                    late_fn()
                    late_fn = None
                if c < 2 and not need_ctx:
                    continue
                yn = yns[ci % 2]
                YFv = YF[p0:p0 + 128, 0:512].rearrange("p (h q) -> p h q", h=4)
                K.op("act", lambda e, yb=yb: e.activation(out=den[:], in_=yb[:, :, 128], func=AF.Abs), [yb.b], [den.b])
                K.op("dve", lambda e: e.tensor_scalar_max(den[:], den[:], 1.0), [den.b], [den.b])
                K.op("dve", lambda e: e.reciprocal(den[:], den[:]), [den.b], [den.b])
                K.op("dve", lambda e, yb=yb, yn=yn: e.tensor_tensor(yn[:], yb[:, :, 0:128], bc_last(den[:], 128), ALU.mult), [yb.b, den.b], [yn.b])
                if d == 0:
                    K.dma("sp", YFv, yn[:], yn.b, reads=[yn.b], writes=[g.dbuf("YFB", c)])
                    continue
                yfb, o2 = yf[ci % 2], ob[ci % 2]
                K.op("pool", lambda e, yfb=yfb, yn=yn: e.tensor_tensor(yn[:], yn[:], yfb[:, :, 0:128], ALU.add), [yn.b, yfb.b], [yn.b])
                K.op("pool", lambda e, yn=yn: e.tensor_tensor(tq[:], yn[:], yn[:], ALU.mult), [yn.b], [tq.b])

                def _late(yn=yn, o2=o2, so=so, c=c):
                    K.op("dve", lambda e: e.tensor_reduce(ss[:], tq[:], AX.X, ALU.add), [tq.b], [ss.b])
                    rstd_cols(g, ss[:], ss[:], 1.0 / 128, [ss.b], ss.b)
                    K.op("dve", lambda e: e.tensor_tensor(yn[:], yn[:], bc_last(ss[:], 128), ALU.mult), [yn.b, ss.b], [yn.b])
                    K.op("pool", lambda e: e.tensor_tensor(yn[:].rearrange("p h q -> p (h q)"), yn[:].rearrange("p h q -> p (h q)"),
                                                           ptm[:, 16:528], ALU.mult), [yn.b, ptm.b], [yn.b])
                    K.op("pool", lambda e: e.tensor_tensor(o2[:], yn[:].rearrange("p h q -> p (h q)"), oA[:, c, :], ALU.mult),
                         [yn.b, oA.b], [o2.b])
                    cm_store(g, S["Y"], 512, 512, c, o2)
                late_fn = _late
        if late_fn is not None:
            late_fn()
        K.barrier()


def phase_lru(g, li, need_ctx):
    nc, K, I, S = g.nc, g.K, g.I, g.S
    GC = 1.5957691216057308
    with ExitStack() as es:
        def sb(name, shape, dt):
            return Tile(es.enter_context(nc.sbuf_tensor(uniq(name), list(shape), dt)), name)
        pc = sb("c_pc", [128, 4, 12], F32)
        Wb = sb("c_W", [128, 16, 128], BF16)
        c8 = sb("c_c8", [128, 4, 2], F32)
        K.dma("sp", pc[:], I["pC_col"][li], pc.b, writes=[pc.b])
        K.dma("pool", Wb[:], I["pC_w"][li].rearrange("w p n -> p w n"), Wb.b, writes=[Wb.b])
        K.op("act", lambda e: e.activation(out=c8[:], in_=pc[:, :, 10:12], func=AF.Exp, scale=-1.0), [pc.b], [c8.b])
        K.op("act", lambda e: e.activation(out=c8[:], in_=c8[:], func=AF.Ln, bias=g.ones_f[:, 0:1]), [c8.b, g.ones_f.b], [c8.b])
        K.op("dve", lambda e: e.tensor_scalar_mul(c8[:], c8[:], -8.0), [c8.b], [c8.b])
        fm = sb("c_fm", [128, 8, T], BF16)
        for ct in range(8):
            K.dma("sp", fm[:, ct, :], S["UCf"][ct * 128:(ct + 1) * 128, :], fm.b, writes=[fm.b])
        xf = sb("c_xf", [128, T], F32)
        xfb = sb("c_xfb", [128, T], BF16)
        A = sb("c_A", [128, T], F32)
        Bx = sb("c_Bx", [128, T], F32)
        H = [sb("c_H%d" % i, [128, T], F32) for i in range(2)]
        blocks = [(t0, min(512, T - t0)) for t0 in range(0, T, 512)]
        dg = build_diag(g, sb, pc, 4, "c_dg")
        for ct in range(4):
            conv_pe(g, fm[:, 4 + ct, :], fm.b, dg, ct, pc, ct, AF.Identity, lambda t0, n: xf[:, t0:t0 + n], xf.b)
            K.op("dve", lambda e: e.tensor_copy(xfb[:], xf[:]), [xf.b], [xfb.b])
            for d in range(2):
                for (which, dst, bcol) in ((0, A, 6 + d), (1, Bx, 8 + d)):
                    wi = (d * 2 + which) * 4 + ct
                    for (t0, n) in blocks:
                        ps = next_ps(g)
                        K.op("pe", lambda e, ps=ps, wi=wi, t0=t0, n=n: e.matmul(ps[:, :n], Wb[:, wi, :], xfb[:, t0:t0 + n],
                                                                                start=True, stop=True), [Wb.b, xfb.b], [ps.b])
                        K.op("act", lambda e, ps=ps, dst=dst, t0=t0, n=n, bcol=bcol, ct=ct: e.activation(
                            out=dst[:, t0:t0 + n], in_=ps[:, :n], func=AF.Sigmoid, bias=pc[:, ct, bcol:bcol + 1]),
                            [ps.b, pc.b], [dst.b])
                Hd = H[d]
                K.op("act", lambda e, ct=ct, d=d: e.activation(out=A[:], in_=A[:], func=AF.Exp, scale=c8[:, ct, d:d + 1]),
                     [A.b, c8.b], [A.b])
                K.op("dve", lambda e, Hd=Hd: e.tensor_tensor(Hd[:], A[:], A[:], ALU.mult), [A.b], [Hd.b])
                K.op("dve", lambda e, Hd=Hd: e.tensor_scalar(Hd[:], Hd[:], -1.0, 1.0, ALU.mult, ALU.add), [Hd.b], [Hd.b])
                K.op("act", lambda e, Hd=Hd: e.activation(out=Hd[:], in_=Hd[:], func=AF.Sqrt), [Hd.b], [Hd.b])
                K.op("dve", lambda e, Hd=Hd: e.tensor_tensor(Bx[:], Bx[:], Hd[:], ALU.mult), [Bx.b, Hd.b], [Bx.b])
                K.op("dve", lambda e: e.tensor_tensor(Bx[:], Bx[:], xf[:], ALU.mult), [Bx.b, xf.b], [Bx.b])
                if d == 0:
                    K.op("dve", lambda e, Hd=Hd: e.tensor_tensor_scan(Hd[:], A[:], Bx[:], 0.0, ALU.mult, ALU.add),
                         [A.b, Bx.b], [Hd.b])
                else:
                    K.op("dve", lambda e, Hd=Hd: e.tensor_tensor_scan(Hd[:, 0:CTX][:, ::-1], A[:, 0:CTX][:, ::-1],
                                                                     Bx[:, 0:CTX][:, ::-1], 0.0, ALU.mult, ALU.add),
                         [A.b, Bx.b], [Hd.b])
                    K.op("dve", lambda e, Hd=Hd: e.tensor_tensor_scan(Hd[:, CTX:T][:, ::-1], A[:, CTX:T][:, ::-1],
                                                                     Bx[:, CTX:T][:, ::-1], Hd[:, 0:1], ALU.mult, ALU.add),
                         [A.b, Bx.b, Hd.b], [Hd.b])
            gt = fm[:, ct, :]
            K.op("dve", lambda e: e.tensor_tensor(H[0][:], H[0][:], H[1][:], ALU.add), [H[0].b, H[1].b], [H[0].b])
            K.op("dve", lambda e, gt=gt: e.tensor_tensor(A[:], gt, gt, ALU.mult), [fm.b], [A.b])
            K.op("dve", lambda e: e.tensor_scalar(A[:], A[:], 0.044715, 1.0, ALU.mult, ALU.add), [A.b], [A.b])
            K.op("dve", lambda e, gt=gt: e.tensor_tensor(A[:], A[:], gt, ALU.mult), [A.b, fm.b], [A.b])
            K.op("act", lambda e: e.activation(out=A[:], in_=A[:], func=AF.Sigmoid, scale=GC), [A.b], [A.b])
            K.op("dve", lambda e, gt=gt: e.tensor_tensor(H[0][:], H[0][:], gt, ALU.mult), [H[0].b, fm.b], [H[0].b])
            K.op("dve", lambda e, ct=ct: e.tensor_tensor(fm[:, 4 + ct, :], H[0][:], A[:], ALU.mult), [H[0].b, A.b], [fm.b])
        st = [sb("c_st%d" % i, [128, 512], BF16) for i in range(2)]
        for c in range(NCH):
            if c < 2 and not need_ctx:
                continue
            pb = next_psb(g)
            for j in range(4):
                K.op("pe", lambda e, j=j, pb=pb, c=c: e.transpose(pb[:, j * 128:(j + 1) * 128], fm[:, 4 + j, c * 128:(c + 1) * 128],
                                                                  g.ident_b[:]), [fm.b, g.ident_b.b], [pb.b])
            s_ = st[c % 2]
            K.op("act", lambda e, pb=pb, s_=s_: e.copy(s_[:], pb[:, :512]), [pb.b], [s_.b])
            K.dma("sp", S["Y"][c * 128:(c + 1) * 128, 1024:1536], s_[:], s_.b, reads=[s_.b], writes=[g.dbuf("Y", ("C", c))])
        K.barrier()


def phase_gla(g, li, need_ctx):
    nc, K, I, S = g.nc, g.K, g.I, g.S
    P_o, P_upd = g.psum[0], g.psum[1]
    with ExitStack() as es:
        def sb(name, shape, dt):
            return Tile(es.enter_context(nc.sbuf_tensor(uniq(name), list(shape), dt)), name)
        ptm = sb("d_ptm", [128, 1024], F32)
        wg2 = sb("d_wg2", [32, 512], F32)
        lnq = sb("d_lnq", [128, 1], F32)
        K.dma("sp", ptm[:], I["pD_tm"][li], ptm.b, writes=[ptm.b])
        K.dma("sp", wg2[:], I["pD_wg2"][li], wg2.b, writes=[wg2.b])
        K.op("dve", lambda e: e.memset(lnq[:], math.log(0.125)), [], [lnq.b])
        g1 = sb("d_g1", [128, NCH, 32], F32)
        K.dma("sp", g1[:], S["GD"].rearrange("(c p) j -> p c j", p=128), g1.b, writes=[g1.b])
        rA = sb("d_rA", [128, NCH, 512], BF16)
        K.dma("sp", rA[:], S["UD"][:, 1024:1536].rearrange("(c p) n -> p c n", p=128), rA.b, writes=[rA.b])
        for (c0_, c1_) in ((0, 17), (17, NCH)):
            K.op("act", lambda e, c0_=c0_, c1_=c1_: e.activation(out=rA[:, c0_:c1_, :], in_=rA[:, c0_:c1_, :], func=AF.Silu), [rA.b], [rA.b])
        Sst = sb("d_S", [128, 2, 128], F32)
        Smid = sb("d_Smid", [128, 2, 128], F32)
        Smb = sb("d_Smb", [128, 2, 128], BF16)
        qk = [sb("d_qk%d" % i, [128, 512], BF16) for i in range(2)]
        vt = [sb("d_v%d" % i, [128, 512], BF16) for i in range(2)]
        ektm = sb("d_ektm", [128, 256], F32)
        eqT = sb("d_eqT", [128, 2, 128], F32)
        ekT = sb("d_ekT", [128, 2, 128], F32)
        ebm = sb("d_ebm", [128, 2], F32)
        ebr = sb("d_ebr", [128, 2], F32)
        qtT = sb("d_qtT", [128, 2, 128], BF16)
        ktT = sb("d_ktT", [128, 2, 128], BF16)
        ktm = sb("d_ktm", [128, 256], BF16)
        att = [sb("d_att%d" % i, [128, 128], BF16) for i in range(2)]
        tmp = sb("d_tmp", [128, 128], F32)
        o = [sb("d_o%d" % i, [128, 512], F32) for i in range(2)]
        of = [sb("d_of%d" % i, [128, 512], F32) for i in range(2)]
        rt = [sb("d_r%d" % i, [128, 512], BF16) for i in range(2)]
        srs = [sb("d_sr%d" % i, [128, 512], F32) for i in range(2)]
        t1 = sb("d_t1", [128, 512], F32)
        ss = sb("d_ss", [128, 4], F32)
        ob = [sb("d_ob%d" % i, [128, 512], BF16) for i in range(2)]
        YF = nc.dram_tensor("s_YFD_%d" % li, [T, 512], F32, kind="Internal").ap()
        laA = sb("d_laA", [128, NCH, 512], F32)
        g1Ts = [sb("d_g1T%d" % i, [32, 128], F32) for i in range(2)]
        for c_ in range(NCH):
            gT = g1Ts[c_ % 2]
            pg = next_ps(g, 2)
            K.op("pe", lambda e, pg=pg, c_=c_: e.transpose(pg[:32, :128], g1[:, c_, :], g.ident_f[:]), [g1.b, g.ident_f.b], [pg.b])
            K.op("act", lambda e, pg=pg, gT=gT: e.copy(gT[:], pg[:32, :128]), [pg.b], [gT.b])
            pl = next_ps(g, 2)
            K.op("pe", lambda e, pl=pl, gT=gT: e.matmul(pl[:], gT[:], wg2[:], start=True, stop=True), [gT.b, wg2.b], [pl.b])
            K.op("dve", lambda e, pl=pl, c_=c_: e.tensor_tensor(laA[:, c_, :], pl[:], ptm[:, 0:512], ALU.add), [pl.b, ptm.b], [laA.b])
        for (c0_, c1_) in ((0, 17), (17, NCH)):
            K.op("act", lambda e, c0_=c0_, c1_=c1_: e.activation(out=laA[:, c0_:c1_, :], in_=laA[:, c0_:c1_, :], func=AF.Exp, scale=-1.0),
                 [laA.b], [laA.b])
            K.op("act", lambda e, c0_=c0_, c1_=c1_: e.activation(out=laA[:, c0_:c1_, :], in_=laA[:, c0_:c1_, :], func=AF.Ln,
                                                                 bias=g.ones_f[:, 0:1]), [laA.b, g.ones_f.b], [laA.b])
            K.op("dve", lambda e, c0_=c0_, c1_=c1_: e.tensor_scalar_mul(laA[:, c0_:c1_, :], laA[:, c0_:c1_, :], -1.0 / 16.0), [laA.b], [laA.b])

        iters = [(d_, c_) for d_ in range(2) for c_ in chunk_order(d_)]

        def pre(j):
            if j > len(iters):
                return
            d_, c_ = iters[j - 1]
            q0 = c_ * 128
            K.dma("sp", qk[j % 2][:], S["UD"][q0:q0 + 128, 0:512], qk[j % 2].b, writes=[qk[j % 2].b])
            K.dma("sp", vt[j % 2][:], S["UD"][q0:q0 + 128, 512:1024], vt[j % 2].b, writes=[vt[j % 2].b])
            if d_ == 1 and not (c_ < 2 and not need_ctx):
                K.dma("sp", of[j % 2][:], YF[q0:q0 + 128, :], of[j % 2].b, reads=[g.dbuf("YFD", c_)], writes=[of[j % 2].b])
        ci = 0
        late_fn = None
        for d in range(2):
            tri = g.triU_f if d == 0 else g.triL_f
            mid = g.mid_f[d]
            midcol = g.triU_f[:, 63:64] if d == 0 else g.triL_f[:, 64:65]
            msk = g.mskF_b if d == 0 else g.mskB_b
            last = 127 if d == 0 else 0
            K.op("dve", lambda e: e.memset(Sst[:], 0.0), [], [Sst.b])
            for c in chunk_order(d):
                p0 = c * 128
                ci += 1
                if ci == 1:
                    pre(1)
                pre(ci + 1)
                qkb, vb, ob_ = qk[ci % 2], vt[ci % 2], o[ci % 2]
                sr, rb = srs[ci % 2], rt[ci % 2]
                la = Sub(laA[:, c, d * 256:(d + 1) * 256], laA.b)
                p1 = next_ps(g, 2)
                K.op("pe", lambda e, p1=p1: e.matmul(p1[:, :256], mid[:], la[:], start=True, stop=True), [mid.b, la.b], [p1.b])
                K.op("act", lambda e, p1=p1: e.activation(out=ektm[:], in_=p1[:, :256], func=AF.Exp, scale=-1.0), [p1.b], [ektm.b])
                p2 = next_ps(g, 2)
                for ct in range(2):
                    K.op("pe", lambda e, p2=p2, ct=ct: e.matmul(p2[:, ct * 129:ct * 129 + 128], la[:, ct * 128:(ct + 1) * 128], mid[:],
                                                                start=True, stop=True), [la.b, mid.b], [p2.b])
                    K.op("pe", lambda e, p2=p2, ct=ct: e.matmul(p2[:, ct * 129 + 128:ct * 129 + 129], la[:, ct * 128:(ct + 1) * 128], midcol,
                                                                start=True, stop=True), [la.b, tri.b], [p2.b])
                p2v = p2[:, :258].rearrange("p (a b) -> p a b", a=2)
                K.op("act", lambda e, p2v=p2v: e.activation(out=eqT[:], in_=p2v[:, :, 0:128], func=AF.Exp, bias=lnq[:, 0:1]),
                     [p2.b, lnq.b], [eqT.b])
                K.op("act", lambda e, p2v=p2v: e.activation(out=ekT[:], in_=p2v[:, :, 0:128], func=AF.Exp, scale=-1.0), [p2.b], [ekT.b])
                K.op("act", lambda e, p2v=p2v: e.activation(out=ebm[:], in_=p2v[:, :, 128], func=AF.Exp), [p2.b], [ebm.b])
                K.op("act", lambda e, p2v=p2v: e.activation(out=ebr[:], in_=p2v[:, :, last], func=AF.Exp), [p2.b], [ebr.b])
                pb = next_psb(g)
                for j in range(4):
                    K.op("pe", lambda e, j=j, pb=pb, qkb=qkb: e.transpose(pb[:, j * 128:(j + 1) * 128], qkb[:, j * 128:(j + 1) * 128],
                                                                          g.ident_b[:]), [qkb.b, g.ident_b.b], [pb.b])
                K.op("dve", lambda e, pb=pb: e.tensor_tensor(qtT[:], pb[:, 0:256].rearrange("p (a b) -> p a b", a=2), eqT[:], ALU.mult),
                     [pb.b, eqT.b], [qtT.b])
                K.op("dve", lambda e, pb=pb: e.tensor_tensor(ktT[:], pb[:, 256:512].rearrange("p (a b) -> p a b", a=2), ekT[:], ALU.mult),
                     [pb.b, ekT.b], [ktT.b])
                K.op("dve", lambda e, qkb=qkb: e.tensor_tensor(ktm[:], qkb[:, 256:512], ektm[:], ALU.mult), [qkb.b, ektm.b], [ktm.b])
                for ct in range(2):
                    K.op("dve", lambda e, ct=ct: e.tensor_scalar_mul(Smid[:, ct, :], Sst[:, ct, :], ebm[:, ct:ct + 1]),
                         [Sst.b, ebm.b], [Smid.b])
                K.op("act", lambda e: e.copy(Smb[:], Smid[:]), [Smid.b], [Smb.b])
                for h in range(4):
                    ct, hp = h // 2, (h % 2) * 64
                    at = att[h % 2]
                    pa = next_ps(g, 2)
                    K.op("pe", lambda e, pa=pa, ct=ct, hp=hp: e.matmul(pa[:, :128], ktT[hp:hp + 64, ct, :], qtT[hp:hp + 64, ct, :],
                                                                       start=True, stop=True), [ktT.b, qtT.b], [pa.b])
                    K.op("dve", lambda e, pa=pa, at=at: e.tensor_tensor(at[:], pa[:, :128], msk[:], ALU.mult), [pa.b, msk.b], [at.b])
                    K.op("pe", lambda e, at=at, h=h, vb=vb: e.matmul(P_o[:, h * 128:(h + 1) * 128], at[:], vb[:, h * 128:(h + 1) * 128],
                                                                     start=True, stop=False), [at.b, vb.b], [P_o.b])
                    K.op("pe", lambda e, h=h, ct=ct, hp=hp: e.matmul(P_o[:, h * 128:(h + 1) * 128], qtT[hp:hp + 64, ct, :],
                                                                     Smb[hp:hp + 64, ct, :], start=False, stop=True),
                         [qtT.b, Smb.b], [P_o.b])
                    K.op("pe", lambda e, h=h, ct=ct, hp=hp, vb=vb: e.matmul(P_upd[hp:hp + 64, ct * 128:(ct + 1) * 128],
                                                                            ktm[:, h * 64:(h + 1) * 64], vb[:, h * 128:(h + 1) * 128],
                                                                            start=True, stop=True), [ktm.b, vb.b], [P_upd.b])
                for ct in range(2):
                    K.op("dve", lambda e, ct=ct: e.tensor_tensor(tmp[:], Smid[:, ct, :], P_upd[:, ct * 128:(ct + 1) * 128], ALU.add),
                         [Smid.b, P_upd.b], [tmp.b])
                    K.op("dve", lambda e, ct=ct: e.tensor_scalar_mul(Sst[:, ct, :], tmp[:], ebr[:, ct:ct + 1]), [tmp.b, ebr.b], [Sst.b])
                if late_fn is not None:
                    late_fn()
                    late_fn = None
                if c < 2 and not need_ctx:
                    continue
                if d == 0:
                    K.op("act", lambda e, ob_=ob_: e.copy(ob_[:], P_o[:]), [P_o.b], [ob_.b])
                    K.dma("sp", YF[p0:p0 + 128, :], ob_[:], ob_.b, reads=[ob_.b], writes=[g.dbuf("YFD", c)])
                    continue
                ofb, o2 = of[ci % 2], ob[ci % 2]
                K.op("dve", lambda e, ob_=ob_, ofb=ofb: e.tensor_tensor(ob_[:], P_o[:], ofb[:], ALU.add), [P_o.b, ofb.b], [ob_.b])
                K.op("pool", lambda e, ob_=ob_: e.tensor_tensor(t1[:], ob_[:], ob_[:], ALU.mult), [ob_.b], [t1.b])

                def _late(ob_=ob_, o2=o2, sr=sr, c=c):
                    K.op("dve", lambda e: e.tensor_reduce(ss[:], t1[:].rearrange("p (h q) -> p h q", h=4), AX.X, ALU.add), [t1.b], [ss.b])
                    rstd_cols(g, ss[:], ss[:], 1.0 / 128, [ss.b], ss.b)
                    K.op("dve", lambda e: e.tensor_tensor(ob_[:].rearrange("p (h q) -> p h q", h=4),
                                                          ob_[:].rearrange("p (h q) -> p h q", h=4), bc_last(ss[:], 128), ALU.mult),
                         [ob_.b, ss.b], [ob_.b])
                    K.op("pool", lambda e: e.tensor_tensor(ob_[:], ob_[:], ptm[:, 512:1024], ALU.mult), [ob_.b, ptm.b], [ob_.b])
                    K.op("pool", lambda e: e.tensor_tensor(o2[:], ob_[:], rA[:, c, :], ALU.mult), [ob_.b, rA.b], [o2.b])
                    cm_store(g, S["Y"], 1536, 512, c, o2)
                late_fn = _late
        if late_fn is not None:
            late_fn()
        K.barrier()
```
